# Optimizing a Trainium2 kernel written in Bass

```python
import jax, jax.numpy as jnp
from jax import lax
import numpy as np

D_MODEL = 1024
BATCH = 2
SEQ = 8192
DEPTH = 1

D_MIX = D_MODEL
HEAD_DIM = 64
D_RWKV = D_MIX // 2
D_NSA = D_MIX - D_RWKV
RWKV_HEADS = D_RWKV // HEAD_DIM
NSA_Q_HEADS = D_NSA // HEAD_DIM
NSA_KV_HEADS = 2
NSA_GROUP = NSA_Q_HEADS // NSA_KV_HEADS
D_KV = NSA_KV_HEADS * HEAD_DIM
LORA_W = 64
LORA_A = 64
LORA_G = 128
RWKV_SIZES = (D_RWKV, D_RWKV, D_RWKV, LORA_W, LORA_A, LORA_G)
N_RWKV_COLS = sum(RWKV_SIZES)
CMP_BLOCK = 32
CMP_STRIDE = 16
CMP_HIDDEN = 256
SEL_BLOCK = 64
SEL_TOPK = 16
WINDOW = 512
Q_BLOCK = 128
NSA_SIZES = (D_NSA, D_KV, D_KV, D_KV, D_KV, D_KV, D_KV, 3 * NSA_Q_HEADS)
N_NSA_COLS = sum(NSA_SIZES)
N_IN = N_RWKV_COLS + N_NSA_COLS
D_FF = -(-8 * D_MODEL // (3 * 256)) * 256
NORM_EPS = 1e-6
GN_EPS = 64e-5
NEG = -1e30
BIG = 1e30

kernel_name = 'hybrid_rwkv7_nsa_block'


def _split(z, sizes):
    return jnp.split(z, [int(c) for c in np.cumsum(sizes)[:-1]], axis=-1)


def rms_norm(x, g):
    xf = x.astype(jnp.float32)
    y = xf * lax.rsqrt(jnp.mean(xf * xf, axis=-1, keepdims=True) + NORM_EPS)
    return (y * g).astype(x.dtype)


def alibi_slopes(n_heads):
    return 2.0 ** (-8.0 * jnp.arange(1, n_heads + 1, dtype=jnp.float32) / n_heads)


def swiglu(x, w_gate, w_up, w_down):
    return (jax.nn.silu(x @ w_gate) * (x @ w_up)) @ w_down


def rwkv7_time_mix(p, mu, w0, w2, a0, a2, g2, k_k, k_a, r_k, lnx_w, lnx_b):
    B, T, _ = p.shape
    H, N = RWKV_HEADS, HEAD_DIM
    p = p.astype(jnp.float32)
    p_prev = jnp.pad(p, ((0, 0), (1, 0), (0, 0)))[:, :-1]
    p = p + mu * (p_prev - p)
    r, k, v, dw, da, dg = _split(p, RWKV_SIZES)
    w_log = -jax.nn.softplus(-(w0 + jnp.tanh(dw) @ w2)) - 0.5
    decay = jnp.exp(-jnp.exp(w_log))
    a = jax.nn.sigmoid(a0 + da @ a2)
    g = jax.nn.sigmoid(dg) @ g2
    heads = lambda z: z.reshape(B, T, H, N)
    kk = heads(k * k_k)
    kk = kk / jnp.maximum(jnp.sqrt(jnp.sum(kk * kk, axis=-1, keepdims=True)), 1e-12)
    k = k * (1.0 + (a - 1.0) * k_a)
    r, k, v, decay, a = map(heads, (r, k, v, decay, a))
    xs = tuple(jnp.moveaxis(z, 1, 0) for z in (r, decay, k, v, -kk, kk * a))

    def step(S, inp):
        r_t, w_t, k_t, v_t, a_t, b_t = inp
        sa = jnp.einsum('bhvk,bhk->bhv', S, a_t)
        S = S * w_t[:, :, None, :] + sa[..., None] * b_t[:, :, None, :] + v_t[..., None] * k_t[:, :, None, :]
        return S, jnp.einsum('bhvk,bhk->bhv', S, r_t)

    S0 = jnp.zeros((B, H, N, N), jnp.float32)
    _, y = lax.scan(step, S0, xs)
    y = jnp.moveaxis(y, 0, 1)
    mean = jnp.mean(y, axis=-1, keepdims=True)
    var = jnp.mean(jnp.square(y - mean), axis=-1, keepdims=True)
    y = ((y - mean) * lax.rsqrt(var + GN_EPS)).reshape(B, T, D_RWKV) * lnx_w + lnx_b
    bonus = jnp.sum(r * k * r_k, axis=-1, keepdims=True) * v
    return (y + bonus.reshape(B, T, D_RWKV)) * g


def nsa_attention(p, pe_k, pe_v, ck1, ck2, cv1, cv2):
    B, T, _ = p.shape
    Hk, G, dk = NSA_KV_HEADS, NSA_GROUP, HEAD_DIM
    p = p.astype(jnp.float32)
    q, kc, vc, ks, vs, kw, vw, gl = _split(p, NSA_SIZES)
    q = q.reshape(B, T, Hk, G, dk) * (dk ** -0.5)
    gates = jax.nn.sigmoid(gl).reshape(B, T, Hk, G, 3)
    kv_heads = lambda z: z.reshape(B, T, Hk, dk)

    n_cmp = (T - CMP_BLOCK) // CMP_STRIDE + 1
    cmp_idx = np.arange(n_cmp)[:, None] * CMP_STRIDE + np.arange(CMP_BLOCK)[None, :]

    def compress(z, pe, w1, w2):
        blk = kv_heads(z)[:, cmp_idx] + pe[None, None, :, None, :]
        blk = jnp.transpose(blk, (0, 3, 1, 2, 4)).reshape(B, Hk, n_cmp, CMP_BLOCK * dk)
        return jax.nn.gelu(blk @ w1) @ w2

    k_cmp = compress(kc, pe_k, ck1, ck2)
    v_cmp = compress(vc, pe_v, cv1, cv2)
    cmp_start = np.arange(n_cmp) * CMP_STRIDE
    cmp_end = jnp.asarray(cmp_start + CMP_BLOCK - 1)

    n_sel = T // SEL_BLOCK
    top_k = min(SEL_TOPK, n_sel)
    sel_start = np.arange(n_sel) * SEL_BLOCK
    ov = np.clip(np.minimum(cmp_start[:, None] + CMP_BLOCK, sel_start[None, :] + SEL_BLOCK)
                 - np.maximum(cmp_start[:, None], sel_start[None, :]), 0, None) / CMP_STRIDE
    ov = jnp.asarray(ov, jnp.float32)
    k_sel = kv_heads(ks).reshape(B, n_sel, SEL_BLOCK, Hk, dk).transpose(0, 3, 1, 2, 4)
    v_sel = kv_heads(vs).reshape(B, n_sel, SEL_BLOCK, Hk, dk).transpose(0, 3, 1, 2, 4)

    pad = ((0, 0), (WINDOW, 0), (0, 0), (0, 0))
    k_win = jnp.pad(kv_heads(kw), pad)
    v_win = jnp.pad(kv_heads(vw), pad)

    slopes = alibi_slopes(NSA_Q_HEADS).reshape(Hk, G)
    b_ix = jnp.arange(B)[:, None, None, None]
    h_ix = jnp.arange(Hk)[None, :, None, None]
    blk_ids = jnp.arange(n_sel)

    def query_block(qi):
        q0 = qi * Q_BLOCK
        t = q0 + jnp.arange(Q_BLOCK)
        qb = lax.dynamic_slice_in_dim(q, q0, Q_BLOCK, 1)
        gb = lax.dynamic_slice_in_dim(gates, q0, Q_BLOCK, 1)

        s = jnp.einsum('bqhgd,bhcd->bhgqc', qb, k_cmp)
        s = s - slopes[None, :, :, None, None] * jnp.abs(t[:, None] - cmp_end[None, :])
        ok_c = cmp_end[None, :] <= t[:, None]
        p_c = jax.nn.softmax(jnp.where(ok_c, s, NEG), axis=-1) * jnp.any(ok_c, axis=-1)[:, None]
        o_c = jnp.einsum('bhgqc,bhcd->bqhgd', p_c, v_cmp)

        imp = jnp.einsum('bhgqc,cn->bhqn', p_c, ov)
        cur = t // SEL_BLOCK
        valid = blk_ids[None, :] <= cur[:, None]
        forced = (blk_ids[None, :] == 0) | (blk_ids[None, :] == cur[:, None]) | (blk_ids[None, :] == cur[:, None] - 1)
        imp = jnp.where(valid, jnp.where(forced, BIG, imp), NEG)
        top_v, top_i = lax.top_k(imp, top_k)
        kg = k_sel[b_ix, h_ix, top_i]
        vg = v_sel[b_ix, h_ix, top_i]
        pos = top_i[..., None] * SEL_BLOCK + jnp.arange(SEL_BLOCK)
        s = jnp.einsum('bqhgd,bhqnjd->bhgqnj', qb, kg)
        s = s - slopes[None, :, :, None, None, None] * jnp.abs(t[:, None, None] - pos)[:, :, None]
        ok_s = ((top_v > 0.5 * NEG)[..., None] & (pos <= t[:, None, None]))[:, :, None]
        s = jnp.where(ok_s, s, NEG).reshape(B, Hk, G, Q_BLOCK, top_k * SEL_BLOCK)
        p_s = jax.nn.softmax(s, axis=-1)
        o_s = jnp.einsum('bhgqm,bhqmd->bqhgd', p_s, vg.reshape(B, Hk, Q_BLOCK, top_k * SEL_BLOCK, dk))

        kwb = lax.dynamic_slice_in_dim(k_win, q0, Q_BLOCK + WINDOW, 1)
        vwb = lax.dynamic_slice_in_dim(v_win, q0, Q_BLOCK + WINDOW, 1)
        pos_w = q0 - WINDOW + jnp.arange(Q_BLOCK + WINDOW)
        dist = t[:, None] - pos_w[None, :]
        s = jnp.einsum('bqhgd,bkhd->bhgqk', qb, kwb) - slopes[None, :, :, None, None] * jnp.abs(dist)
        ok_w = (pos_w[None, :] >= 0) & (dist >= 0) & (dist < WINDOW)
        p_w = jax.nn.softmax(jnp.where(ok_w, s, NEG), axis=-1)
        o_w = jnp.einsum('bhgqk,bkhd->bqhgd', p_w, vwb)

        return gb[..., 0:1] * o_c + gb[..., 1:2] * o_s + gb[..., 2:3] * o_w

    out = lax.map(query_block, jnp.arange(T // Q_BLOCK))
    return jnp.moveaxis(out, 0, 1).reshape(B, T, D_NSA)


def setup_inputs(seed: int = 0) -> dict:
    key = jax.random.key(seed)
    ks = jax.random.split(key, 26)
    f32 = jnp.float32
    L = DEPTH

    def nrm(k, shape, scale):
        return jax.random.normal(k, shape, f32) * scale

    def gain(k, shape):
        return 1.0 + 0.02 * jax.random.normal(k, shape, f32)

    return {
        'x': nrm(ks[0], (BATCH, SEQ, D_MODEL), 1.0),
        'norm1_g': gain(ks[1], (L, D_MODEL)),
        'w_in': nrm(ks[2], (L, D_MODEL, N_IN), D_MODEL ** -0.5),
        'mu_shift': jax.random.uniform(ks[3], (L, N_RWKV_COLS), f32),
        'rwkv_w0': jax.random.uniform(ks[4], (L, D_RWKV), f32, -4.0, 1.0),
        'rwkv_w2': nrm(ks[5], (L, LORA_W, D_RWKV), 0.1),
        'rwkv_a0': nrm(ks[6], (L, D_RWKV), 0.5),
        'rwkv_a2': nrm(ks[7], (L, LORA_A, D_RWKV), 0.1),
        'rwkv_g2': nrm(ks[8], (L, LORA_G, D_RWKV), LORA_G ** -0.5),
        'rwkv_k_k': 0.85 + 0.05 * jax.random.normal(ks[9], (L, D_RWKV), f32),
        'rwkv_k_a': gain(ks[10], (L, D_RWKV)),
        'rwkv_r_k': nrm(ks[11], (L, RWKV_HEADS, HEAD_DIM), 0.1),
        'rwkv_lnx_w': gain(ks[12], (L, D_RWKV)),
        'rwkv_lnx_b': nrm(ks[13], (L, D_RWKV), 0.01),
        'nsa_pe_k': nrm(ks[14], (L, CMP_BLOCK, HEAD_DIM), 0.1),
        'nsa_pe_v': nrm(ks[15], (L, CMP_BLOCK, HEAD_DIM), 0.1),
        'nsa_cmp_k_w1': nrm(ks[16], (L, CMP_BLOCK * HEAD_DIM, CMP_HIDDEN), (CMP_BLOCK * HEAD_DIM) ** -0.5),
        'nsa_cmp_k_w2': nrm(ks[17], (L, CMP_HIDDEN, HEAD_DIM), CMP_HIDDEN ** -0.5),
        'nsa_cmp_v_w1': nrm(ks[18], (L, CMP_BLOCK * HEAD_DIM, CMP_HIDDEN), (CMP_BLOCK * HEAD_DIM) ** -0.5),
        'nsa_cmp_v_w2': nrm(ks[19], (L, CMP_HIDDEN, HEAD_DIM), CMP_HIDDEN ** -0.5),
        'w_out': nrm(ks[20], (L, D_MIX, D_MODEL), D_MIX ** -0.5),
        'norm2_g': gain(ks[21], (L, D_MODEL)),
        'ffn_w_gate': nrm(ks[22], (L, D_MODEL, D_FF), D_MODEL ** -0.5),
        'ffn_w_up': nrm(ks[23], (L, D_MODEL, D_FF), D_MODEL ** -0.5),
        'ffn_w_down': nrm(ks[24], (L, D_FF, D_MODEL), D_FF ** -0.5),
        'norm_f_g': gain(ks[25], (D_MODEL,)),
    }


def reference(x, norm1_g, w_in, mu_shift, rwkv_w0, rwkv_w2, rwkv_a0, rwkv_a2, rwkv_g2,
              rwkv_k_k, rwkv_k_a, rwkv_r_k, rwkv_lnx_w, rwkv_lnx_b, nsa_pe_k, nsa_pe_v,
              nsa_cmp_k_w1, nsa_cmp_k_w2, nsa_cmp_v_w1, nsa_cmp_v_w2, w_out, norm2_g,
              ffn_w_gate, ffn_w_up, ffn_w_down, norm_f_g):
    h = x
    for i in range(DEPTH):
        u = rms_norm(h, norm1_g[i])
        p_rwkv, p_nsa = jnp.split(u @ w_in[i], [N_RWKV_COLS], axis=-1)
        y_rwkv = rwkv7_time_mix(p_rwkv, mu_shift[i], rwkv_w0[i], rwkv_w2[i], rwkv_a0[i], rwkv_a2[i],
                                rwkv_g2[i], rwkv_k_k[i], rwkv_k_a[i], rwkv_r_k[i],
                                rwkv_lnx_w[i], rwkv_lnx_b[i])
        y_nsa = nsa_attention(p_nsa, nsa_pe_k[i], nsa_pe_v[i], nsa_cmp_k_w1[i], nsa_cmp_k_w2[i],
                              nsa_cmp_v_w1[i], nsa_cmp_v_w2[i])
        y = jnp.concatenate([y_rwkv, y_nsa], axis=-1).astype(h.dtype)
        h = h + y @ w_out[i]
        h = h + swiglu(rms_norm(h, norm2_g[i]), ffn_w_gate[i], ffn_w_up[i], ffn_w_down[i])
    return rms_norm(h, norm_f_g)
```

```python
import numpy as np
from contextlib import ExitStack
import concourse.bass as bass
import concourse.mybir as mybir
from concourse.bass_utils import run_bass_kernel_spmd

F32 = mybir.dt.float32
BF16 = mybir.dt.bfloat16
AF = mybir.ActivationFunctionType
ALU = mybir.AluOpType
AX = mybir.AxisListType

ENGS = ['tensor', 'vector', 'scalar', 'gpsimd', 'sync']

D = 1024
T = 8192
NT = 16
NM = 16
DFF = 2816
NFS = 22
NEGM = -30000.0
RT = NT
R2 = True
T_STEPS = 1
U_STEPS = 3
F_EVERY = 16
PE_ALT = 'vector'
U_XN_ENG = 'vector'
PE_SIDE = 'vector'
NO_SELF_SYNC = ()
R_STAG = 'inproj'
NQ = NM
NSUB = 'abc'
NAV = '123'


class Prog:
    def __init__(self, nc):
        self.nc = nc
        self.ops = {e: [] for e in ENGS}
        self.cnt = {e: 0 for e in ENGS}
        self.seen = {e: {} for e in ENGS}
        self.lastw = {}
        self.lastr = {}
        self.dcnt = {}

    def _deps(self, eng, reads, writes):
        need = {}

        def add(k, v):
            if k == eng and (eng == 'tensor' or eng in NO_SELF_SYNC):
                return
            if need.get(k, 0) < v:
                need[k] = v
        for r in reads:
            if r in self.lastw:
                add(*self.lastw[r])
        for w in writes:
            if w in self.lastw:
                add(*self.lastw[w])
            for k, v in self.lastr.get(w, {}).items():
                if k != eng:
                    add(k, v)
        out = []
        for k, v in need.items():
            if self.seen[eng].get(k, 0) < v:
                self.seen[eng][k] = v
                out.append((k, v))
        return out

    EXCL = ('pb', 'sc', 'pc', 'pm', 'Os', 'Ow', 'pT', 'pa')

    def _x(self, r, w):
        xr = [k for k in r if k.startswith(self.EXCL)]
        if xr:
            return [k for k in r if k not in xr], list(w) + xr
        return r, w

    def op(self, eng, fn, r=(), w=()):
        r, w = self._x(r, w)
        waits = self._deps(eng, r, w)
        self.cnt[eng] += 1
        c = self.cnt[eng]
        self.ops[eng].append((waits, fn, (eng, 1)))
        for x in r:
            self.lastr.setdefault(x, {})[eng] = c
        for x in w:
            self.lastw[x] = (eng, c)
            self.lastr[x] = {}

    def dma(self, eng, out, in_, r=(), w=(), sem='dma', **kw):
        waits = self._deps(eng, r, w)
        self.dcnt[sem] = self.dcnt.get(sem, 0) + 16
        c = self.dcnt[sem]
        self.ops[eng].append((waits, lambda e: e.dma_start(out=out, in_=in_, **kw), (sem, 16)))
        for x in r:
            self.lastr.setdefault(x, {})[sem] = c
        for x in w:
            self.lastw[x] = (sem, c)
            self.lastr[x] = {}

    def cc(self, fn, r=(), w=(), sem='cc'):
        waits = self._deps('gpsimd', r, w)
        self.dcnt[sem] = self.dcnt.get(sem, 0) + 1
        c = self.dcnt[sem]
        self.ops['gpsimd'].append((waits, fn, (sem, 1)))
        for x in r:
            self.lastr.setdefault(x, {})[sem] = c
        for x in w:
            self.lastw[x] = (sem, c)
            self.lastr[x] = {}

    def finish_group(self, sem):
        c = self.dcnt[sem]
        for k, (s, v) in list(self.lastw.items()):
            if s == sem:
                self.lastw[k] = (s, c)

    def mm(self, out, lhsT, rhs, start, stop, r, w):
        self.op('tensor', lambda e: e.matmul(out, lhsT=lhsT, rhs=rhs, start=start, stop=stop), r, w)

    def tr(self, out, in_, ident, r, w):
        self.op('tensor', lambda e: e.transpose(out, in_, ident), r, w)

    def act(self, out, in_, func, r, w, **kw):
        self.op('scalar', lambda e: e.activation(out=out, in_=in_, func=func, **kw), r, w)

    def tt(self, eng, out, in0, in1, op, r, w):
        self.op(eng, lambda e: e.tensor_tensor(out=out, in0=in0, in1=in1, op=op), r, w)

    def ts(self, eng, out, in0, s1, s2, op0, op1, r, w):
        if eng == 'scalar':
            assert s2 is None and op0 == ALU.mult
            self.op(eng, lambda e: e.activation(out=out, in_=in0, func=AF.Copy, scale=s1), r, w)
            return
        if s2 is None:
            self.op(eng, lambda e: e.tensor_scalar(out=out, in0=in0, scalar1=s1, scalar2=None, op0=op0), r, w)
        else:
            self.op(eng, lambda e: e.tensor_scalar(out=out, in0=in0, scalar1=s1, scalar2=s2, op0=op0, op1=op1), r, w)

    def stt(self, out, in0, scalar, in1, op0, op1, r, w):
        self.op('vector', lambda e: e.scalar_tensor_tensor(out=out, in0=in0, scalar=scalar, in1=in1, op0=op0, op1=op1), r, w)

    def cp(self, eng, out, in_, r, w):
        if eng == 'scalar':
            self.op(eng, lambda e: e.activation(out=out, in_=in_, func=AF.Copy), r, w)
        else:
            self.op(eng, lambda e: e.tensor_copy(out=out, in_=in_), r, w)

    def emit(self, stack):
        nc = self.nc
        names = sorted(set(ENGS) | set(self.dcnt.keys()))
        sems = {n: stack.enter_context(nc.semaphore('s_' + n)) for n in names}
        final = dict(self.cnt)
        final.update(self.dcnt)
        block = stack.enter_context(nc.Block())

        def mk(engname):
            def body(e):
                for waits, fn, inc in self.ops[engname]:
                    for k, v in waits:
                        e.wait_ge(sems[k], v)
                    fn(e).then_inc(sems[inc[0]], inc[1])
                for k in names:
                    if final.get(k, 0) > 0:
                        e.wait_ge(sems[k], final[k])
            return body
        block.tensor(mk('tensor'))
        block.vector(mk('vector'))
        block.scalar(mk('scalar'))
        block.gpsimd(mk('gpsimd'))
        block.sync(mk('sync'))


def rr(lst, i):
    return lst[i % len(lst)]


def block_u(nc, xb, xo, UT, UTo, ident_d):
    P = Prog(nc)
    with ExitStack() as st:
        sb = lambda n, s, d: st.enter_context(nc.sbuf_tensor(n, s, d))
        ident = sb("u_ident", [128, 128], BF16)
        xt = [sb(f"u_xt{i}", [128, D], F32) for i in range(2)]
        junk = sb("u_junk", [128, D], BF16)
        xn = [sb(f"u_xn{i}", [128, D], BF16) for i in range(2)]
        ss = [sb(f"u_ss{i}", [128, 1], F32) for i in range(2)]
        uT = [sb(f"u_uT{i}", [128, 8, 512], BF16) for i in range(2)]
        pT = [st.enter_context(nc.psum_tensor(f"u_pT{i}", [128, 8, 128], BF16)) for i in range(2)]
        P.dma('sync', ident[:], ident_d, w=['ident'], sem='c')
        nsub = 64 + 16
        for s in range(nsub):
            b = s % 2
            g = s // 4
            gb = g % 2
            if s < 64:
                src = xb[s * 128:(s + 1) * 128, :]
            else:
                src = xo[(s - 64) * 128:(s - 63) * 128, :]
            P.dma('sync', xt[b][:], src, w=[f'xt{b}'], sem=f'x{b}')
            P.act(junk[:], xt[b][:], AF.Square, r=[f'xt{b}'], w=['junk', f'ss{b}'], accum_out=ss[b][:])
            P.ts('vector', ss[b][:], ss[b][:], 1.0 / D, 1e-6, ALU.mult, ALU.add, r=[f'ss{b}'], w=[f'ss{b}'])
            P.act(ss[b][:], ss[b][:], AF.Sqrt, r=[f'ss{b}'], w=[f'ss{b}'])
            P.op('vector', lambda e, b=b: e.reciprocal(out=ss[b][:], in_=ss[b][:]), r=[f'ss{b}'], w=[f'ss{b}'])
            P.ts('vector', xn[b][:], xt[b][:], ss[b][:, 0:1], None, ALU.mult, None, r=[f'xt{b}', f'ss{b}'], w=[f'xn{b}'])
            for c in range(8):
                P.tr(pT[b][:, c, :], xn[b][:, c * 128:(c + 1) * 128], ident[:], r=[f'xn{b}', 'ident'], w=[f'pT{b}'])
            sub = s % 4
            P.cp('scalar', uT[gb][:, :, sub * 128:(sub + 1) * 128], pT[b][:], r=[f'pT{b}'], w=[f'uT{gb}'])
            if sub == 3:
                if s < 64:
                    dst = UT[:, :, g * 512:(g + 1) * 512]
                else:
                    dst = UTo[:, :, (g - 16) * 512:(g - 15) * 512]
                P.dma('gpsimd', dst, uT[gb][:], r=[f'uT{gb}'], w=['UTd'], sem=f'us{gb}')
        P.emit(st)


def block_f(nc, xo, YN, YG, selt_d, w_out, g2T_d, w_gate, w_up, w_down, gfb_d, ident_d, out, WGU, use_y=True, wgu_done=False,
            WOb=None, WDb=None):
    P = Prog(nc)
    with ExitStack() as st:
        sb = lambda n, s, d: st.enter_context(nc.sbuf_tensor(n, s, d))
        ps = lambda n, s, d: st.enter_context(nc.psum_tensor(n, s, d))
        ident = sb("f_ident", [128, 128], BF16)
        g2T = sb("f_g2T", [128, 8], F32)
        selt = sb("f_selt", [128, 4], F32)
        gfb = sb("f_gfb", [128, D], F32)
        wo = sb("f_wo", [128, 8, D], BF16)
        wd = sb("f_wd", [128, NFS, D], BF16)
        wsl = [sb(f"f_wsl{i}", [128, 2, 8, 128], BF16) for i in range(3)]
        if not wgu_done:
            stg = [sb(f"f_stg{i}", [128, DFF], F32) for i in range(2)]
            cbuf = [sb(f"f_cbuf{i}", [128, DFF], BF16) for i in range(2)]
        xt = [sb(f"f_xt{i}", [128, D], F32) for i in range(2)]
        cand = [sb(f"f_cand{i}", [128, 4, 512], BF16) for i in range(2)]
        ycat = [sb(f"f_ycat{i}", [128, D], BF16) for i in range(2)]
        yT = [sb(f"f_yT{i}", [128, 8, 128], BF16) for i in range(2)]
        hsb = [[sb(f"f_h{a}_{i}", [128, D], F32) for i in range(4)] for a in range(2)]
        hn = [sb(f"f_hn{i}", [128, D], BF16) for i in range(2)]
        junk = sb("f_junk", [128, D], BF16)
        ss = [sb(f"f_ss{i}", [128, 1], F32) for i in range(4)]
        hnT = [sb(f"f_hnT{i}", [128, 8, 512], BF16) for i in range(2)]
        sg = [sb(f"f_sg{i}", [128, 512], F32) for i in range(2)]
        hid = sb("f_hid", [128, NFS, 512], BF16)
        h2 = [sb(f"f_h2{i}", [128, D], F32) for i in range(2)]
        pT = [ps(f"f_pT{i}", [128, 8, 128], BF16) for i in range(2)]
        pa = [ps(f"f_pa{i}", [128, 512], F32) for i in range(6)]

        P.dma('sync', ident[:], ident_d, w=['ident'], sem='c')
        P.dma('sync', g2T[:], g2T_d, w=['g2T'], sem='c')
        P.dma('sync', selt[:], selt_d, w=['selt'], sem='c')
        P.dma('sync', gfb[:], gfb_d, w=['gfb'], sem='c')
        if wgu_done:
            P.dma('sync', wo[:], WOb, w=['wo'], sem='c')
            P.dma('gpsimd', wd[:, 0:11, :], WDb[:, 0:11, :], w=['wd'], sem='cwd')
            P.dma('gpsimd', wd[:, 11:22, :], WDb[:, 11:22, :], w=['wd'], sem='cwd')
            P.finish_group('cwd')
        P.finish_group('c')
        k = 0
        engs = ['gpsimd', 'vector']
        if not wgu_done:
            wov = w_out.rearrange("(c p) n -> p c n", p=128)
            wgv = w_gate.rearrange("(c p) n -> p c n", p=128)
            wuv = w_up.rearrange("(c p) n -> p c n", p=128)
            wdv = w_down.rearrange("(c p) n -> p c n", p=128)
            WGUv = WGU.rearrange("fs p w c n -> p w c fs n")
            for wi, wv in enumerate((wgv, wuv)):
                for c in range(8):
                    b = k % 2
                    P.dma('sync', stg[b][:], wv[:, c, :], w=[f'stg{b}'], sem=f'w{b}')
                    P.ts(rr(engs, k), cbuf[b][:], stg[b][:], g2T[:, c:c + 1], None, ALU.mult, None, r=[f'stg{b}', 'g2T'], w=[f'cbuf{b}'])
                    P.dma('gpsimd', WGUv[:, wi, c, :, :], cbuf[b][:].rearrange("p (fs n) -> p fs n", n=128), r=[f'cbuf{b}'], w=['WGU'],
                          sem=f'cs{b}')
                    k += 1
            for c in range(8):
                b = k % 2
                P.dma('sync', stg[b][:, 0:D], wov[:, c, :], w=[f'stg{b}'], sem=f'w{b}')
                P.cp(rr(engs, k), wo[:, c, :], stg[b][:, 0:D], r=[f'stg{b}'], w=['wo'])
                k += 1
            for c in range(0, NFS, 2):
                b = k % 2
                P.dma('sync', stg[b][:, 0:2 * D].rearrange("p (c n) -> p c n", c=2), wdv[:, c:c + 2, :], w=[f'stg{b}'], sem=f'w{b}')
                P.cp(rr(engs, k), wd[:, c:c + 2, :], stg[b][:, 0:2 * D].rearrange("p (c n) -> p c n", c=2), r=[f'stg{b}'], w=['wd'])
                k += 1

        def rms_scale(src, srckey, b):
            P.act(junk[:], src, AF.Square, r=[srckey], w=['junk', f'ss{b}'], accum_out=ss[b][:])
            P.ts('vector', ss[b][:], ss[b][:], 1.0 / D, 1e-6, ALU.mult, ALU.add, r=[f'ss{b}'], w=[f'ss{b}'])
            P.act(ss[b][:], ss[b][:], AF.Sqrt, r=[f'ss{b}'], w=[f'ss{b}'])
            P.op('vector', lambda e: e.reciprocal(out=ss[b][:], in_=ss[b][:]), r=[f'ss{b}'], w=[f'ss{b}'])

        def front(gq):
            a = gq % 2
            for sub in range(4):
                m = gq * 4 + sub
                b = m % 2
                hk = f'h{a}_{sub}'
                P.dma('sync', xt[b][:], xo[m * 128:(m + 1) * 128, :], w=[f'xt{b}'], sem=f'x{b}')
                if use_y:
                    P.dma('gpsimd', ycat[b][:, 512:1024], YN[m * 128:(m + 1) * 128, :], w=[f'ycatn{b}'], sem=f'yn{b}')
                    ygv = YG[m // 4].rearrange("(r q t) f -> r t q f", r=4, t=128)
                    for r_ in range(4):
                        P.dma('gpsimd', cand[b][:, :, r_ * 128:(r_ + 1) * 128], ygv[r_][:, 4 * (m % 4):4 * (m % 4) + 4, :], w=[f'cand{b}'],
                              sem=f'cd{b}')
                    yield
                    P.ts('vector', ycat[b][:, 0:512], cand[b][:, 0, :], selt[:, 0:1], None, ALU.mult, None,
                         r=[f'cand{b}', 'selt'], w=[f'ycatr{b}'])
                    for jj in range(1, 4):
                        P.stt(ycat[b][:, 0:512], cand[b][:, jj, :], selt[:, jj:jj + 1], ycat[b][:, 0:512], ALU.mult, ALU.add,
                              r=[f'cand{b}', f'ycatr{b}'], w=[f'ycatr{b}'])
                    yield
                    for c in range(8):
                        P.tr(pT[0][:, c, :], ycat[b][:, c * 128:(c + 1) * 128], ident[:],
                             r=[f'ycatr{b}', f'ycatn{b}', 'ident'], w=['pT0'])
                    P.cp('scalar', yT[b][:], pT[0][:], r=['pT0'], w=[f'yT{b}'])
                    yield
                    for nh in range(2):
                        for c in range(8):
                            P.mm(pa[nh][:], yT[b][:, c, :], wo[:, c, nh * 512:(nh + 1) * 512], c == 0, c == 7,
                                 r=[f'yT{b}', 'wo'], w=[f'pa{nh}'])
                        P.tt('vector', hsb[a][sub][:, nh * 512:(nh + 1) * 512], pa[nh][:], xt[b][:, nh * 512:(nh + 1) * 512], ALU.add,
                             r=[f'pa{nh}', f'xt{b}'], w=[hk])
                    yield
                else:
                    P.cp('vector', hsb[a][sub][:], xt[b][:], r=[f'xt{b}'], w=[hk])
                rms_scale(hsb[a][sub][:], hk, b)
                P.ts('vector', hn[b][:], hsb[a][sub][:], ss[b][:, 0:1], None, ALU.mult, None, r=[hk, f'ss{b}'], w=[f'hn{b}'])
                yield
                for c in range(8):
                    P.tr(pT[1][:, c, :], hn[b][:, c * 128:(c + 1) * 128], ident[:], r=[f'hn{b}', 'ident'], w=['pT1'])
                P.cp('scalar', hnT[a][:, :, sub * 128:(sub + 1) * 128], pT[1][:], r=['pT1'], w=[f'hnT{a}'])
                yield

        def step(g_, n=1):
            if g_ is None:
                return None
            for _ in range(n):
                try:
                    next(g_)
                except StopIteration:
                    return None
            return g_

        fr = front(0)
        while fr is not None:
            fr = step(fr)
        wk = 0
        for gq in range(4):
            a = gq % 2
            nf = front(gq + 1) if gq < 3 else None
            for fs in range(NFS):
                b = fs % 2
                wb = wk % 3
                wk += 1
                P.dma('sync', wsl[wb][:], WGU[fs], r=['WGU'], w=[f'wsl{wb}'], sem=f'wl{wb}')
                pg, pu = pa[2 + 2 * b], pa[3 + 2 * b]
                for c in range(8):
                    P.mm(pg[:], wsl[wb][:, 0, c, :], hnT[a][:, c, :], c == 0, c == 7, r=[f'wsl{wb}', f'hnT{a}'], w=[f'pa{2 + 2 * b}'])
                for c in range(8):
                    P.mm(pu[:], wsl[wb][:, 1, c, :], hnT[a][:, c, :], c == 0, c == 7, r=[f'wsl{wb}', f'hnT{a}'], w=[f'pa{3 + 2 * b}'])
                P.act(sg[b][:], pg[:], AF.Silu, r=[f'pa{2 + 2 * b}'], w=[f'sg{b}'])
                P.tt('vector', hid[:, fs, :], sg[b][:], pu[:], ALU.mult, r=[f'sg{b}', f'pa{3 + 2 * b}'], w=[f'hid{fs}'])
                nf = step(nf)
            for sub in range(4):
                m = gq * 4 + sub
                b = sub % 2
                for nh in range(2):
                    for fs in range(NFS):
                        P.mm(pa[nh][:], hid[:, fs, sub * 128:(sub + 1) * 128], wd[:, fs, nh * 512:(nh + 1) * 512], fs == 0, fs == NFS - 1,
                             r=[f'hid{fs}', 'wd'], w=[f'pa{nh}'])
                    P.tt('vector', h2[b][:, nh * 512:(nh + 1) * 512], pa[nh][:], hsb[a][sub][:, nh * 512:(nh + 1) * 512], ALU.add,
                         r=[f'pa{nh}', f'h{a}_{sub}'], w=[f'h2{b}'])
                rms_scale(h2[b][:], f'h2{b}', 2 + b)
                P.stt(h2[b][:], h2[b][:], ss[2 + b][:, 0:1], gfb[:], ALU.mult, ALU.mult, r=[f'h2{b}', f'ss{2 + b}', 'gfb'], w=[f'h2{b}'])
                P.dma('sync', out[m * 128:(m + 1) * 128, :], h2[b][:], r=[f'h2{b}'], w=['outd'], sem=f'o{b}')
                nf = step(nf)
            while nf is not None:
                nf = step(nf)
        P.emit(st)


def block_r(nc, UT, w_rw, g1T_d, rwp_d, w2a2_d, g2_d, lnwb_d, rmask_d, bones_d, ident32_d, resetm_d, YR, ntiles=NT):
    P = Prog(nc)
    with ExitStack() as st:
        sb = lambda n, s, d=F32: st.enter_context(nc.sbuf_tensor("r_" + n, s, d))
        pb = [st.enter_context(nc.psum_tensor(f"r_pb{i}", [128, 512], F32)) for i in range(8)]
        g1T = sb("g1T", [128, 8])
        rwp = sb("rwp", [128, 16])
        w2a2 = sb("w2a2", [128, 128])
        g2 = sb("g2", [128, 128])
        lnwb = sb("lnwb", [128, 2, 64])
        rmask = sb("rmask", [128, 3, 64])
        bones = sb("bones", [128, 128])
        ident = sb("ident", [128, 128])
        resetm = sb("resetm", [128, 512])
        ones2 = sb("ones2", [128, 2])
        negw0 = sb("negw0", [128, 1])
        wrw = sb("wrw", [128, 8, 640], BF16)
        stg = [sb(f"stg{i}", [128, 640]) for i in range(2)]
        uT = [sb(f"uT{i}", [128, 8, 512], BF16) for i in range(2)]
        praw = sb("praw", [128, 5, 513])
        psh = sb("psh", [128, 5, 512])
        tA = sb("tA", [128, 512]); e2 = sb("e2", [128, 512]); cumE = sb("cumE", [128, 512]); cumX = sb("cumX", [128, 512])
        Wc = sb("Wc", [128, 512]); Winv = sb("Winv", [128, 512]); Wprev = sb("Wprev", [128, 512])
        av = sb("a", [128, 512]); kk = sb("kk", [128, 512]); tB = sb("tB", [128, 512]); km = sb("km", [128, 512])
        tK = sb("tK", [128, 512]); tX = sb("tX", [128, 512]); Bt = sb("Bt", [128, 512])
        AR = sb("AR", [128, 8, 2, 64])
        bdA = sb("bdA", [128, 8, 128]); bdR = sb("bdR", [128, 8, 128]); bdK = sb("bdK", [128, 8, 128])
        bdB = sb("bdB", [128, 8, 128]); bdV = sb("bdV", [128, 8, 128]); bdX = sb("bdX", [128, 8, 128])
        bdArb = sb("bdArb", [128, 8, 128]); bdAak = sb("bdAak", [128, 8, 128]); bdArk = sb("bdArk", [128, 8, 128])
        Lb = [sb(f"L{i}", [128, 8, 128]) for i in range(2)]
        Gb = [sb(f"G{i}", [128, 8, 128]) for i in range(2)]
        Qb = [sb(f"Q{i}", [128, 8, 128]) for i in range(2)]
        KT = sb("KT", [128, 8, 128]); BT = sb("BT", [128, 8, 128])
        Vst = sb("Vst", [128, 8, 64]); AV = sb("AV", [128, 8, 64]); YV = sb("YV", [128, 8, 64]); KV = sb("KV", [128, 8, 64])
        rk = sb("rk", [128, 8, 2]); gtok = sb("gtok", [128, 8, 64]); Ysb = sb("Ysb", [128, 8, 64])
        tq = sb("tq", [128, 8, 64]); yn = sb("yn", [128, 8, 64]); yo = sb("yo", [128, 8, 64], BF16)
        st8 = sb("st8", [128, 4, 8])
        H = [sb(f"H{i}", [128, 64]) for i in range(2)]
        rhsb = sb("rhsb", [128, 64]); Usb = sb("Usb", [128, 64]); tw = sb("tw", [128, 64])

        for (t, d_, k) in ((g1T, g1T_d, 'g1T'), (rwp, rwp_d, 'rwp'), (w2a2, w2a2_d, 'w2a2'), (g2, g2_d, 'g2'),
                           (lnwb, lnwb_d, 'lnwb'), (rmask, rmask_d, 'rmask'), (bones, bones_d, 'bones'),
                           (ident, ident32_d, 'ident'), (resetm, resetm_d, 'resetm')):
            P.dma('sync', t[:], d_, w=[k], sem='c')
        P.finish_group('c')
        wv = w_rw.rearrange("(c p) n -> p c n", p=128)
        for c in range(8):
            b = c % 2
            P.dma('sync', stg[b][:], wv[:, c, :], w=[f'stg{b}'], sem=f'w{b}')
            P.ts(rr(['gpsimd', 'vector'], c), wrw[:, c, :], stg[b][:], g1T[:, c:c + 1], None, ALU.mult, None,
                 r=[f'stg{b}', 'g1T'], w=['wrw'])
        for t, k in ((bdA, 'bdA'), (bdR, 'bdR'), (bdK, 'bdK'), (bdB, 'bdB'), (bdV, 'bdV'), (bdX, 'bdX'), (bdArb, 'bdArb'),
                     (bdAak, 'bdAak'), (bdArk, 'bdArk'), (Lb[0], 'L0'), (Gb[0], 'G0')):
            P.op('gpsimd', lambda e, t=t: e.memset(t[:], 0.0), w=[k])
        P.op('vector', lambda e: e.memset(H[0][:], 0.0), w=['H0'])
        P.op('vector', lambda e: e.memset(ones2[:], 1.0), w=['ones2'])
        P.op('vector', lambda e: e.memset(praw[:, :, 0:1], 0.0), w=['praw'])
        P.ts('vector', negw0[:], rwp[:, 5:6], -1.0, None, ALU.mult, None, r=['rwp'], w=['negw0'])

        c3 = lambda ap: ap.rearrange("p (c t) -> p c t", t=64)
        mTs, mTi, msk = rmask[:, 0, :], rmask[:, 1, :], rmask[:, 2, :]
        YRv = [y.rearrange("(q t) (h v) -> h t q v", t=64, h=2) for y in YR]

        for i in range(ntiles):
            b = i % 2
            P.dma('sync', uT[b][:], UT[:, :, i * 512:(i + 1) * 512], w=[f'uT{b}'], sem=f'u{b}')
            if i > 0:
                P.cp('gpsimd', praw[:, :, 0:1], praw[:, :, 512:513], r=['praw'], w=['praw'])
            for g in range(5):
                pg = pb[g % 2]
                for c in range(8):
                    P.mm(pg[:], wrw[:, c, g * 128:(g + 1) * 128], uT[b][:, c, :], c == 0, c == 7, r=['wrw', f'uT{b}'], w=[f'pb{g % 2}'])
                P.cp('scalar', praw[:, g, 1:513], pg[:], r=[f'pb{g % 2}'], w=['praw'])
            for g in range(5):
                P.tt('gpsimd', tA[:], praw[:, g, 0:512], praw[:, g, 1:513], ALU.subtract, r=['praw'], w=['tA'])
                P.stt(psh[:, g, :], tA[:], rwp[:, g:g + 1], praw[:, g, 1:513], ALU.mult, ALU.add, r=['tA', 'rwp', 'praw'], w=[f'psh{g}'])
            r_, k_, v_ = psh[:, 0, :], psh[:, 1, :], psh[:, 2, :]
            P.act(psh[0:64, 3, :], psh[0:64, 3, :], AF.Tanh, r=['psh3'], w=['psh3'])
            P.mm(pb[2][:], w2a2[0:64, :], psh[0:64, 3, :], True, True, r=['w2a2', 'psh3'], w=['pb2'])
            P.act(tA[:], pb[2][:], AF.Exp, r=['pb2', 'negw0'], w=['tA'], scale=-1.0, bias=negw0[:, 0:1])
            P.act(tA[:], tA[:], AF.Ln, r=['tA'], w=['tA'], bias=1.0)
            P.act(e2[:], tA[:], AF.Exp, r=['tA'], w=['e2'], scale=-1.0, bias=-0.5)
            P.op('vector', lambda e: e.tensor_tensor_scan(out=cumE[:], data0=resetm[:], data1=e2[:], initial=0.0,
                                                          op0=ALU.mult, op1=ALU.add), r=['resetm', 'e2'], w=['cumE'])
            P.tt('gpsimd', cumX[:], cumE[:], e2[:], ALU.subtract, r=['cumE', 'e2'], w=['cumX'])
            P.act(Wc[:], cumE[:], AF.Exp, r=['cumE'], w=['Wc'], scale=-1.0)
            P.act(Winv[:], cumE[:], AF.Exp, r=['cumE'], w=['Winv'])
            P.act(Wprev[:], cumX[:], AF.Exp, r=['cumX'], w=['Wprev'], scale=-1.0)
            P.mm(pb[3][:], w2a2[64:128, :], psh[64:128, 3, :], True, True, r=['w2a2', 'psh3'], w=['pb3'])
            P.act(av[:], pb[3][:], AF.Sigmoid, r=['pb3', 'rwp'], w=['a'], bias=rwp[:, 6:7])
            P.act(psh[:, 4, :], psh[:, 4, :], AF.Sigmoid, r=['psh4'], w=['psh4'])
            P.ts('gpsimd', kk[:], k_, rwp[:, 7:8], None, ALU.mult, None, r=['psh1', 'rwp'], w=['kk'])
            P.tt('gpsimd', tB[:], kk[:], kk[:], ALU.mult, r=['kk'], w=['tB'])
            P.mm(pb[2][:], bones[:], tB[:], True, True, r=['bones', 'tB'], w=['pb2'])
            P.act(tB[:], pb[2][:], AF.Sqrt, r=['pb2'], w=['tB'])
            P.ts('vector', tB[:], tB[:], 1e-12, None, ALU.max, None, r=['tB'], w=['tB'])
            P.op('vector', lambda e: e.reciprocal(out=tB[:], in_=tB[:]), r=['tB'], w=['tB'])
            P.tt('gpsimd', kk[:], kk[:], tB[:], ALU.mult, r=['kk', 'tB'], w=['kk'])
            P.ts('vector', tB[:], av[:], -1.0, rwp[:, 8:9], ALU.add, ALU.mult, r=['a', 'rwp'], w=['tB'])
            P.stt(km[:], tB[:], 1.0, k_, ALU.add, ALU.mult, r=['tB', 'psh1'], w=['km'])
            P.tt('gpsimd', av[:], kk[:], av[:], ALU.mult, r=['kk', 'a'], w=['a'])
            P.stt(AR[:, :, 0, :], c3(kk[:]), -1.0, c3(Wprev[:]), ALU.mult, ALU.mult, r=['kk', 'Wprev'], w=['AR'])
            P.tt('gpsimd', AR[:, :, 1, :], c3(r_), c3(Wc[:]), ALU.mult, r=['psh0', 'Wc'], w=['AR'])
            P.tt('vector', Bt[:], av[:], Winv[:], ALU.mult, r=['a', 'Winv'], w=['Bt'])
            P.tt('gpsimd', tK[:], km[:], Winv[:], ALU.mult, r=['km', 'Winv'], w=['tK'])
            P.stt(tX[:], r_, rwp[:, 9:10], km[:], ALU.mult, ALU.mult, r=['psh0', 'rwp', 'km'], w=['tX'])
            for h in range(2):
                hs = slice(64 * h, 64 * h + 64)
                for (dst, src, dk, sk, eng) in ((bdA, AR[hs, :, 0, :], 'bdA', 'AR', 'gpsimd'), (bdR, AR[hs, :, 1, :], 'bdR', 'AR', 'scalar'),
                                                (bdK, c3(tK[hs, :]), 'bdK', 'tK', 'gpsimd'), (bdB, c3(Bt[hs, :]), 'bdB', 'Bt', 'scalar'),
                                                (bdV, c3(psh[hs, 2, :]), 'bdV', 'psh2', 'gpsimd'), (bdX, c3(tX[hs, :]), 'bdX', 'tX', 'scalar')):
                    P.cp(eng, dst[hs, :, hs], src, r=[sk], w=[dk])
            for ch in range(8):
                q, o = ch // 4, (ch % 4) * 128
                arc = AR[:, ch, :, :].rearrange("p a t -> p (a t)")
                P.mm(pb[3 + q][:, o:o + 128], bdB[:, ch, :], arc, True, True, r=['bdB', 'AR'], w=[f'pb{3 + q}'])
                P.mm(pb[5 + q][:, o:o + 128], bdK[:, ch, :], arc, True, True, r=['bdK', 'AR'], w=[f'pb{5 + q}'])
                P.mm(pb[2][:, ch * 64:(ch + 1) * 64], bdA[:, ch, :], Bt[:, ch * 64:(ch + 1) * 64], True, True, r=['bdA', 'Bt'], w=['pb2'])
            L, G, Q = Lb[0], Gb[0], Qb[0]
            for h in range(2):
                hs = slice(64 * h, 64 * h + 64)
                bc4 = lambda m_: m_[hs, :].unsqueeze(1).to_broadcast([64, 4, 64])
                for q in range(2):
                    cs = slice(4 * q, 4 * q + 4)
                    v1 = pb[3 + q][hs, :].rearrange("p (c a t) -> p c a t", c=4, a=2)
                    v2 = pb[5 + q][hs, :].rearrange("p (c a t) -> p c a t", c=4, a=2)
                    P.tt('vector', L[hs, cs, hs], v1[:, :, 0, :], bc4(mTs), ALU.mult, r=[f'pb{3 + q}', 'rmask'], w=['L0'])
                    P.tt('vector', bdArb[hs, cs, hs], v1[:, :, 1, :], bc4(mTi), ALU.mult, r=[f'pb{3 + q}', 'rmask'], w=['bdArb'])
                    P.tt('vector', bdAak[hs, cs, hs], v2[:, :, 0, :], bc4(mTs), ALU.mult, r=[f'pb{5 + q}', 'rmask'], w=['bdAak'])
                    P.tt('vector', bdArk[hs, cs, hs], v2[:, :, 1, :], bc4(mTi), ALU.mult, r=[f'pb{5 + q}', 'rmask'], w=['bdArk'])
                P.tt('vector', G[hs, :, hs], c3(pb[2][hs, :]), msk[hs, :].unsqueeze(1).to_broadcast([64, 8, 64]), ALU.mult,
                     r=['pb2', 'rmask'], w=['G0'])
            P.tt('gpsimd', Q[:], L[:], ident[:].unsqueeze(1).to_broadcast([128, 8, 128]), ALU.add, r=['L0', 'ident'], w=['Q0'])
            li = gi = qi = 0
            for lev in range(5):
                Ln_, Gn_, Qn_ = Lb[1 - li], Gb[1 - gi], Qb[1 - qi]
                for ch in range(8):
                    q, o = ch // 4, (ch % 4) * 128
                    P.mm(pb[3 + q][:, o:o + 128], Lb[li][:, ch, :], Gb[gi][:, ch, :], True, True, r=[f'L{li}', f'G{gi}'], w=[f'pb{3 + q}'])
                if lev < 4:
                    for ch in range(8):
                        q, o = ch // 4, (ch % 4) * 128
                        P.mm(pb[5 + q][:, o:o + 128], Gb[gi][:, ch, :], Lb[li][:, ch, :], True, True, r=[f'L{li}', f'G{gi}'], w=[f'pb{5 + q}'])
                for q in range(2):
                    P.cp('scalar', Gn_[:, 4 * q:4 * q + 4, :].rearrange("p c t -> p (c t)"), pb[3 + q][:], r=[f'pb{3 + q}'], w=[f'G{1 - gi}'])
                if lev < 4:
                    for q in range(2):
                        P.cp('vector', Ln_[:, 4 * q:4 * q + 4, :].rearrange("p c t -> p (c t)"), pb[5 + q][:], r=[f'pb{5 + q}'], w=[f'L{1 - li}'])
                gi = 1 - gi
                if lev < 4:
                    li = 1 - li
                for ch in range(8):
                    q, o = ch // 4, (ch % 4) * 128
                    P.mm(pb[q][:, o:o + 128], Gb[gi][:, ch, :], Qb[qi][:, ch, :], True, True, r=[f'G{gi}', f'Q{qi}'], w=[f'pb{q}'])
                for q in range(2):
                    P.tt('vector', Qn_[:, 4 * q:4 * q + 4, :].rearrange("p c t -> p (c t)"), pb[q][:],
                         Qb[qi][:, 4 * q:4 * q + 4, :].rearrange("p c t -> p (c t)"), ALU.add, r=[f'pb{q}', f'Q{qi}'], w=[f'Q{1 - qi}'])
                qi = 1 - qi
            TT = Qb[qi]
            TTk = f'Q{qi}'
            for (src, dst, sk, dk, p0) in ((bdK, KT, 'bdK', 'KT', 3), (bdB, BT, 'bdB', 'BT', 5)):
                for ch in range(8):
                    q, o = ch // 4, (ch % 4) * 128
                    P.tr(pb[p0 + q][:, o:o + 128], src[:, ch, :], ident[:], r=[sk, 'ident'], w=[f'pb{p0 + q}'])
                for q in range(2):
                    P.cp('scalar', dst[:, 4 * q:4 * q + 4, :].rearrange("p c t -> p (c t)"), pb[p0 + q][:], r=[f'pb{p0 + q}'], w=[dk])
            for ch in range(8):
                q, o = ch // 4, (ch % 4) * 128
                P.tr(pb[q][:, o:o + 128], bdV[:, ch, :], ident[:], r=['bdV', 'ident'], w=[f'pb{q}'])
            for h in range(2):
                hs = slice(64 * h, 64 * h + 64)
                for q in range(2):
                    P.cp('vector', Vst[hs, 4 * q:4 * q + 4, :], pb[q][hs, :].rearrange("p (c t) -> p c t", c=4)[:, :, hs],
                         r=[f'pb{q}'], w=['Vst'])
            for ch in range(8):
                cs = slice(ch * 64, ch * 64 + 64)
                P.mm(pb[3][:, cs], bdAak[:, ch, :], Vst[:, ch, :], True, True, r=['bdAak', 'Vst'], w=['pb3'])
                P.mm(pb[4][:, cs], bdArk[:, ch, :], Vst[:, ch, :], True, True, r=['bdArk', 'Vst'], w=['pb4'])
                P.mm(pb[5][:, cs], KT[:, ch, :], Vst[:, ch, :], True, True, r=['KT', 'Vst'], w=['pb5'])
                P.mm(pb[6][:, 2 * ch:2 * ch + 2], bdX[:, ch, :], ones2[:], True, True, r=['bdX', 'ones2'], w=['pb6'])
                for h in range(2):
                    hs = slice(64 * h, 64 * h + 64)
                    P.mm(pb[1][hs, cs], psh[:, 4, cs], g2[:, hs], True, True, r=['psh4', 'g2'], w=['pb1'])
            P.cp('scalar', AV[:].rearrange("p c t -> p (c t)"), pb[3][:], r=['pb3'], w=['AV'])
            P.cp('scalar', YV[:].rearrange("p c t -> p (c t)"), pb[4][:], r=['pb4'], w=['YV'])
            P.cp('scalar', KV[:].rearrange("p c t -> p (c t)"), pb[5][:], r=['pb5'], w=['KV'])
            P.cp('vector', rk[:].rearrange("p c t -> p (c t)"), pb[6][:, 0:16], r=['pb6'], w=['rk'])
            P.cp('vector', gtok[:].rearrange("p c t -> p (c t)"), pb[1][:], r=['pb1'], w=['gtok'])
            for ch in range(8):
                gc = i * 8 + ch
                hc, hn_ = gc % 2, (gc + 1) % 2
                cs = slice(ch * 64, ch * 64 + 64)
                wcol = Wc[:, ch * 64 + 63:ch * 64 + 64]
                P.mm(pb[7][:, 0:64], bdA[:, ch, :], H[hc][:], True, True, r=['bdA', f'H{hc}'], w=['pb7'])
                P.tt('vector', rhsb[:], pb[7][:, 0:64], AV[:, ch, :], ALU.add, r=['pb7', 'AV'], w=['rhsb'])
                P.tt('gpsimd', tw[:], KV[:, ch, :], H[hc][:], ALU.add, r=['KV', f'H{hc}'], w=['tw'])
                P.ts('gpsimd', tw[:], tw[:], wcol, None, ALU.mult, None, r=['tw', 'Wc'], w=['tw'])
                P.mm(pb[7][:, 64:128], TT[:, ch, :], rhsb[:], True, True, r=[TTk, 'rhsb'], w=['pb7'])
                P.cp('scalar', Usb[:], pb[7][:, 64:128], r=['pb7'], w=['Usb'])
                P.mm(pb[7][:, 128:192], BT[:, ch, :], Usb[:], True, True, r=['BT', 'Usb'], w=['pb7'])
                P.stt(H[hn_][:], pb[7][:, 128:192], wcol, tw[:], ALU.mult, ALU.add, r=['pb7', 'Wc', 'tw'], w=[f'H{hn_}'])
                P.mm(pb[2][:, cs], bdR[:, ch, :], H[hc][:], True, False, r=['bdR', f'H{hc}'], w=['pb2'])
                P.mm(pb[2][:, cs], bdArb[:, ch, :], Usb[:], False, True, r=['bdArb', 'Usb'], w=['pb2'])
            P.tt('vector', Ysb[:].rearrange("p c t -> p (c t)"), pb[2][:], YV[:].rearrange("p c t -> p (c t)"), ALU.add,
                 r=['pb2', 'YV'], w=['Ysb'])
            s1, s2, mean, rstd = st8[:, 0, :], st8[:, 1, :], st8[:, 2, :], st8[:, 3, :]
            bc8 = lambda a_: a_.unsqueeze(2).to_broadcast([128, 8, 64])
            P.op('vector', lambda e: e.tensor_reduce(out=s1, in_=Ysb[:], axis=AX.X, op=ALU.add), r=['Ysb'], w=['st8'])
            P.tt('gpsimd', tq[:], Ysb[:], Ysb[:], ALU.mult, r=['Ysb'], w=['tq'])
            P.op('vector', lambda e: e.tensor_reduce(out=s2, in_=tq[:], axis=AX.X, op=ALU.add), r=['tq'], w=['st8'])
            P.ts('vector', mean, s1, 1.0 / 64, None, ALU.mult, None, r=['st8'], w=['st8'])
            P.tt('vector', s1, mean, mean, ALU.mult, r=['st8'], w=['st8'])
            P.stt(s2, s2, 1.0 / 64, s1, ALU.mult, ALU.subtract, r=['st8'], w=['st8'])
            P.ts('vector', s2, s2, 64e-5, None, ALU.add, None, r=['st8'], w=['st8'])
            P.act(s2, s2, AF.Sqrt, r=['st8'], w=['st8'])
            P.op('vector', lambda e: e.reciprocal(out=rstd, in_=s2), r=['st8'], w=['st8'])
            P.tt('gpsimd', yn[:], Ysb[:], bc8(mean), ALU.subtract, r=['Ysb', 'st8'], w=['yn'])
            P.tt('gpsimd', yn[:], yn[:], bc8(rstd), ALU.mult, r=['yn', 'st8'], w=['yn'])
            P.tt('vector', yn[:], yn[:], lnwb[:, 0, :].unsqueeze(1).to_broadcast([128, 8, 64]), ALU.mult, r=['yn', 'lnwb'], w=['yn'])
            P.tt('gpsimd', yn[:], yn[:], lnwb[:, 1, :].unsqueeze(1).to_broadcast([128, 8, 64]), ALU.add, r=['yn', 'lnwb'], w=['yn'])
            P.tt('vector', tq[:], Vst[:], bc8(rk[:, :, 0]), ALU.mult, r=['Vst', 'rk'], w=['tq'])
            P.tt('gpsimd', yn[:], yn[:], tq[:], ALU.add, r=['yn', 'tq'], w=['yn'])
            P.tt('vector', yo[:], yn[:], gtok[:], ALU.mult, r=['yn', 'gtok'], w=['yo'])
            for h in range(2):
                P.dma('gpsimd', YRv[i // 4][h][:, 8 * (i % 4):8 * (i % 4) + 8, :], yo[64 * h:64 * h + 64, :, :], r=['yo'], w=['YRd'],
                      sem=f'yo{h}')
        P.emit(st)


def block_r2(nc, UT, w_rw, g1T_d, rwp_d, w2a2_d, g2_d, lnwb_d, rmask_d, bones_d, ident32_d, resetm_d, YR, identb_d, ntiles=32,
             xb=None, xo=None, UTo=None, fpro=None, fpro2=None):
    P = Prog(nc)
    TS, NC_ = 256, 4
    with ExitStack() as st:
        sb = lambda n, s, d=F32: st.enter_context(nc.sbuf_tensor("r_" + n, s, d))
        pball = [st.enter_context(nc.psum_tensor(f"r_pb{i}", [128, 512], F32)) for i in range(8)]
        g1T = sb("g1T", [128, 8])
        rwp = sb("rwp", [128, 16])
        w2a2 = sb("w2a2", [128, 128])
        g2 = sb("g2", [128, 128])
        lnwb = sb("lnwb", [128, 2, 64])
        rmask = sb("rmask", [128, 3, 64])
        bones = sb("bones", [128, 128])
        ident = sb("ident", [128, 128])
        resetm = sb("resetm", [128, 512])
        ones2 = sb("ones2", [128, 2], BF16)
        identb = sb("identb", [128, 128], BF16)
        g2b = sb("g2b", [128, 128], BF16)
        Hb = [sb(f"Hb{i}", [128, 64], BF16) for i in range(2)]
        negw0 = sb("negw0", [128, 1])
        wrw = sb("wrw", [128, 8, 640], BF16)
        stg = [sb(f"stg{i}", [128, 640]) for i in range(2)]
        H = [sb(f"H{i}", [128, 64]) for i in range(2)]

        uTg = [sb(f"uTg{i}", [128, 8, 512], BF16) for i in range(3)]
        u_xt = [sb(f"u_xt{i}", [128, D]) for i in range(2)]
        u_junk = sb("u_junk", [128, D], BF16)
        u_xn = [sb(f"u_xn{i}", [128, D], BF16) for i in range(2)]
        u_ss = [sb(f"u_ss{i}", [128, 1]) for i in range(2)]
        if fpro is not None:
            f_g2T = sb("f_g2T", [128, 8])
            f_stg = [sb(f"f_stg{i}", [128, DFF]) for i in range(2)]
            f_cbuf = [sb(f"f_cbuf{i}", [128, DFF], BF16) for i in range(2)]

        class S:
            pass
        sets = []
        for s in range(2):
            o = S()
            o.s = s
            o.pb = pball[4 * s:4 * s + 4]
            o.pk = [f'pb{4 * s + i}' for i in range(4)]
            t2 = lambda n: sb(f"{n}_{s}", [128, TS])
            t3 = lambda n, w_, d=F32: sb(f"{n}_{s}", [128, NC_, w_], d)
            o.praw = sb(f"praw_{s}", [128, 5, TS + 1])
            o.psh = sb(f"psh_{s}", [128, 5, TS])
            for n in ('tA', 'e2', 'cumE', 'cumX', 'Wc', 'Winv', 'Wprev', 'av', 'kk', 'tB', 'km', 'tK', 'tX'):
                setattr(o, n, t2(n))
            o.Bt = sb(f"Bt_{s}", [128, TS], BF16)
            o.sgb = sb(f"sgb_{s}", [128, TS], BF16)
            o.AR = sb(f"AR_{s}", [128, NC_, 2, 64], BF16)
            for n in ('bdA', 'bdR', 'bdK', 'bdB', 'bdV', 'bdX', 'bdArb', 'bdAak', 'bdArk', 'KT', 'BT'):
                setattr(o, n, t3(n, 128, BF16))
            o.Lb = [t3(f"L{i}", 128, BF16) for i in range(2)]
            o.Gb = [t3(f"G{i}", 128, BF16) for i in range(2)]
            o.Qb = [t3(f"Q{i}", 128, BF16) for i in range(2)]
            o.Vst = t3('Vst', 64, BF16)
            for n in ('AV', 'YV', 'KV', 'gtok', 'Ysb', 'tq', 'yn'):
                setattr(o, n, t3(n, 64))
            o.rk = t3('rk', 2)
            o.yo = sb(f"yo_{s}", [128, NC_, 64], BF16)
            o.st8 = sb(f"st8_{s}", [128, 4, NC_])
            o.rhsb = sb(f"rhsb_{s}", [128, 64], BF16); o.Usb = sb(f"Usb_{s}", [128, 64], BF16); o.tw = sb(f"tw_{s}", [128, 64])
            sets.append(o)

        for (t, d_, k) in ((g1T, g1T_d, 'g1T'), (rwp, rwp_d, 'rwp'), (w2a2, w2a2_d, 'w2a2'), (g2, g2_d, 'g2'),
                           (lnwb, lnwb_d, 'lnwb'), (rmask, rmask_d, 'rmask'), (bones, bones_d, 'bones'),
                           (ident, ident32_d, 'ident'), (resetm, resetm_d, 'resetm'), (identb, identb_d, 'identb')):
            P.dma('sync', t[:], d_, w=[k], sem='c')
        P.finish_group('c')
        wv = w_rw.rearrange("(c p) n -> p c n", p=128)
        for c in range(8):
            b = c % 2
            P.dma('sync', stg[b][:], wv[:, c, :], w=[f'stg{b}'], sem=f'w{b}')
            P.ts(rr(['vector', 'scalar'], c), wrw[:, c, :], stg[b][:], g1T[:, c:c + 1], None, ALU.mult, None,
                 r=[f'stg{b}', 'g1T'], w=['wrw'])
        for o in sets:
            for n in ('bdA', 'bdR', 'bdK', 'bdB', 'bdV', 'bdX', 'bdArb', 'bdAak', 'bdArk'):
                P.op('gpsimd', lambda e, t=getattr(o, n): e.memset(t[:], 0.0), w=[f'{n}{o.s}'])
            P.op('gpsimd', lambda e, t=o.Lb[0]: e.memset(t[:], 0.0), w=[f'L0{o.s}'])
            P.op('gpsimd', lambda e, t=o.Gb[0]: e.memset(t[:], 0.0), w=[f'G0{o.s}'])
        P.op('vector', lambda e: e.memset(H[0][:], 0.0), w=['H0'])
        P.op('vector', lambda e: e.memset(Hb[0][:], 0.0), w=['Hb0'])
        P.cp('vector', g2b[:], g2[:], r=['g2'], w=['g2b'])
        P.op('vector', lambda e: e.memset(ones2[:], 1.0), w=['ones2'])
        P.op('vector', lambda e: e.memset(sets[0].praw[:, :, 0:1], 0.0), w=['praw0'])
        P.ts('vector', negw0[:], rwp[:, 5:6], -1.0, None, ALU.mult, None, r=['rwp'], w=['negw0'])

        c3 = lambda ap: ap.rearrange("p (c t) -> p c t", t=64)
        fl = lambda ap: ap.rearrange("p c t -> p (c t)")
        mTs, mTi, msk = rmask[:, 0, :], rmask[:, 1, :], rmask[:, 2, :]
        YRv = [y.rearrange("(q t) (h v) -> h t q v", t=64, h=2) for y in YR]

        def tile(i):
            o = sets[i % 2]
            s = o.s
            K = lambda n: f'{n}{s}'
            pb, pk = o.pb, o.pk
            prev = sets[(i - 1) % 2]
            ug = uTg[(i // 2) % 3]
            ugk = f'uTg{(i // 2) % 3}'
            uc = slice((i % 2) * TS, (i % 2) * TS + TS)
            if i > 0:
                P.cp(PE_SIDE, o.praw[:, :, 0:1], prev.praw[:, :, TS:TS + 1], r=[f'praw{prev.s}'], w=[K('praw')])
            for g in range(5):
                pg = pb[g % 2]
                for c in range(8):
                    P.mm(pg[:, 0:TS], wrw[:, c, g * 128:(g + 1) * 128], ug[:, c, uc], c == 0, c == 7, r=['wrw', ugk], w=[pk[g % 2]])
                P.cp('scalar', o.praw[:, g, 1:TS + 1], pg[:, 0:TS], r=[pk[g % 2]], w=[K('praw')])
                yield ('INPROJ_DONE' if g == 4 else None)
            if R_STAG == 'inproj':
                yield 'HALF'
            for g in range(5):
                P.tt(PE_ALT, o.tA[:], o.praw[:, g, 0:TS], o.praw[:, g, 1:TS + 1], ALU.subtract, r=[K('praw')], w=[K('tA')])
                P.stt(o.psh[:, g, :], o.tA[:], rwp[:, g:g + 1], o.praw[:, g, 1:TS + 1], ALU.mult, ALU.add, r=[K('tA'), 'rwp', K('praw')],
                      w=[K(f'psh{g}')])
            yield
            r_, k_, v_ = o.psh[:, 0, :], o.psh[:, 1, :], o.psh[:, 2, :]
            P.mm(pb[1][:, 0:TS], w2a2[64:128, :], o.psh[64:128, 3, :], True, True, r=['w2a2', K('psh3')], w=[pk[1]])
            P.act(o.psh[0:64, 3, :], o.psh[0:64, 3, :], AF.Sigmoid, r=[K('psh3')], w=[K('psh3')], scale=2.0)
            P.act(o.av[:], pb[1][:, 0:TS], AF.Sigmoid, r=[pk[1], 'rwp'], w=[K('a')], bias=rwp[:, 6:7])
            P.act(o.sgb[:], o.psh[:, 4, :], AF.Sigmoid, r=[K('psh4')], w=[K('sgb')])
            P.ts('vector', o.psh[0:64, 3, :], o.psh[0:64, 3, :], 2.0, -1.0, ALU.mult, ALU.add, r=[K('psh3')], w=[K('psh3')])
            P.mm(pb[2][:, 0:TS], w2a2[0:64, :], o.psh[0:64, 3, :], True, True, r=['w2a2', K('psh3')], w=[pk[2]])
            P.act(o.tA[:], pb[2][:, 0:TS], AF.Exp, r=[pk[2], 'negw0'], w=[K('tA')], scale=-1.0, bias=negw0[:, 0:1])
            P.act(o.tA[:], o.tA[:], AF.Ln, r=[K('tA')], w=[K('tA')], bias=1.0)
            P.act(o.e2[:], o.tA[:], AF.Exp, r=[K('tA')], w=[K('e2')], scale=-1.0, bias=-0.5)
            yield
            P.op('vector', lambda e: e.tensor_tensor_scan(out=o.cumE[:], data0=resetm[:, 0:TS], data1=o.e2[:], initial=0.0,
                                                          op0=ALU.mult, op1=ALU.add), r=['resetm', K('e2')], w=[K('cumE')])
            P.tt(PE_SIDE, o.cumX[:], o.cumE[:], o.e2[:], ALU.subtract, r=[K('cumE'), K('e2')], w=[K('cumX')])
            P.act(o.Wc[:], o.cumE[:], AF.Exp, r=[K('cumE')], w=[K('Wc')], scale=-1.0)
            P.act(o.Winv[:], o.cumE[:], AF.Exp, r=[K('cumE')], w=[K('Winv')])
            P.act(o.Wprev[:], o.cumX[:], AF.Exp, r=[K('cumX')], w=[K('Wprev')], scale=-1.0)
            yield
            P.ts(PE_ALT, o.kk[:], k_, rwp[:, 7:8], None, ALU.mult, None, r=[K('psh1'), 'rwp'], w=[K('kk')])
            P.tt(PE_ALT, o.tB[:], o.kk[:], o.kk[:], ALU.mult, r=[K('kk')], w=[K('tB')])
            P.mm(pb[2][:, 0:TS], bones[:], o.tB[:], True, True, r=['bones', K('tB')], w=[pk[2]])
            P.ts('vector', o.tB[:], pb[2][:, 0:TS], 2.0 ** -60, None, ALU.max, None, r=[pk[2]], w=[K('tB')])
            P.act(o.tB[:], o.tB[:], AF.Ln, r=[K('tB')], w=[K('tB')])
            P.act(o.tB[:], o.tB[:], AF.Exp, r=[K('tB')], w=[K('tB')], scale=-0.5)
            yield
            P.tt(PE_ALT, o.kk[:], o.kk[:], o.tB[:], ALU.mult, r=[K('kk'), K('tB')], w=[K('kk')])
            P.ts('vector', o.tB[:], o.av[:], -1.0, rwp[:, 8:9], ALU.add, ALU.mult, r=[K('a'), 'rwp'], w=[K('tB')])
            P.stt(o.km[:], o.tB[:], 1.0, k_, ALU.add, ALU.mult, r=[K('tB'), K('psh1')], w=[K('km')])
            P.tt(PE_ALT, o.av[:], o.kk[:], o.av[:], ALU.mult, r=[K('kk'), K('a')], w=[K('a')])
            yield
            P.stt(o.AR[:, :, 0, :], c3(o.kk[:]), -1.0, c3(o.Wprev[:]), ALU.mult, ALU.mult, r=[K('kk'), K('Wprev')], w=[K('AR')])
            P.tt(PE_SIDE, o.AR[:, :, 1, :], c3(r_), c3(o.Wc[:]), ALU.mult, r=[K('psh0'), K('Wc')], w=[K('AR')])
            P.tt('vector', o.Bt[:], o.av[:], o.Winv[:], ALU.mult, r=[K('a'), K('Winv')], w=[K('Bt')])
            P.tt(PE_SIDE, o.tK[:], o.km[:], o.Winv[:], ALU.mult, r=[K('km'), K('Winv')], w=[K('tK')])
            P.stt(o.tX[:], r_, rwp[:, 9:10], o.km[:], ALU.mult, ALU.mult, r=[K('psh0'), 'rwp', K('km')], w=[K('tX')])
            yield
            for h in range(2):
                hs = slice(64 * h, 64 * h + 64)
                for (dst, src_, dk, sk, eng) in ((o.bdA, o.AR[hs, :, 0, :], 'bdA', 'AR', PE_SIDE), (o.bdR, o.AR[hs, :, 1, :], 'bdR', 'AR', 'scalar'),
                                                 (o.bdK, c3(o.tK[hs, :]), 'bdK', 'tK', 'vector'), (o.bdB, c3(o.Bt[hs, :]), 'bdB', 'Bt', 'scalar'),
                                                 (o.bdV, c3(o.psh[hs, 2, :]), 'bdV', 'psh2', PE_SIDE), (o.bdX, c3(o.tX[hs, :]), 'bdX', 'tX', 'vector')):
                    P.cp(eng, dst[hs, :, hs], src_, r=[K(sk)], w=[K(dk)])
            yield
            for ch in range(NC_):
                oo = ch * 128
                arc = o.AR[:, ch, :, :].rearrange("p a t -> p (a t)")
                P.mm(pb[0][:, oo:oo + 128], o.bdB[:, ch, :], arc, True, True, r=[K('bdB'), K('AR')], w=[pk[0]])
                P.mm(pb[1][:, oo:oo + 128], o.bdK[:, ch, :], arc, True, True, r=[K('bdK'), K('AR')], w=[pk[1]])
                P.mm(pb[2][:, ch * 64:(ch + 1) * 64], o.bdA[:, ch, :], o.Bt[:, ch * 64:(ch + 1) * 64], True, True, r=[K('bdA'), K('Bt')], w=[pk[2]])
            yield
            L, G, Q = o.Lb[0], o.Gb[0], o.Qb[0]
            for h in range(2):
                hs = slice(64 * h, 64 * h + 64)
                bc4 = lambda m_: m_[hs, :].unsqueeze(1).to_broadcast([64, NC_, 64])
                v1 = pb[0][hs, :].rearrange("p (c a t) -> p c a t", c=NC_, a=2)
                v2 = pb[1][hs, :].rearrange("p (c a t) -> p c a t", c=NC_, a=2)
                P.tt('vector', L[hs, :, hs], v1[:, :, 0, :], bc4(mTs), ALU.mult, r=[pk[0], 'rmask'], w=[K('L0')])
                P.tt('vector', o.bdArb[hs, :, hs], v1[:, :, 1, :], bc4(mTi), ALU.mult, r=[pk[0], 'rmask'], w=[K('bdArb')])
                P.tt('vector', o.bdAak[hs, :, hs], v2[:, :, 0, :], bc4(mTs), ALU.mult, r=[pk[1], 'rmask'], w=[K('bdAak')])
                P.tt('vector', o.bdArk[hs, :, hs], v2[:, :, 1, :], bc4(mTi), ALU.mult, r=[pk[1], 'rmask'], w=[K('bdArk')])
                P.tt('vector', G[hs, :, hs], c3(pb[2][hs, 0:TS]), bc4(msk), ALU.mult, r=[pk[2], 'rmask'], w=[K('G0')])
            P.tt(PE_ALT, Q[:], L[:], ident[:].unsqueeze(1).to_broadcast([128, NC_, 128]), ALU.add, r=[K('L0'), 'ident'], w=[K('Q0')])
            yield ('HALF' if R_STAG == 'half' else None)
            li = gi = qi = 0
            for lev in range(5):
                Ln_, Gn_, Qn_ = o.Lb[1 - li], o.Gb[1 - gi], o.Qb[1 - qi]
                for ch in range(NC_):
                    oo = ch * 128
                    P.mm(pb[0][:, oo:oo + 128], o.Lb[li][:, ch, :], o.Gb[gi][:, ch, :], True, True, r=[K(f'L{li}'), K(f'G{gi}')], w=[pk[0]])
                if lev < 4:
                    for ch in range(NC_):
                        oo = ch * 128
                        P.mm(pb[1][:, oo:oo + 128], o.Gb[gi][:, ch, :], o.Lb[li][:, ch, :], True, True, r=[K(f'L{li}'), K(f'G{gi}')], w=[pk[1]])
                yield
                P.cp('scalar', fl(Gn_[:]), pb[0][:], r=[pk[0]], w=[K(f'G{1 - gi}')])
                if lev < 4:
                    P.cp('vector', fl(Ln_[:]), pb[1][:], r=[pk[1]], w=[K(f'L{1 - li}')])
                gi = 1 - gi
                if lev < 4:
                    li = 1 - li
                for ch in range(NC_):
                    oo = ch * 128
                    P.mm(pb[2][:, oo:oo + 128], o.Gb[gi][:, ch, :], o.Qb[qi][:, ch, :], True, True, r=[K(f'G{gi}'), K(f'Q{qi}')], w=[pk[2]])
                yield
                P.tt('vector', fl(Qn_[:]), pb[2][:], fl(o.Qb[qi][:]), ALU.add, r=[pk[2], K(f'Q{qi}')], w=[K(f'Q{1 - qi}')])
                qi = 1 - qi
            TT, TTk = o.Qb[qi], K(f'Q{qi}')
            if R_STAG == 'pre':
                yield 'HALF'
            for (src_, dst, sk, dk, p0) in ((o.bdK, o.KT, 'bdK', 'KT', 0), (o.bdB, o.BT, 'bdB', 'BT', 1)):
                pbv = pb[p0][:].bitcast(BF16)
                for ch in range(NC_):
                    oo = ch * 128
                    P.tr(pbv[:, oo:oo + 128], src_[:, ch, :], identb[:], r=[K(sk), 'identb'], w=[pk[p0]])
                P.cp('scalar', fl(dst[:]), pbv[:, 0:NC_ * 128], r=[pk[p0]], w=[K(dk)])
            pbv2 = pb[2][:].bitcast(BF16)
            for ch in range(NC_):
                oo = ch * 128
                P.tr(pbv2[:, oo:oo + 128], o.bdV[:, ch, :], identb[:], r=[K('bdV'), 'identb'], w=[pk[2]])
            for h in range(2):
                hs = slice(64 * h, 64 * h + 64)
                P.cp('vector', o.Vst[hs, :, :], pbv2[hs, 0:NC_ * 128].rearrange("p (c t) -> p c t", c=NC_)[:, :, hs], r=[pk[2]], w=[K('Vst')])
            yield
            for ch in range(NC_):
                cs = slice(ch * 64, ch * 64 + 64)
                P.mm(pb[0][:, cs], o.bdAak[:, ch, :], o.Vst[:, ch, :], True, True, r=[K('bdAak'), K('Vst')], w=[pk[0]])
                P.mm(pb[0][:, 256 + ch * 64:256 + ch * 64 + 64], o.bdArk[:, ch, :], o.Vst[:, ch, :], True, True, r=[K('bdArk'), K('Vst')], w=[pk[0]])
                P.mm(pb[1][:, cs], o.KT[:, ch, :], o.Vst[:, ch, :], True, True, r=[K('KT'), K('Vst')], w=[pk[1]])
                P.mm(pb[1][:, 256 + 2 * ch:256 + 2 * ch + 2], o.bdX[:, ch, :], ones2[:], True, True, r=[K('bdX'), 'ones2'], w=[pk[1]])
                for h in range(2):
                    hs = slice(64 * h, 64 * h + 64)
                    P.mm(pb[2][hs, cs], o.sgb[:, cs], g2b[:, hs], True, True, r=[K('sgb'), 'g2b'], w=[pk[2]])
            yield
            P.cp('scalar', fl(o.AV[:]), pb[0][:, 0:256], r=[pk[0]], w=[K('AV')])
            P.cp('scalar', fl(o.YV[:]), pb[0][:, 256:512], r=[pk[0]], w=[K('YV')])
            P.cp('scalar', fl(o.KV[:]), pb[1][:, 0:256], r=[pk[1]], w=[K('KV')])
            P.cp('vector', fl(o.rk[:]), pb[1][:, 256:256 + 2 * NC_], r=[pk[1]], w=[K('rk')])
            P.cp('vector', fl(o.gtok[:]), pb[2][:, 0:256], r=[pk[2]], w=[K('gtok')])
            if R_STAG == 'chain':
                yield 'HALF'
            yield 'CHAIN'
            for ch in range(NC_):
                gc = i * NC_ + ch
                hc, hn_ = gc % 2, (gc + 1) % 2
                cs = slice(ch * 64, ch * 64 + 64)
                wcol = o.Wc[:, ch * 64 + 63:ch * 64 + 64]
                P.mm(pb[3][:, 0:64], o.bdA[:, ch, :], Hb[hc][:], True, True, r=[K('bdA'), f'Hb{hc}'], w=[pk[3]])
                P.tt('vector', o.rhsb[:], pb[3][:, 0:64], o.AV[:, ch, :], ALU.add, r=[pk[3], K('AV')], w=[K('rhsb')])
                P.tt(PE_SIDE, o.tw[:], o.KV[:, ch, :], H[hc][:], ALU.add, r=[K('KV'), f'H{hc}'], w=[K('tw')])
                P.ts(PE_SIDE, o.tw[:], o.tw[:], wcol, None, ALU.mult, None, r=[K('tw'), K('Wc')], w=[K('tw')])
                P.mm(pb[3][:, 64:128], TT[:, ch, :], o.rhsb[:], True, True, r=[TTk, K('rhsb')], w=[pk[3]])
                P.cp('scalar', o.Usb[:], pb[3][:, 64:128], r=[pk[3]], w=[K('Usb')])
                P.mm(pb[3][:, 128:192], o.BT[:, ch, :], o.Usb[:], True, True, r=[K('BT'), K('Usb')], w=[pk[3]])
                P.stt(Hb[hn_][:], pb[3][:, 128:192], wcol, o.tw[:], ALU.mult, ALU.add, r=[pk[3], K('Wc'), K('tw')], w=[f'Hb{hn_}'])
                P.stt(H[hn_][:], pb[3][:, 128:192], wcol, o.tw[:], ALU.mult, ALU.add, r=[pk[3], K('Wc'), K('tw')], w=[f'H{hn_}'])
                P.mm(pb[3][:, 256 + ch * 64:256 + ch * 64 + 64], o.bdR[:, ch, :], Hb[hc][:], True, False, r=[K('bdR'), f'Hb{hc}'], w=[pk[3]])
                P.mm(pb[3][:, 256 + ch * 64:256 + ch * 64 + 64], o.bdArb[:, ch, :], o.Usb[:], False, True, r=[K('bdArb'), K('Usb')], w=[pk[3]])
                yield
            yield 'CHAIN_DONE'
            P.tt('vector', fl(o.Ysb[:]), pb[3][:, 256:512], fl(o.YV[:]), ALU.add, r=[pk[3], K('YV')], w=[K('Ysb')])
            s1, s2, mean, rstd = o.st8[:, 0, :], o.st8[:, 1, :], o.st8[:, 2, :], o.st8[:, 3, :]
            bc8 = lambda a_: a_.unsqueeze(2).to_broadcast([128, NC_, 64])
            P.op('vector', lambda e: e.tensor_reduce(out=s1, in_=o.Ysb[:], axis=AX.X, op=ALU.add), r=[K('Ysb')], w=[K('st8')])
            P.tt(PE_SIDE, o.tq[:], o.Ysb[:], o.Ysb[:], ALU.mult, r=[K('Ysb')], w=[K('tq')])
            P.op('vector', lambda e: e.tensor_reduce(out=s2, in_=o.tq[:], axis=AX.X, op=ALU.add), r=[K('tq')], w=[K('st8')])
            P.ts('vector', mean, s1, 1.0 / 64, None, ALU.mult, None, r=[K('st8')], w=[K('st8')])
            P.tt('vector', s1, mean, mean, ALU.mult, r=[K('st8')], w=[K('st8')])
            P.stt(s2, s2, 1.0 / 64, s1, ALU.mult, ALU.subtract, r=[K('st8')], w=[K('st8')])
            P.ts('vector', s2, s2, 64e-5, None, ALU.add, None, r=[K('st8')], w=[K('st8')])
            P.act(s2, s2, AF.Ln, r=[K('st8')], w=[K('st8')])
            P.act(rstd, s2, AF.Exp, r=[K('st8')], w=[K('st8')], scale=-0.5)
            yield
            P.tt(PE_ALT, o.yn[:], o.Ysb[:], bc8(mean), ALU.subtract, r=[K('Ysb'), K('st8')], w=[K('yn')])
            P.tt(PE_ALT, o.yn[:], o.yn[:], bc8(rstd), ALU.mult, r=[K('yn'), K('st8')], w=[K('yn')])
            P.tt('vector', o.yn[:], o.yn[:], lnwb[:, 0, :].unsqueeze(1).to_broadcast([128, NC_, 64]), ALU.mult, r=[K('yn'), 'lnwb'], w=[K('yn')])
            P.tt(PE_ALT, o.yn[:], o.yn[:], lnwb[:, 1, :].unsqueeze(1).to_broadcast([128, NC_, 64]), ALU.add, r=[K('yn'), 'lnwb'], w=[K('yn')])
            P.tt('vector', o.tq[:], o.Vst[:], bc8(o.rk[:, :, 0]), ALU.mult, r=[K('Vst'), K('rk')], w=[K('tq')])
            P.tt(PE_ALT, o.yn[:], o.yn[:], o.tq[:], ALU.add, r=[K('yn'), K('tq')], w=[K('yn')])
            P.tt('vector', o.yo[:], o.yn[:], o.gtok[:], ALU.mult, r=[K('yn'), K('gtok')], w=[K('yo')])
            for h in range(2):
                P.dma('sync', YRv[i // 8][h][:, NC_ * (i % 8):NC_ * (i % 8) + NC_, :], o.yo[64 * h:64 * h + 64, :, :], r=[K('yo')], w=['YRd'],
                      sem=f'yo{s}{h}')

        upT = pball[7][:].bitcast(BF16).rearrange("p (c t) -> p c t", c=8)

        def ugen():
            for s_ in range(64 + 16):
                b = s_ % 2
                g = s_ // 4
                gb = g % 3
                sub = s_ % 4
                if sub == 0:
                    yield ('UGROUP', g)
                srcx = xb[s_ * 128:(s_ + 1) * 128, :] if s_ < 64 else xo[(s_ - 64) * 128:(s_ - 63) * 128, :]
                P.dma('sync', u_xt[b][:], srcx, w=[f'uxt{b}'], sem=f'ux{b}')
                P.act(u_junk[:], u_xt[b][:], AF.Square, r=[f'uxt{b}'], w=['ujunk', f'uss{b}'], accum_out=u_ss[b][:])
                P.ts('vector', u_ss[b][:], u_ss[b][:], 1.0 / D, 1e-6, ALU.mult, ALU.add, r=[f'uss{b}'], w=[f'uss{b}'])
                P.act(u_ss[b][:], u_ss[b][:], AF.Ln, r=[f'uss{b}'], w=[f'uss{b}'])
                P.act(u_ss[b][:], u_ss[b][:], AF.Exp, r=[f'uss{b}'], w=[f'uss{b}'], scale=-0.5)
                P.ts(U_XN_ENG, u_xn[b][:], u_xt[b][:], u_ss[b][:, 0:1], None, ALU.mult, None, r=[f'uxt{b}', f'uss{b}'], w=[f'uxn{b}'])
                for c in range(8):
                    P.tr(upT[:, c, :], u_xn[b][:, c * 128:(c + 1) * 128], identb[:], r=[f'uxn{b}', 'identb'], w=['pb7'])
                P.cp('scalar', uTg[gb][:, :, sub * 128:(sub + 1) * 128], upT, r=['pb7'], w=[f'uTg{gb}'])
                if sub == 3:
                    dst = UT[:, :, g * 512:(g + 1) * 512] if s_ < 64 else UTo[:, :, (g - 16) * 512:(g - 15) * 512]
                    P.dma('sync', dst, uTg[gb][:], r=[f'uTg{gb}'], w=['UTd'], sem=f'us{gb}')
                    yield ('UDONE', g)
                else:
                    yield None

        def fgen():
            w_gate, w_up, g2T_d_, WGU = fpro
            P.dma('sync', f_g2T[:], g2T_d_, w=['fg2T'], sem='fc')
            WGUv = WGU.rearrange("fs p w c n -> p w c fs n")
            k = 0
            for wi, wv_ in enumerate((w_gate.rearrange("(c p) n -> p c n", p=128), w_up.rearrange("(c p) n -> p c n", p=128))):
                for c in range(8):
                    b = k % 2
                    P.dma('sync', f_stg[b][:], wv_[:, c, :], w=[f'fstg{b}'], sem=f'fw{b}')
                    P.ts(rr(['vector', 'scalar'], k), f_cbuf[b][:], f_stg[b][:], f_g2T[:, c:c + 1], None, ALU.mult, None,
                         r=[f'fstg{b}', 'fg2T'], w=[f'fcbuf{b}'])
                    P.dma('sync', WGUv[:, wi, c, :, :], f_cbuf[b][:].rearrange("p (fs n) -> p fs n", n=128), r=[f'fcbuf{b}'], w=['WGU'],
                          sem=f'fcs{b}')
                    k += 1
                    yield None
            w_out_, w_down_, WOb_, WDb_ = fpro2
            wov = w_out_.rearrange("(c p) n -> p c n", p=128)
            wdv = w_down_.rearrange("(c p) n -> p c n", p=128)
            for c in range(0, 8, 2):
                b = k % 2
                P.dma('sync', f_stg[b][:, 0:2 * D].rearrange("p (c n) -> p c n", c=2), wov[:, c:c + 2, :], w=[f'fstg{b}'], sem=f'fw{b}')
                P.cp(rr(['vector', 'scalar'], k), f_cbuf[b][:, 0:2 * D], f_stg[b][:, 0:2 * D], r=[f'fstg{b}'], w=[f'fcbuf{b}'])
                P.dma('sync', WOb_[:, c:c + 2, :], f_cbuf[b][:, 0:2 * D].rearrange("p (c n) -> p c n", c=2), r=[f'fcbuf{b}'], w=['WOb'],
                      sem=f'fcs{b}')
                k += 1
                yield None
            for c in range(0, NFS, 2):
                b = k % 2
                P.dma('sync', f_stg[b][:, 0:2 * D].rearrange("p (c n) -> p c n", c=2), wdv[:, c:c + 2, :], w=[f'fstg{b}'], sem=f'fw{b}')
                P.cp(rr(['vector', 'scalar'], k), f_cbuf[b][:, 0:2 * D], f_stg[b][:, 0:2 * D], r=[f'fstg{b}'], w=[f'fcbuf{b}'])
                P.dma('sync', WDb_[:, c:c + 2, :], f_cbuf[b][:, 0:2 * D].rearrange("p (c n) -> p c n", c=2), r=[f'fcbuf{b}'], w=['WDb'],
                      sem=f'fcs{b}')
                k += 1
                yield None

        active = []
        nxt = 0
        chain_turn = 0
        waiting = {}
        half_done = -1
        inproj_done = -1
        ug = ugen()
        fg = fgen() if fpro is not None else None
        u_groups_done = 0
        u_blocked = None
        u_alive = True
        rounds = 0
        while nxt < ntiles or active or u_alive or fg is not None:
            rounds += 1
            for _ in range(U_STEPS):
                if not u_alive:
                    break
                if u_blocked is not None:
                    if u_blocked >= 3 and u_blocked - 3 < 16 and inproj_done < min(ntiles - 1, 2 * (u_blocked - 3) + 1):
                        break
                    u_blocked = None
                try:
                    tok = next(ug)
                except StopIteration:
                    u_alive = False
                    break
                if tok is not None and tok[0] == 'UGROUP':
                    u_blocked = tok[1]
                elif tok is not None and tok[0] == 'UDONE':
                    u_groups_done = tok[1] + 1
            if fg is not None and rounds % F_EVERY == 0:
                try:
                    next(fg)
                except StopIteration:
                    fg = None
            while len(active) < 2 and nxt < ntiles and half_done >= nxt - 1 and u_groups_done > nxt // 2:
                active.append((nxt, tile(nxt)))
                nxt += 1
            for (ti, g) in list(active):
                for _ in range(T_STEPS):
                    if waiting.get(ti) == 'CHAIN' and chain_turn != ti:
                        break
                    waiting.pop(ti, None)
                    try:
                        tok = next(g)
                    except StopIteration:
                        active.remove((ti, g))
                        break
                    if tok == 'HALF':
                        half_done = ti
                    elif tok == 'INPROJ_DONE':
                        inproj_done = ti
                    elif tok == 'CHAIN':
                        waiting[ti] = 'CHAIN'
                    elif tok == 'CHAIN_DONE':
                        chain_turn = ti + 1
            assert rounds < 100000
        P.emit(st)


def nct_of(m):
    return min(4, (8 * (4 * m + 3) + 6) // 128 + 1)


def block_na(nc, pers, UT, w_kv, g1T_d, kaug_d, gather=None):
    P = Prog(nc)
    if gather is not None:
        YR_, YG_ = gather
        for k_ in range(4):
            P.cc(lambda e, k_=k_: e.collective_compute("AllGather", ALU.bypass, replica_groups=[[0, 1, 2, 3], [4, 5, 6, 7]],
                                                       ins=[YR_[k_].ap().opt()], outs=[YG_[k_].ap().opt()]), w=[f'YG{k_}'])
    KaS, KaW, Vs, Vw, kcT, vcT = pers['KaS'], pers['KaW'], pers['Vs'], pers['Vw'], pers['kcT'], pers['vcT']
    with ExitStack() as st:
        sb = lambda n, s, d=F32: st.enter_context(nc.sbuf_tensor("na_" + n, s, d))
        pb = [st.enter_context(nc.psum_tensor(f"na_pb{i}", [128, 512], F32)) for i in range(8)]
        g1T = sb("g1T", [128, 8])
        wkv = sb("wkv", [128, 8, 768], BF16)
        wst = sb("wst", [128, 8, 768])
        uT = [sb(f"uT{i}", [128, 8, 512], BF16) for i in range(2)]
        kst = [sb(f"kst{i}", [128, 512], BF16) for i in range(2)]
        P.dma('sync', g1T[:], g1T_d, w=['g1T'], sem='c')
        for kvh in range(2):
            P.dma('sync', KaS[kvh][64:68, :], kaug_d, w=[f'KaS{kvh}'], sem='c')
            P.dma('sync', KaW[kvh][64:68, :], kaug_d, w=[f'KaW{kvh}'], sem='c')
        P.finish_group('c')
        wv = w_kv.rearrange("(c p) n -> p c n", p=128)
        P.dma('sync', wst[:, 0:4], wv[:, 0:4], w=['wst0'], sem='w0')
        P.dma('gpsimd', wst[:, 4:8], wv[:, 4:8], w=['wst1'], sem='w1')
        for c in range(8):
            P.ts(rr(['vector', 'scalar'], c), wkv[:, c, :], wst[:, c, :], g1T[:, c:c + 1], None, ALU.mult, None,
                 r=[f'wst{c // 4}', 'g1T'], w=['wkv'])
        P.op('gpsimd', lambda e: e.memset(Vs[:, :, :, 64:66], 1.0), w=['Vs'])
        P.op('gpsimd', lambda e: e.memset(Vw[:, :, :, 64:66], 1.0), w=['Vw'])
        P.op('gpsimd', lambda e: e.memset(kcT[:, :, 512:514], 0.0), w=['kcT'])
        P.op('gpsimd', lambda e: e.memset(vcT[:, :, 512:514], 0.0), w=['vcT'])
        k = 0
        for i in range(NT):
            b = i % 2
            cs = slice(i * 512, (i + 1) * 512)
            P.dma('sync', uT[b][:], UT[:, :, cs], w=[f'uT{b}'], sem=f'u{b}')
            for (dst, dk, c0) in ((kcT, 'kcT', 0), (vcT, 'vcT', 128)) if '1' in NAV else ():
                pk = k % 4; k += 1
                for c in range(8):
                    P.mm(pb[pk][:], wkv[:, c, c0:c0 + 128], uT[b][:, c, :], c == 0, c == 7, r=['wkv', f'uT{b}'], w=[f'pb{pk}'])
                P.cp(rr(['scalar', 'vector'], k), dst[:, :, 32 * i:32 * i + 32], pb[pk][:].rearrange("p (n ph) -> p ph n", ph=16),
                     r=[f'pb{pk}'], w=[dk])
            for ti_, (dst, dk, c0) in enumerate(((KaS, 'KaS', 256), (KaW, 'KaW', 384))) if '2' in NAV else ():
                pk = k % 4; k += 1
                sk = kst[(2 * i + ti_) % 2]
                skk = f'kst{(2 * i + ti_) % 2}'
                for c in range(8):
                    P.mm(pb[pk][:], wkv[:, c, c0:c0 + 128], uT[b][:, c, :], c == 0, c == 7, r=['wkv', f'uT{b}'], w=[f'pb{pk}'])
                P.cp('scalar', dst[0][0:64, cs], pb[pk][0:64, :], r=[f'pb{pk}'], w=[f'{dk}0'])
                P.cp('vector', sk[64:128, :], pb[pk][64:128, :], r=[f'pb{pk}'], w=[skk])
                P.dma('gpsimd', dst[1][0:64, cs], sk[64:128, :], r=[skk], w=[f'{dk}1'], sem=f'km{(2 * i + ti_) % 2}')
            for sub in range(4) if '3' in NAV else ():
                ti = 4 * i + sub
                pk = 4 + (k % 4); k += 1
                for c in range(8):
                    P.mm(pb[pk][:, 0:256], uT[b][:, c, sub * 128:(sub + 1) * 128], wkv[:, c, 512:768], c == 0, c == 7,
                         r=['wkv', f'uT{b}'], w=[f'pb{pk}'])
                P.cp('scalar', Vs[:, ti, :, 0:64], pb[pk][:, 0:128].rearrange("p (h d) -> p h d", h=2), r=[f'pb{pk}'], w=['Vs'])
                P.cp('vector', Vw[:, ti, :, 0:64], pb[pk][:, 128:256].rearrange("p (h d) -> p h d", h=2), r=[f'pb{pk}'], w=['Vw'])
        P.emit(st)


def block_nb(nc, pers, w1k_d, w1v_d, w2k_d, w2v_d, pek_d, pev_d, kcaug_d, ovc_d):
    P = Prog(nc)
    kcT, vcT, KaC, CV = pers['kcT'], pers['vcT'], pers['KaC'], pers['CV']
    with ExitStack() as st:
        sb = lambda n, s, d=F32: st.enter_context(nc.sbuf_tensor("nb_" + n, s, d))
        pb = [st.enter_context(nc.psum_tensor(f"nb_pb{i}", [128, 512], F32)) for i in range(8)]
        w1b = [sb(f"w1b{i}", [128, 32, 256], BF16) for i in range(2)]
        stg = [sb(f"stg{i}", [128, 8, 256]) for i in range(2)]
        w2s = sb("w2s", [128, 2, 2, 64])
        w2b = sb("w2b", [128, 2, 2, 64], BF16)
        pes = sb("pes", [128, 2, 64])
        peb = sb("peb", [128, 2, 64], BF16)
        bias = sb("bias", [128, 2, 2, 2])
        hb = [sb(f"hb{i}", [128, 512]) for i in range(2)]
        t1 = [sb(f"t1{i}", [128, 512]) for i in range(2)]
        hT = [sb(f"hT{i}", [128, 512], BF16) for i in range(2)]
        P.dma('sync', w2s[:, 0, :, :], w2k_d.rearrange("(h p) d -> p h d", p=128), w=['w2s'], sem='c')
        P.dma('sync', w2s[:, 1, :, :], w2v_d.rearrange("(h p) d -> p h d", p=128), w=['w2s'], sem='c')
        P.dma('sync', pes[:, 0, :], pek_d, w=['pes'], sem='c')
        P.dma('sync', pes[:, 1, :], pev_d, w=['pes'], sem='c')
        for kvh in range(2):
            P.dma('sync', KaC[kvh][64:68, :], kcaug_d, w=[f'KaC{kvh}'], sem='c')
            P.dma('sync', CV[:, :, kvh, 0:128], ovc_d, w=['CV'], sem='c')
        P.finish_group('c')
        P.cp('vector', w2b[:], w2s[:], r=['w2s'], w=['w2b'])
        P.cp('vector', peb[:], pes[:], r=['pes'], w=['peb'])
        P.op('gpsimd', lambda e: e.memset(CV[:, :, :, 192:194], 1.0), w=['CV'])
        k = 0
        for z, w1d in enumerate((w1k_d, w1v_d)):
            w1v_ = w1d.rearrange("(l d) n -> d l n", d=64)
            for q in range(4):
                for dup in range(2):
                    b = k % 2; k += 1
                    P.dma('sync', stg[b][64 * dup:64 * dup + 64, :, :], w1v_[:, 8 * q:8 * q + 8, :], w=[f'stg{b}'], sem=f'w{b}')
                    P.cp(rr(['vector', 'scalar'], k), w1b[z][64 * dup:64 * dup + 64, 8 * q:8 * q + 8, :], stg[b][64 * dup:64 * dup + 64, :, :],
                         r=[f'stg{b}'], w=[f'w1b{z}'])
        kb = 0
        for z in range(2):
            zT = kcT if z == 0 else vcT
            zk = 'kcT' if z == 0 else 'vcT'
            for half in range(2):
                for l in range(32):
                    P.mm(pb[7][:, 2 * half:2 * half + 2], w1b[z][0:64, l, half * 128:(half + 1) * 128], peb[0:64, z, 2 * l:2 * l + 2],
                         l == 0, l == 31, r=[f'w1b{z}', 'peb'], w=['pb7'])
                P.cp('vector', bias[:, z, half, :], pb[7][:, 2 * half:2 * half + 2], r=['pb7'], w=['bias'])
            for kvh in range(2):
                ks = slice(64 * kvh, 64 * kvh + 64)
                for half in range(2):
                    pk = kb % 4; kb += 1
                    b = half
                    for l in range(32):
                        P.mm(pb[pk][:], w1b[z][ks, l, half * 128:(half + 1) * 128], zT[ks, l % 16, l // 16:l // 16 + 512], l == 0, l == 31,
                             r=[f'w1b{z}', zk], w=[f'pb{pk}'])
                    P.act(hb[b][:], pb[pk][:], AF.Identity, r=[f'pb{pk}', 'bias'], w=[f'hb{b}'], bias=bias[:, z, half, 0:1])
                    P.tt('vector', t1[b][:], hb[b][:], hb[b][:], ALU.mult, r=[f'hb{b}'], w=[f't1{b}'])
                    P.ts('vector', t1[b][:], t1[b][:], 0.044715, 1.0, ALU.mult, ALU.add, r=[f't1{b}'], w=[f't1{b}'])
                    P.tt('vector', t1[b][:], t1[b][:], hb[b][:], ALU.mult, r=[f't1{b}', f'hb{b}'], w=[f't1{b}'])
                    P.act(t1[b][:], t1[b][:], AF.Sigmoid, r=[f't1{b}'], w=[f't1{b}'], scale=1.5957691216057308)
                    P.tt('vector', hT[b][:], hb[b][:], t1[b][:], ALU.mult, r=[f'hb{b}', f't1{b}'], w=[f'hT{b}'])
                if z == 0:
                    for half in range(2):
                        P.mm(pb[4][0:64, :], w2b[:, 0, half, :], hT[half][:], half == 0, half == 1, r=['w2b', f'hT{half}'], w=['pb4'])
                    P.cp('scalar', KaC[kvh][0:64, :], pb[4][0:64, :], r=['pb4'], w=[f'KaC{kvh}'])
                else:
                    for ct in range(4):
                        for half in range(2):
                            P.mm(pb[5][:, ct * 64:(ct + 1) * 64], hT[half][:, ct * 128:(ct + 1) * 128], w2b[:, 1, half, :], half == 0, half == 1,
                                 r=['w2b', f'hT{half}'], w=['pb5'])
                    P.cp('scalar', CV[:, :, kvh, 128:192], pb[5][:, 0:256].rearrange("p (c d) -> p c d", c=4), r=['pb5'], w=['CV'])
        P.emit(st)


def block_nc(nc, pers, UTo, w_q, w_gl, g1T_d, identb_d, ident32_d, OHx_d, cmask_d, seldiag_d, winmask_d, fmk_d, fma_d, qaug_d, YN,
             nm=NM):
    P = Prog(nc)
    KaS, KaW, Vs, Vw, KaC, CV = pers['KaS'], pers['KaW'], pers['Vs'], pers['Vw'], pers['KaC'], pers['CV']
    with ExitStack() as st:
        sb = lambda n, s, d=F32: st.enter_context(nc.sbuf_tensor("nc_" + n, s, d))
        sc = [st.enter_context(nc.psum_tensor(f"nc_sc{i}", [128, 512], F32)) for i in range(3)]
        Os = st.enter_context(nc.psum_tensor("nc_Os", [128, 512], F32))
        Ow = st.enter_context(nc.psum_tensor("nc_Ow", [128, 512], F32))
        pc = [st.enter_context(nc.psum_tensor(f"nc_pc{i}", [128, 512], F32)) for i in range(2)]
        pm = st.enter_context(nc.psum_tensor("nc_pm", [128, 512], F32))
        g1T = sb("g1T", [128, 8])
        identb = sb("identb", [128, 128], BF16)
        ident32 = sb("ident32", [128, 128])
        cmask = sb("cmask", [128, NM, 4, 128], BF16)
        seldiag = sb("seldiag", [128, 4, 512], BF16)
        winmask = sb("winmask", [128, 8, 512], BF16)
        fmk = sb("fmk", [128, NM, 128], BF16)
        fma = sb("fma", [128, NM, 128], BF16)
        wq = sb("wq", [128, 8, 512], BF16)
        wgl = sb("wgl", [128, 8, 24], BF16)
        wqs = sb("wqs", [128, 8, 512])
        wgs = sb("wgs", [128, 8, 24])
        uTo = [sb(f"uTo{i}", [128, 8, 128], BF16) for i in range(2)]
        QAx = [[sb(f"QA{i}_{p}", [128, 512], BF16) for p in range(3)] for i in range(2)]
        gates = [sb(f"gates{i}", [128, 24]) for i in range(2)]
        PT = [sb(f"PT{i}", [128, 512], BF16) for i in range(4)]
        PcT = [[sb(f"PcT{b}_{i}", [128, 512], BF16) for i in range(4)] for b in range(2)]
        NMr = [sb(f"NMr{i}", [128, 512], BF16) for i in range(2)]
        imp = sb("imp", [128, 128]); imp2 = sb("imp2", [128, 128]); tmpi = sb("tmpi", [128, 128]); negm = sb("negm", [128, 128])
        m8 = sb("m8", [128, 2, 8])
        zc = [sb(f"zc{i}", [128, 3, 4]) for i in range(2)]
        rg = [sb(f"rg{i}", [128, 3, 4]) for i in range(2)]
        Osb = [sb(f"Osb{i}", [66, 512]) for i in range(2)]
        yacc = [sb(f"yacc{i}", [128, 4, 64]) for i in range(2)]
        ysb = [sb(f"ysb{i}", [128, 512], BF16) for i in range(2)]

        for (t, d_, k) in ((g1T, g1T_d, 'g1T'), (identb, identb_d, 'identb'), (ident32, ident32_d, 'ident32'),
                           (cmask, cmask_d, 'cmask'), (seldiag, seldiag_d, 'seldiag'), (winmask, winmask_d, 'winmask'),
                           (fmk, fmk_d, 'fmk'), (fma, fma_d, 'fma')):
            P.dma('sync', t[:], d_, w=[k], sem='c')
        for kvh in range(2):
            P.dma('sync', KaS[kvh][68:128, :], OHx_d, w=[f'KaS{kvh}'], sem='c')
        P.finish_group('c')
        for kvh in range(2):
            for p in range(3):
                P.op('gpsimd', lambda e, t=QAx[kvh][p]: e.memset(t[:], 0.0), w=[f'QA{kvh}_{p}'])
        wqv = w_q.rearrange("(c p) n -> p c n", p=128)
        wgv = w_gl.rearrange("(c p) n -> p c n", p=128)
        P.dma('gpsimd', wqs[:], wqv, w=['wqs'], sem='w0')
        P.dma('gpsimd', wgs[:], wgv, w=['wgs'], sem='w1')
        for c in range(8):
            P.ts(rr(['vector', 'scalar'], c), wq[:, c, :], wqs[:, c, :], g1T[:, c:c + 1], None, ALU.mult, None,
                 r=['wqs', 'g1T'], w=['wq'])
            P.ts(rr(['scalar', 'vector'], c), wgl[:, c, :], wgs[:, c, :], g1T[:, c:c + 1], None, ALU.mult, None,
                 r=['wgs', 'g1T'], w=['wgl'])
        for bi in range(2):
            P.op('gpsimd', lambda e, bi=bi: e.memset(Osb[bi][:], 0.0), w=[f'Osb{bi}'])

        units = [(m, kvh) for m in range(nm) for kvh in range(2)]
        NU = len(units)
        gview = lambda m, kvh: gates[m % 2][:, 12 * kvh:12 * kvh + 12].rearrange("p (g b) -> p g b", b=3)

        def stA(u, part=None):
            m, kvh = units[u]
            ub = m % 2
            nph = (8 * m + 6) // 60 + 1
            if kvh == 0 and part in (None, 0):
                P.dma('sync', uTo[ub][:], UTo[:, :, m * 128:(m + 1) * 128], w=[f'uTo{ub}'], sem=f'u{ub}')
                for c in range(8):
                    P.mm(pc[1][:, 0:24], uTo[ub][:, c, :], wgl[:, c, :], c == 0, c == 7, r=[f'uTo{ub}', 'wgl'], w=['pc1'])
                P.act(gates[ub][:], pc[1][:, 0:24], AF.Exp, r=['pc1'], w=[f'gates{ub}'], scale=-1.0)
                P.ts('vector', gates[ub][:], gates[ub][:], 1.0, None, ALU.add, None, r=[f'gates{ub}'], w=[f'gates{ub}'])
                P.op('vector', lambda e, ub=ub: e.reciprocal(out=gates[ub][:], in_=gates[ub][:]), r=[f'gates{ub}'], w=[f'gates{ub}'])
            for p in range(nph if part in (None, 0) else 0):
                P.dma('sync', QAx[kvh][p][64:68, :], qaug_d[m, kvh], w=[f'QA{kvh}_{p}'], sem=f'qa{kvh}_{p}')
            for g in (range(4) if part is None else ((0, 1) if part == 0 else (2, 3))):
                hq = 4 * kvh + g
                for c in range(8):
                    P.mm(pc[0][0:64, g * 128:(g + 1) * 128], wq[:, c, hq * 64:(hq + 1) * 64], uTo[ub][:, c, :], c == 0, c == 7,
                         r=['wq', f'uTo{ub}'], w=['pc0'])
            for p in range(nph if part in (None, 1) else 0):
                P.ts('vector', QAx[kvh][p][0:64, :], pc[0][0:64, :], 0.125, None, ALU.mult, None, r=['pc0'], w=[f'QA{kvh}_{p}'])

        def stB(u):
            m, kvh = units[u]
            qa, qk = QAx[kvh][0], f'QA{kvh}_0'
            for ct in range(nct_of(m)):
                P.mm(pm[:], KaC[kvh][0:68, ct * 128:(ct + 1) * 128], qa[0:68, :], True, False, r=[f'KaC{kvh}', qk], w=['pm'])
                P.mm(pm[:].rearrange("p (g t) -> p g t", g=4), identb[:],
                     cmask[:, m, ct, :].unsqueeze(1).to_broadcast([128, 4, 128]), False, True, r=['identb', 'cmask'], w=['pm'])
                P.act(PcT[u % 2][ct][:], pm[:], AF.Exp, r=['pm'], w=[f'PcT{u % 2}_{ct}'])
        stB.sck = 0

        def stC(u, gs=(0, 1, 2, 3)):
            m, kvh = units[u]
            nct = nct_of(m)
            for g in gs:
                pcg = pc[g // 2][:, (g % 2) * 193:(g % 2) * 193 + 193]
                for ct in range(nct):
                    P.mm(pcg, PcT[u % 2][ct][:, g * 128:(g + 1) * 128], CV[:, ct, kvh, 0:193], ct == 0, ct == nct - 1,
                         r=[f'PcT{u % 2}_{ct}', 'CV'], w=[f'pc{g // 2}'])

        def stD(u):
            m, kvh = units[u]
            b = u % 2
            gv = gview(m, kvh)
            gk = f'gates{m % 2}'
            for g in range(4):
                zz = pc[g // 2][:, (g % 2) * 193 + 192:(g % 2) * 193 + 193]
                P.ts('vector', zc[b][:, 0, g:g + 1], zz, 1e-30, None, ALU.max, None, r=[f'pc{g // 2}'], w=[f'zc{b}0'])
            P.op('vector', lambda e: e.reciprocal(out=zc[b][:, 0, :], in_=zc[b][:, 0, :]), r=[f'zc{b}0'], w=[f'zc{b}0'])
            P.tt('vector', rg[b][:, 0, :], zc[b][:, 0, :], gv[:, :, 0], ALU.mult, r=[f'zc{b}0', gk], w=[f'rg{b}0'])
            for g in range(4):
                pcg = pc[g // 2][:, (g % 2) * 193:(g % 2) * 193 + 193]
                if g == 0:
                    P.ts('vector', imp[:], pcg[:, 0:128], zc[b][:, 0, 0:1], None, ALU.mult, None, r=['pc0', f'zc{b}0'], w=['imp'])
                else:
                    P.stt(imp[:], pcg[:, 0:128], zc[b][:, 0, g:g + 1], imp[:], ALU.mult, ALU.add, r=[f'pc{g // 2}', f'zc{b}0', 'imp'], w=['imp'])
                P.ts('vector', yacc[b][:, g, :], pcg[:, 128:192], rg[b][:, 0, g:g + 1], None, ALU.mult, None,
                     r=[f'pc{g // 2}', f'rg{b}0'], w=[f'yacc{b}'])
            P.tt('vector', imp2[:], imp[:], fmk[:, m, :], ALU.mult, r=['imp', 'fmk'], w=['imp2'])
            P.tt('vector', imp2[:], imp2[:], fma[:, m, :], ALU.add, r=['imp2', 'fma'], w=['imp2'])
            P.op('vector', lambda e: e.max(out=m8[:, 0, :], in_=imp2[:]), r=['imp2'], w=['m8'])
            P.op('vector', lambda e: e.match_replace(out=tmpi[:], in_to_replace=m8[:, 0, :], in_values=imp2[:], imm_value=-3.0e38),
                 r=['imp2', 'm8'], w=['tmpi'])
            P.op('vector', lambda e: e.max(out=m8[:, 1, :], in_=tmpi[:]), r=['tmpi'], w=['m8'])
            P.ts('vector', tmpi[:], imp2[:], m8[:, 1, 7:8], None, ALU.is_ge, None, r=['imp2', 'm8'], w=['tmpi'])
            P.stt(tmpi[:], imp2[:], -5.0e29, tmpi[:], ALU.is_gt, ALU.mult, r=['imp2', 'tmpi'], w=['tmpi'])
            P.ts('vector', negm[:], tmpi[:], -NEGM, NEGM, ALU.mult, ALU.add, r=['tmpi'], w=['negm'])

        def stE(u):
            b = u % 2
            P.tr(pm[:, 0:128], negm[:], ident32[:], r=['negm', 'ident32'], w=['pm'])
            P.cp('vector', NMr[b][:].rearrange("p (g t) -> p g t", g=4), pm[:, 0:128].unsqueeze(1).to_broadcast([128, 4, 128]),
                 r=['pm'], w=[f'NMr{b}'])
            m, kvh = units[u]
            for p in range((8 * m + 6) // 60 + 1):
                rows = min(60, 128 - 60 * p)
                P.dma('sync', QAx[kvh][p][68:68 + rows, :], NMr[b][60 * p:60 * p + rows, :], r=[f'NMr{b}'], w=[f'QA{kvh}_{p}'],
                      sem=f'nm{kvh}_{p}')

        def tiles_of(u):
            m, kvh = units[u]
            tl = [('s', i, None) for i in range(4 * m + 4)]
            tl += [('w', 4 * m - 4 + r_, r_) for r_ in range(8) if 4 * m - 4 + r_ >= 0]
            return tl

        st_ = {'sck': 0}

        def score(u, tile):
            m, kvh = units[u]
            qa, qk = QAx[kvh][0], f'QA{kvh}_0'
            kind, i, r_ = tile
            k = stB.sck % 3; stB.sck += 1
            ks_ = slice(i * 128, (i + 1) * 128)
            if kind == 'c':
                m1, kvh1 = units[u + 1]
                P.mm(sc[k][:], KaC[kvh1][0:68, ks_], QAx[kvh1][0][0:68, :], True, False, r=[f'KaC{kvh1}', f'QA{kvh1}_0'], w=[f'sc{k}'])
                P.mm(sc[k][:].rearrange("p (g t) -> p g t", g=4), identb[:],
                     cmask[:, m1, i, :].unsqueeze(1).to_broadcast([128, 4, 128]), False, True, r=['identb', 'cmask'], w=[f'sc{k}'])
                return k
            if kind == 's':
                diag = i >= 4 * m
                p = (2 * i) // 60
                P.mm(sc[k][:], KaS[kvh][0:128, ks_], QAx[kvh][p][0:128, :], True, not diag, r=[f'KaS{kvh}', f'QA{kvh}_{p}'], w=[f'sc{k}'])
                if diag:
                    P.mm(sc[k][:], identb[:], seldiag[:, i - 4 * m, :], False, True, r=['identb', 'seldiag'], w=[f'sc{k}'])
            else:
                P.mm(sc[k][:], KaW[kvh][0:68, ks_], qa[0:68, :], True, False, r=[f'KaW{kvh}', qk], w=[f'sc{k}'])
                P.mm(sc[k][:], identb[:], winmask[:, r_, :], False, True, r=['identb', 'winmask'], w=[f'sc{k}'])
            return k

        def exp_pv(u, tile, k, pt, first, last):
            m, kvh = units[u]
            kind, i, r_ = tile
            if kind == 'c':
                P.act(PcT[(u + 1) % 2][i][:], sc[k][:], AF.Exp, r=[f'sc{k}'], w=[f'PcT{(u + 1) % 2}_{i}'])
                return
            P.act(PT[pt][:], sc[k][:], AF.Exp, r=[f'sc{k}'], w=[f'PT{pt}'])
            if kind == 's':
                P.mm(Os[0:65, :], Vs[:, i, kvh, 0:65], PT[pt][:], first, last, r=['Vs', f'PT{pt}'], w=['Os'])
            else:
                P.mm(Ow[0:65, :], Vw[:, i, kvh, 0:65], PT[pt][:], first, last, r=['Vw', f'PT{pt}'], w=['Ow'])

        def copyO(u, which):
            if which == 's':
                P.cp('vector', Osb[0][0:65, :], Os[0:65, :], r=['Os'], w=['Osb0'])
            else:
                P.cp('vector', Osb[1][0:65, :], Ow[0:65, :], r=['Ow'], w=['Osb1'])

        def fin(u, parts=(0, 1)):
            m, kvh = units[u]
            b = u % 2
            ub = m % 2
            gv = gview(m, kvh)
            gk = f'gates{m % 2}'
            for bi in parts:
                br = bi + 1
                for g in range(4):
                    P.tr(pm[:, g * 66:(g + 1) * 66], Osb[bi][0:66, g * 128:(g + 1) * 128], ident32[0:66, 0:66],
                         r=[f'Osb{bi}', 'ident32'], w=['pm'])
                pv = pm[:, 0:264].rearrange("p (g c) -> p g c", c=66)
                P.ts('vector', zc[b][:, br, :], pv[:, :, 64], 1e-30, None, ALU.max, None, r=['pm'], w=[f'zc{b}{br}'])
                P.op('vector', lambda e, br=br: e.reciprocal(out=zc[b][:, br, :], in_=zc[b][:, br, :]), r=[f'zc{b}{br}'], w=[f'zc{b}{br}'])
                P.tt('vector', rg[b][:, br, :], zc[b][:, br, :], gv[:, :, br], ALU.mult, r=[f'zc{b}{br}', gk], w=[f'rg{b}{br}'])
                for g in range(4):
                    P.stt(yacc[b][:, g, :], pv[:, g, 0:64], rg[b][:, br, g:g + 1], yacc[b][:, g, :], ALU.mult, ALU.add,
                          r=['pm', f'rg{b}{br}', f'yacc{b}'], w=[f'yacc{b}'])
            if 1 not in parts:
                return
            P.cp('vector', ysb[ub][:, kvh * 256:(kvh + 1) * 256], yacc[b][:].rearrange("p g d -> p (g d)"), r=[f'yacc{b}'], w=[f'ysb{ub}'])
            if kvh == 1:
                P.dma('sync', YN[m * 128:(m + 1) * 128, :], ysb[ub][:], r=[f'ysb{ub}'], w=['YNd'], sem=f'y{ub}')

        stA(0); stB(0); stC(0); stD(0); stE(0)
        ptk = 0
        pre = []
        for u in range(NU):
            tl = tiles_of(u)
            ctl = [('c', ct, None) for ct in range(nct_of(units[u + 1][0]))] if u + 1 < NU else []
            ins = min(5, len(tl))
            tl = tl[:ins] + ctl + tl[ins:]
            n = len(tl)
            idx_s = [i for i, t_ in enumerate(tl) if t_[0] == 's']
            idx_w = [i for i, t_ in enumerate(tl) if t_[0] == 'w']
            c_end = ins + len(ctl) - 1
            kq = list(pre)
            pre = []
            for i in range(len(kq), min(2, n)):
                kq.append(score(u, tl[i]))
            tl_next = tiles_of(u + 1) if u + 1 < NU else []
            for i in range(n):
                if i + 2 < n:
                    kq.append(score(u, tl[i + 2]))
                elif u + 1 < NU and len(pre) < min(2, len(tl_next)):
                    pre.append(score(u + 1, tl_next[len(pre)]))
                kind = tl[i][0]
                first = (i == idx_s[0]) if kind == 's' else (kind == 'w' and i == idx_w[0])
                last = (i == idx_s[-1]) if kind == 's' else (kind == 'w' and i == idx_w[-1])
                exp_pv(u, tl[i], kq[i], ptk % 4, first, last)
                if kind != 'c':
                    ptk += 1
                if i == idx_s[-1]:
                    copyO(u, 's')
                if i == 0 and u > 0:
                    fin(u - 1, (0,))
                if i == 2 and u > 0:
                    fin(u - 1, (1,))
                if u + 1 < NU:
                    if i == 1:
                        stA(u + 1, 0)
                    if i == 2:
                        stA(u + 1, 1)
                    if i == c_end:
                        stC(u + 1, (0, 1))
                    if i == c_end + 1:
                        stC(u + 1, (2, 3))
                        stD(u + 1)
                    if i == max(c_end + 2, n - 8):
                        stE(u + 1)
            copyO(u, 'w')
        fin(NU - 1)
        P.emit(st)


def block_g(nc, YR, YG):
    P = Prog(nc)
    with ExitStack() as st:
        for k in range(4):
            P.cc(lambda e, k=k: e.collective_compute("AllGather", ALU.bypass, replica_groups=[[0, 1, 2, 3], [4, 5, 6, 7]],
                                                     ins=[YR[k].ap().opt()], outs=[YG[k].ap().opt()]), w=[f'YG{k}'])
        P.emit(st)


def _f32(a):
    return np.ascontiguousarray(a, dtype=np.float32)


def own_rows(j):
    return np.concatenate([np.arange(128 * (4 * m + j), 128 * (4 * m + j) + 128) for m in range(NM)])


def _r_consts():
    s = np.arange(64)
    mTs = (s[None, :] > s[:, None]).astype(np.float32)
    mTi = (s[None, :] >= s[:, None]).astype(np.float32)
    ms = (s[None, :] < s[:, None]).astype(np.float32)
    rmask = np.stack([np.tile(mTs, (2, 1)), np.tile(mTi, (2, 1)), np.tile(ms, (2, 1))], axis=1)
    bones = np.kron(np.eye(2, dtype=np.float32), np.ones((64, 64), np.float32))
    resetm = np.ones((128, 512), np.float32)
    resetm[:, ::64] = 0.0
    return {'rmask': _f32(rmask), 'bones': bones, 'ident32': np.eye(128, dtype=np.float32), 'resetm': resetm}


R_CONSTS = _r_consts()


def _bf(a):
    import ml_dtypes
    return np.ascontiguousarray(np.asarray(a, dtype=np.float32).astype(ml_dtypes.bfloat16))


def _nsa_consts_common():
    jpos = np.arange(T)
    kaug = np.stack([jpos // 64, jpos % 64, np.ones(T), np.ones(T)])
    ce = 16 * np.arange(512) + 31
    kcaug = np.stack([ce // 64, ce % 64, np.ones(512), np.ones(512)])
    OH = ((jpos[None, :] // 64) % 60 == np.arange(60)[:, None])
    cst = 16 * np.arange(512)
    sst = 64 * np.arange(128)
    ov = np.clip(np.minimum(cst[:, None] + 32, sst[None, :] + 64) - np.maximum(cst[:, None], sst[None, :]), 0, None) / 16.0
    ov[511] = 0.0
    ovc = ov.reshape(4, 128, 128).transpose(1, 0, 2)
    return {'kaug': _bf(kaug), 'kcaug': _bf(kcaug), 'OHx': _bf(OH), 'ovc': _bf(ovc)}


def _nsa_consts_core(j):
    tt = np.arange(128)
    jj = np.arange(128)
    qaug = np.zeros((NM, 2, 4, 4, 128), np.float32)
    cmask = np.zeros((128, NM, 4, 128), np.float32)
    fmk = np.zeros((128, NM, 128), np.float32)
    fma = np.zeros((128, NM, 128), np.float32)
    n = np.arange(128)
    for m in range(NM):
        t = 128 * (4 * m + j) + tt
        for kvh in range(2):
            for g in range(4):
                s = 2.0 ** -(4 * kvh + g + 1)
                qaug[m, kvh, 0, g] = 64 * s
                qaug[m, kvh, 1, g] = s
                qaug[m, kvh, 2, g] = -64 * s * (t // 64)
                qaug[m, kvh, 3, g] = -s * (t % 64)
        for ct in range(4):
            c = 128 * ct + jj
            vis = (c[:, None] <= 510) & (16 * c[:, None] + 31 <= t[None, :])
            cmask[:, m, ct, :] = np.where(vis, 0.0, NEGM)
        cur = t // 64
        valid = n[None, :] <= cur[:, None]
        forced = (n[None, :] == 0) | (n[None, :] == cur[:, None]) | (n[None, :] == cur[:, None] - 1)
        fmk[:, m, :] = (valid & ~forced)
        fma[:, m, :] = np.where(valid, np.where(forced, 1e30, 0.0), -1e30)
    seldiag = np.zeros((128, 4, 4, 128), np.float32)
    for r in range(4):
        if r == j:
            seldiag[:, r, :, :] = np.where(jj[:, None] <= tt[None, :], 0.0, NEGM)[:, None, :]
        elif r > j:
            seldiag[:, r] = NEGM
    winmask = np.zeros((128, 8, 4, 128), np.float32)
    for r in range(8):
        dist = 128 * (j + 4 - r) + tt[None, :] - jj[:, None]
        winmask[:, r, :, :] = np.where((dist >= 0) & (dist < 512), 0.0, NEGM)[:, None, :]
    return {'qaug': _bf(qaug.reshape(NM, 2, 4, 512)), 'cmask': _bf(cmask), 'fmk': _bf(fmk), 'fma': _bf(fma),
            'seldiag': _bf(seldiag.reshape(128, 4, 512)), 'winmask': _bf(winmask.reshape(128, 8, 512))}


N_COMMON = _nsa_consts_common()
N_CORE = [_nsa_consts_core(j) for j in range(4)]


def build_inputs(inp, stages):
    x = inp['x']
    maps = []
    ident = np.eye(128, dtype=np.float32)
    import ml_dtypes
    identb = ident.astype(ml_dtypes.bfloat16)
    for c in range(8):
        b, j = c // 4, c % 4
        d = {}
        d['xb'] = _f32(x[b])
        d['xo'] = _f32(x[b][own_rows(j)])
        d['identb'] = identb
        if 'R' in stages:
            hA = 2 * j
            wi = inp['w_in'][0]
            hc = slice(128 * j, 128 * j + 128)
            d['w_rw'] = _f32(np.concatenate([wi[:, 0:512][:, hc], wi[:, 512:1024][:, hc], wi[:, 1024:1536][:, hc],
                                             wi[:, 1536:1792]], axis=1))
            d['g1T'] = _f32(inp['norm1_g'][0].reshape(8, 128).T)
            mu = inp['mu_shift'][0]
            rwp = np.zeros((128, 16), np.float32)
            rwp[:, 0] = mu[0:512][hc]; rwp[:, 1] = mu[512:1024][hc]; rwp[:, 2] = mu[1024:1536][hc]
            rwp[:, 3] = mu[1536:1664]; rwp[:, 4] = mu[1664:1792]
            rwp[:, 5] = inp['rwkv_w0'][0][hc]; rwp[:, 6] = inp['rwkv_a0'][0][hc]
            rwp[:, 7] = inp['rwkv_k_k'][0][hc]; rwp[:, 8] = inp['rwkv_k_a'][0][hc]
            rwp[:, 9] = inp['rwkv_r_k'][0].reshape(512)[hc]
            d['rwp'] = rwp
            d['w2a2'] = _f32(np.concatenate([inp['rwkv_w2'][0][:, hc], inp['rwkv_a2'][0][:, hc]], axis=0))
            d['g2'] = _f32(inp['rwkv_g2'][0][:, hc])
            lw = inp['rwkv_lnx_w'][0][hc].reshape(2, 1, 64)
            lb = inp['rwkv_lnx_b'][0][hc].reshape(2, 1, 64)
            d['lnwb'] = _f32(np.stack([np.broadcast_to(lw, (2, 64, 64)).reshape(128, 64),
                                       np.broadcast_to(lb, (2, 64, 64)).reshape(128, 64)], axis=1))
            d.update(R_CONSTS)
        if 'N' in stages:
            wi = inp['w_in'][0]
            o = 1792
            cols = lambda a, b_: wi[:, o + a:o + b_]
            d['w_kv'] = _f32(np.concatenate([cols(512, 640), cols(640, 768), cols(768, 896), cols(1024, 1152),
                                             cols(896, 1024), cols(1152, 1280)], axis=1))
            d['w_q'] = _f32(cols(0, 512))
            d['w_gl'] = _f32(cols(1280, 1304))
            d['g1T'] = _f32(inp['norm1_g'][0].reshape(8, 128).T)
            d['w1k'] = _f32(inp['nsa_cmp_k_w1'][0]); d['w1v'] = _f32(inp['nsa_cmp_v_w1'][0])
            d['w2k'] = _f32(inp['nsa_cmp_k_w2'][0]); d['w2v'] = _f32(inp['nsa_cmp_v_w2'][0])
            pe2 = lambda pe: _f32(np.tile(np.repeat(pe.T, 2, axis=1), (2, 1)))
            d['pek'] = pe2(inp['nsa_pe_k'][0]); d['pev'] = pe2(inp['nsa_pe_v'][0])
            d['ident32'] = np.eye(128, dtype=np.float32)
            d.update(N_COMMON)
            d.update(N_CORE[j])
        if 'F' in stages:
            sel = np.zeros((128, 4), np.float32)
            sel[:, j] = 1.0
            d['selt'] = sel
            d['w_out'] = _f32(inp['w_out'][0])
            d['g2T'] = _f32(inp['norm2_g'][0].reshape(8, 128).T)
            d['w_gate'] = _f32(inp['ffn_w_gate'][0])
            d['w_up'] = _f32(inp['ffn_w_up'][0])
            d['w_down'] = _f32(inp['ffn_w_down'][0])
            d['gfb'] = _f32(np.broadcast_to(inp['norm_f_g'][None, :], (128, D)))
        maps.append(d)
    return maps


def build_program(stages, dbg=()):
    nc = bass.Bass("TRN2", target_bir_lowering=False)
    declared = []

    def ein(n, s, d=F32):
        declared.append(n)
        return nc.dram_tensor(n, s, d, kind="ExternalInput").ap()
    nc._declared_inputs = declared
    xb = ein("xb", [T, D])
    xo = ein("xo", [NM * 128, D])
    identb = ein("identb", [128, 128], BF16)
    out = nc.dram_tensor("out", [NM * 128, D], F32, kind="ExternalOutput").ap()
    kind = lambda n: "ExternalOutput" if n in dbg else "Internal"
    UT = nc.dram_tensor("UT", [128, 8, T], BF16, kind=kind("UT")).ap()
    UTo = nc.dram_tensor("UTo", [128, 8, NM * 128], BF16, kind=kind("UTo")).ap()
    YN = nc.dram_tensor("YN", [NM * 128, 512], BF16, kind=kind("YN")).ap()
    YR = [nc.dram_tensor(f"YR{k}", [2048, 128], BF16, **({'kind': 'ExternalOutput'} if 'YR' in dbg else {})) for k in range(4)]
    YG = [nc.dram_tensor(f"YG{k}", [4 * 2048, 128], BF16) for k in range(4)]
    merged = R2 and 'R' in stages and 'U' in stages
    WGU = nc.dram_tensor("WGU", [NFS, 128, 2, 8, 128], BF16, kind="Internal").ap()
    WOb = nc.dram_tensor("WOb", [128, 8, D], BF16, kind="Internal").ap()
    WDb = nc.dram_tensor("WDb", [128, NFS, D], BF16, kind="Internal").ap()
    f_in = None
    if 'F' in stages:
        f_in = dict(selt=ein("selt", [128, 4]), w_out=ein("w_out", [D, D]), g2T=ein("g2T", [128, 8]), w_gate=ein("w_gate", [D, DFF]),
                    w_up=ein("w_up", [D, DFF]), w_down=ein("w_down", [DFF, D]), gfb=ein("gfb", [128, D]))
    if 'U' in stages and not merged:
        block_u(nc, xb, xo, UT, UTo, identb)
    g1T_d = i32_d = None
    if 'R' in stages:
        g1T_d = ein("g1T", [128, 8])
        i32_d = ein("ident32", [128, 128])
        (block_r2 if R2 else block_r)(nc, UT, ein("w_rw", [D, 640]), g1T_d, ein("rwp", [128, 16]), ein("w2a2", [128, 128]),
                ein("g2", [128, 128]), ein("lnwb", [128, 2, 64]), ein("rmask", [128, 3, 64]), ein("bones", [128, 128]),
                i32_d, ein("resetm", [128, 512]), [y.ap() for y in YR], *((identb,) if R2 else ()), ntiles=(2 * RT if R2 else RT),
                **(dict(xb=xb, xo=xo, UTo=UTo, fpro=((f_in['w_gate'], f_in['w_up'], f_in['g2T'], WGU) if f_in else None),
                        fpro2=((f_in['w_out'], f_in['w_down'], WOb, WDb) if f_in else None)) if merged else {}))
    if 'N' in stages:
        with ExitStack() as ps_:
            sbp = lambda n, s, d: ps_.enter_context(nc.sbuf_tensor("p_" + n, s, d))
            pers = {'KaS': [sbp(f"KaS{i}", [128, T], BF16) for i in range(2)], 'KaW': [sbp(f"KaW{i}", [68, T], BF16) for i in range(2)],
                    'Vs': sbp("Vs", [128, 64, 2, 66], BF16), 'Vw': sbp("Vw", [128, 64, 2, 66], BF16),
                    'KaC': [sbp(f"KaC{i}", [68, 512], BF16) for i in range(2)], 'CV': sbp("CV", [128, 4, 2, 194], BF16)}
            g1T_d = ein("g1T", [128, 8]) if 'R' not in stages else g1T_d
            i32_d = ein("ident32", [128, 128]) if 'R' not in stages else i32_d
            with ExitStack() as ps2:
                pers['kcT'] = ps2.enter_context(nc.sbuf_tensor("p_kcT", [128, 16, 514], BF16))
                pers['vcT'] = ps2.enter_context(nc.sbuf_tensor("p_vcT", [128, 16, 514], BF16))
                if 'a' in NSUB:
                    block_na(nc, pers, UT, ein("w_kv", [D, 768]), g1T_d, ein("kaug", [4, T], BF16),
                             gather=((YR, YG) if ('G' in stages and 'R' in stages) else None))
                if 'b' in NSUB:
                  block_nb(nc, pers, ein("w1k", [2048, 256]), ein("w1v", [2048, 256]), ein("w2k", [256, 64]), ein("w2v", [256, 64]),
                         ein("pek", [128, 64]), ein("pev", [128, 64]), ein("kcaug", [4, 512], BF16), ein("ovc", [128, 4, 128], BF16))
            if 'c' in NSUB:
              block_nc(nc, pers, UTo, ein("w_q", [D, 512]), ein("w_gl", [D, 24]), g1T_d, identb, i32_d, ein("OHx", [60, T], BF16),
                     ein("cmask", [128, NM, 4, 128], BF16), ein("seldiag", [128, 4, 512], BF16), ein("winmask", [128, 8, 512], BF16),
                     ein("fmk", [128, NM, 128], BF16), ein("fma", [128, NM, 128], BF16), ein("qaug", [NM, 2, 4, 512], BF16), YN, nm=NQ)
    if 'G' in stages and not ('N' in stages and 'a' in NSUB and 'R' in stages):
        block_g(nc, YR, YG)
    if 'F' in stages:
        block_f(nc, xo, YN, [y.ap() for y in YG], f_in['selt'], f_in['w_out'], f_in['g2T'], f_in['w_gate'], f_in['w_up'], f_in['w_down'],
                f_in['gfb'], identb, out, WGU, use_y=('Y' in stages), wgu_done=merged, WOb=WOb, WDb=WDb)
    return nc


def run(inp, stages, dbg=()):
    nc = build_program(stages, dbg)
    maps = build_inputs(inp, stages)
    maps = [{k: v for k, v in mp.items() if k in nc._declared_inputs} for mp in maps]
    res = run_bass_kernel_spmd(nc, maps, core_ids=list(range(8)))
    return res


def kernel(**inputs):
    inp = {k: np.asarray(v) for k, v in inputs.items()}
    res = run(inp, stages=('U', 'R', 'N', 'G', 'Y', 'F'))
    out = np.empty((2, T, D), np.float32)
    for c in range(8):
        b, j = c // 4, c % 4
        out[b][own_rows(j)] = res.results[c]['out']
    return out
```

```python
import numpy as np
from contextlib import ExitStack
import concourse.bass as bass
import concourse.mybir as mybir
from concourse.bass_utils import run_bass_kernel_spmd

F32 = mybir.dt.float32
BF16 = mybir.dt.bfloat16
AF = mybir.ActivationFunctionType
ALU = mybir.AluOpType
AX = mybir.AxisListType

ENGS = ['tensor', 'vector', 'scalar', 'gpsimd', 'sync']

D = 1024
T = 8192
NT = 16
NM = 16
DFF = 2816
NFS = 22
NEGM = -30000.0
RT = NT
R2 = True
T_STEPS = 1
U_STEPS = 3
F_EVERY = 16
PE_ALT = 'vector'
U_XN_ENG = 'vector'
PE_SIDE = 'vector'
NO_SELF_SYNC = ()
R_STAG = 'inproj'
NQ = NM
NSUB = 'abc'
NAV = '123'


class Prog:
    def __init__(self, nc):
        self.nc = nc
        self.ops = {e: [] for e in ENGS}
        self.cnt = {e: 0 for e in ENGS}
        self.seen = {e: {} for e in ENGS}
        self.lastw = {}
        self.lastr = {}
        self.dcnt = {}

    def _deps(self, eng, reads, writes):
        need = {}

        def add(k, v):
            if k == eng and (eng == 'tensor' or eng in NO_SELF_SYNC):
                return
            if need.get(k, 0) < v:
                need[k] = v
        for r in reads:
            if r in self.lastw:
                add(*self.lastw[r])
        for w in writes:
            if w in self.lastw:
                add(*self.lastw[w])
            for k, v in self.lastr.get(w, {}).items():
                if k != eng:
                    add(k, v)
        out = []
        for k, v in need.items():
            if self.seen[eng].get(k, 0) < v:
                self.seen[eng][k] = v
                out.append((k, v))
        return out

    EXCL = ('pb', 'sc', 'pc', 'pm', 'Os', 'Ow', 'pT', 'pa')

    def _x(self, r, w):
        xr = [k for k in r if k.startswith(self.EXCL)]
        if xr:
            return [k for k in r if k not in xr], list(w) + xr
        return r, w

    def op(self, eng, fn, r=(), w=()):
        r, w = self._x(r, w)
        waits = self._deps(eng, r, w)
        self.cnt[eng] += 1
        c = self.cnt[eng]
        self.ops[eng].append((waits, fn, (eng, 1)))
        for x in r:
            self.lastr.setdefault(x, {})[eng] = c
        for x in w:
            self.lastw[x] = (eng, c)
            self.lastr[x] = {}

    def dma(self, eng, out, in_, r=(), w=(), sem='dma', **kw):
        waits = self._deps(eng, r, w)
        self.dcnt[sem] = self.dcnt.get(sem, 0) + 16
        c = self.dcnt[sem]
        self.ops[eng].append((waits, lambda e: e.dma_start(out=out, in_=in_, **kw), (sem, 16)))
        for x in r:
            self.lastr.setdefault(x, {})[sem] = c
        for x in w:
            self.lastw[x] = (sem, c)
            self.lastr[x] = {}

    def cc(self, fn, r=(), w=(), sem='cc'):
        waits = self._deps('gpsimd', r, w)
        self.dcnt[sem] = self.dcnt.get(sem, 0) + 1
        c = self.dcnt[sem]
        self.ops['gpsimd'].append((waits, fn, (sem, 1)))
        for x in r:
            self.lastr.setdefault(x, {})[sem] = c
        for x in w:
            self.lastw[x] = (sem, c)
            self.lastr[x] = {}

    def finish_group(self, sem):
        c = self.dcnt[sem]
        for k, (s, v) in list(self.lastw.items()):
            if s == sem:
                self.lastw[k] = (s, c)

    def mm(self, out, lhsT, rhs, start, stop, r, w):
        self.op('tensor', lambda e: e.matmul(out, lhsT=lhsT, rhs=rhs, start=start, stop=stop), r, w)

    def tr(self, out, in_, ident, r, w):
        self.op('tensor', lambda e: e.transpose(out, in_, ident), r, w)

    def act(self, out, in_, func, r, w, **kw):
        self.op('scalar', lambda e: e.activation(out=out, in_=in_, func=func, **kw), r, w)

    def tt(self, eng, out, in0, in1, op, r, w):
        self.op(eng, lambda e: e.tensor_tensor(out=out, in0=in0, in1=in1, op=op), r, w)

    def ts(self, eng, out, in0, s1, s2, op0, op1, r, w):
        if eng == 'scalar':
            assert s2 is None and op0 == ALU.mult
            self.op(eng, lambda e: e.activation(out=out, in_=in0, func=AF.Copy, scale=s1), r, w)
            return
        if s2 is None:
            self.op(eng, lambda e: e.tensor_scalar(out=out, in0=in0, scalar1=s1, scalar2=None, op0=op0), r, w)
        else:
            self.op(eng, lambda e: e.tensor_scalar(out=out, in0=in0, scalar1=s1, scalar2=s2, op0=op0, op1=op1), r, w)

    def stt(self, out, in0, scalar, in1, op0, op1, r, w):
        self.op('vector', lambda e: e.scalar_tensor_tensor(out=out, in0=in0, scalar=scalar, in1=in1, op0=op0, op1=op1), r, w)

    def cp(self, eng, out, in_, r, w):
        if eng == 'scalar':
            self.op(eng, lambda e: e.activation(out=out, in_=in_, func=AF.Copy), r, w)
        else:
            self.op(eng, lambda e: e.tensor_copy(out=out, in_=in_), r, w)

    def emit(self, stack):
        nc = self.nc
        names = sorted(set(ENGS) | set(self.dcnt.keys()))
        sems = {n: stack.enter_context(nc.semaphore('s_' + n)) for n in names}
        final = dict(self.cnt)
        final.update(self.dcnt)
        block = stack.enter_context(nc.Block())

        def mk(engname):
            def body(e):
                for waits, fn, inc in self.ops[engname]:
                    for k, v in waits:
                        e.wait_ge(sems[k], v)
                    fn(e).then_inc(sems[inc[0]], inc[1])
                for k in names:
                    if final.get(k, 0) > 0:
                        e.wait_ge(sems[k], final[k])
            return body
        block.tensor(mk('tensor'))
        block.vector(mk('vector'))
        block.scalar(mk('scalar'))
        block.gpsimd(mk('gpsimd'))
        block.sync(mk('sync'))


def rr(lst, i):
    return lst[i % len(lst)]


def block_u(nc, xb, xo, UT, UTo, ident_d):
    P = Prog(nc)
    with ExitStack() as st:
        sb = lambda n, s, d: st.enter_context(nc.sbuf_tensor(n, s, d))
        ident = sb("u_ident", [128, 128], BF16)
        xt = [sb(f"u_xt{i}", [128, D], F32) for i in range(2)]
        junk = sb("u_junk", [128, D], BF16)
        xn = [sb(f"u_xn{i}", [128, D], BF16) for i in range(2)]
        ss = [sb(f"u_ss{i}", [128, 1], F32) for i in range(2)]
        uT = [sb(f"u_uT{i}", [128, 8, 512], BF16) for i in range(2)]
        pT = [st.enter_context(nc.psum_tensor(f"u_pT{i}", [128, 8, 128], BF16)) for i in range(2)]
        P.dma('sync', ident[:], ident_d, w=['ident'], sem='c')
        nsub = 64 + 16
        for s in range(nsub):
            b = s % 2
            g = s // 4
            gb = g % 2
            if s < 64:
                src = xb[s * 128:(s + 1) * 128, :]
            else:
                src = xo[(s - 64) * 128:(s - 63) * 128, :]
            P.dma('sync', xt[b][:], src, w=[f'xt{b}'], sem=f'x{b}')
            P.act(junk[:], xt[b][:], AF.Square, r=[f'xt{b}'], w=['junk', f'ss{b}'], accum_out=ss[b][:])
            P.ts('vector', ss[b][:], ss[b][:], 1.0 / D, 1e-6, ALU.mult, ALU.add, r=[f'ss{b}'], w=[f'ss{b}'])
            P.act(ss[b][:], ss[b][:], AF.Sqrt, r=[f'ss{b}'], w=[f'ss{b}'])
            P.op('vector', lambda e, b=b: e.reciprocal(out=ss[b][:], in_=ss[b][:]), r=[f'ss{b}'], w=[f'ss{b}'])
            P.ts('vector', xn[b][:], xt[b][:], ss[b][:, 0:1], None, ALU.mult, None, r=[f'xt{b}', f'ss{b}'], w=[f'xn{b}'])
            for c in range(8):
                P.tr(pT[b][:, c, :], xn[b][:, c * 128:(c + 1) * 128], ident[:], r=[f'xn{b}', 'ident'], w=[f'pT{b}'])
            sub = s % 4
            P.cp('scalar', uT[gb][:, :, sub * 128:(sub + 1) * 128], pT[b][:], r=[f'pT{b}'], w=[f'uT{gb}'])
            if sub == 3:
                if s < 64:
                    dst = UT[:, :, g * 512:(g + 1) * 512]
                else:
                    dst = UTo[:, :, (g - 16) * 512:(g - 15) * 512]
                P.dma('gpsimd', dst, uT[gb][:], r=[f'uT{gb}'], w=['UTd'], sem=f'us{gb}')
        P.emit(st)


def block_f(nc, xo, YN, YG, selt_d, w_out, g2T_d, w_gate, w_up, w_down, gfb_d, ident_d, out, WGU, use_y=True, wgu_done=False,
            WOb=None, WDb=None):
    P = Prog(nc)
    with ExitStack() as st:
        sb = lambda n, s, d: st.enter_context(nc.sbuf_tensor(n, s, d))
        ps = lambda n, s, d: st.enter_context(nc.psum_tensor(n, s, d))
        ident = sb("f_ident", [128, 128], BF16)
        g2T = sb("f_g2T", [128, 8], F32)
        selt = sb("f_selt", [128, 4], F32)
        gfb = sb("f_gfb", [128, D], F32)
        wo = sb("f_wo", [128, 8, D], BF16)
        wd = sb("f_wd", [128, NFS, D], BF16)
        wsl = [sb(f"f_wsl{i}", [128, 2, 8, 128], BF16) for i in range(3)]
        if not wgu_done:
            stg = [sb(f"f_stg{i}", [128, DFF], F32) for i in range(2)]
            cbuf = [sb(f"f_cbuf{i}", [128, DFF], BF16) for i in range(2)]
        xt = [sb(f"f_xt{i}", [128, D], F32) for i in range(2)]
        cand = [sb(f"f_cand{i}", [128, 4, 512], BF16) for i in range(2)]
        ycat = [sb(f"f_ycat{i}", [128, D], BF16) for i in range(2)]
        yT = [sb(f"f_yT{i}", [128, 8, 128], BF16) for i in range(2)]
        hsb = [[sb(f"f_h{a}_{i}", [128, D], F32) for i in range(4)] for a in range(2)]
        hn = [sb(f"f_hn{i}", [128, D], BF16) for i in range(2)]
        junk = sb("f_junk", [128, D], BF16)
        ss = [sb(f"f_ss{i}", [128, 1], F32) for i in range(4)]
        hnT = [sb(f"f_hnT{i}", [128, 8, 512], BF16) for i in range(2)]
        sg = [sb(f"f_sg{i}", [128, 512], F32) for i in range(2)]
        hid = sb("f_hid", [128, NFS, 512], BF16)
        h2 = [sb(f"f_h2{i}", [128, D], F32) for i in range(2)]
        pT = [ps(f"f_pT{i}", [128, 8, 128], BF16) for i in range(2)]
        pa = [ps(f"f_pa{i}", [128, 512], F32) for i in range(6)]

        P.dma('sync', ident[:], ident_d, w=['ident'], sem='c')
        P.dma('sync', g2T[:], g2T_d, w=['g2T'], sem='c')
        P.dma('sync', selt[:], selt_d, w=['selt'], sem='c')
        P.dma('sync', gfb[:], gfb_d, w=['gfb'], sem='c')
        if wgu_done:
            P.dma('sync', wo[:], WOb, w=['wo'], sem='c')
            P.dma('gpsimd', wd[:, 0:11, :], WDb[:, 0:11, :], w=['wd'], sem='cwd')
            P.dma('gpsimd', wd[:, 11:22, :], WDb[:, 11:22, :], w=['wd'], sem='cwd')
            P.finish_group('cwd')
        P.finish_group('c')
        k = 0
        engs = ['gpsimd', 'vector']
        if not wgu_done:
            wov = w_out.rearrange("(c p) n -> p c n", p=128)
            wgv = w_gate.rearrange("(c p) n -> p c n", p=128)
            wuv = w_up.rearrange("(c p) n -> p c n", p=128)
            wdv = w_down.rearrange("(c p) n -> p c n", p=128)
            WGUv = WGU.rearrange("fs p w c n -> p w c fs n")
            for wi, wv in enumerate((wgv, wuv)):
                for c in range(8):
                    b = k % 2
                    P.dma('sync', stg[b][:], wv[:, c, :], w=[f'stg{b}'], sem=f'w{b}')
                    P.ts(rr(engs, k), cbuf[b][:], stg[b][:], g2T[:, c:c + 1], None, ALU.mult, None, r=[f'stg{b}', 'g2T'], w=[f'cbuf{b}'])
                    P.dma('gpsimd', WGUv[:, wi, c, :, :], cbuf[b][:].rearrange("p (fs n) -> p fs n", n=128), r=[f'cbuf{b}'], w=['WGU'],
                          sem=f'cs{b}')
                    k += 1
            for c in range(8):
                b = k % 2
                P.dma('sync', stg[b][:, 0:D], wov[:, c, :], w=[f'stg{b}'], sem=f'w{b}')
                P.cp(rr(engs, k), wo[:, c, :], stg[b][:, 0:D], r=[f'stg{b}'], w=['wo'])
                k += 1
            for c in range(0, NFS, 2):
                b = k % 2
                P.dma('sync', stg[b][:, 0:2 * D].rearrange("p (c n) -> p c n", c=2), wdv[:, c:c + 2, :], w=[f'stg{b}'], sem=f'w{b}')
                P.cp(rr(engs, k), wd[:, c:c + 2, :], stg[b][:, 0:2 * D].rearrange("p (c n) -> p c n", c=2), r=[f'stg{b}'], w=['wd'])
                k += 1

        def rms_scale(src, srckey, b):
            P.act(junk[:], src, AF.Square, r=[srckey], w=['junk', f'ss{b}'], accum_out=ss[b][:])
            P.ts('vector', ss[b][:], ss[b][:], 1.0 / D, 1e-6, ALU.mult, ALU.add, r=[f'ss{b}'], w=[f'ss{b}'])
            P.act(ss[b][:], ss[b][:], AF.Sqrt, r=[f'ss{b}'], w=[f'ss{b}'])
            P.op('vector', lambda e: e.reciprocal(out=ss[b][:], in_=ss[b][:]), r=[f'ss{b}'], w=[f'ss{b}'])

        def front(gq):
            a = gq % 2
            for sub in range(4):
                m = gq * 4 + sub
                b = m % 2
                hk = f'h{a}_{sub}'
                P.dma('sync', xt[b][:], xo[m * 128:(m + 1) * 128, :], w=[f'xt{b}'], sem=f'x{b}')
                if use_y:
                    P.dma('gpsimd', ycat[b][:, 512:1024], YN[m * 128:(m + 1) * 128, :], w=[f'ycatn{b}'], sem=f'yn{b}')
                    ygv = YG[m // 4].rearrange("(r q t) f -> r t q f", r=4, t=128)
                    for r_ in range(4):
                        P.dma('gpsimd', cand[b][:, :, r_ * 128:(r_ + 1) * 128], ygv[r_][:, 4 * (m % 4):4 * (m % 4) + 4, :], w=[f'cand{b}'],
                              sem=f'cd{b}')
                    yield
                    P.ts('vector', ycat[b][:, 0:512], cand[b][:, 0, :], selt[:, 0:1], None, ALU.mult, None,
                         r=[f'cand{b}', 'selt'], w=[f'ycatr{b}'])
                    for jj in range(1, 4):
                        P.stt(ycat[b][:, 0:512], cand[b][:, jj, :], selt[:, jj:jj + 1], ycat[b][:, 0:512], ALU.mult, ALU.add,
                              r=[f'cand{b}', f'ycatr{b}'], w=[f'ycatr{b}'])
                    yield
                    for c in range(8):
                        P.tr(pT[0][:, c, :], ycat[b][:, c * 128:(c + 1) * 128], ident[:],
                             r=[f'ycatr{b}', f'ycatn{b}', 'ident'], w=['pT0'])
                    P.cp('scalar', yT[b][:], pT[0][:], r=['pT0'], w=[f'yT{b}'])
                    yield
                    for nh in range(2):
                        for c in range(8):
                            P.mm(pa[nh][:], yT[b][:, c, :], wo[:, c, nh * 512:(nh + 1) * 512], c == 0, c == 7,
                                 r=[f'yT{b}', 'wo'], w=[f'pa{nh}'])
                        P.tt('vector', hsb[a][sub][:, nh * 512:(nh + 1) * 512], pa[nh][:], xt[b][:, nh * 512:(nh + 1) * 512], ALU.add,
                             r=[f'pa{nh}', f'xt{b}'], w=[hk])
                    yield
                else:
                    P.cp('vector', hsb[a][sub][:], xt[b][:], r=[f'xt{b}'], w=[hk])
                rms_scale(hsb[a][sub][:], hk, b)
                P.ts('vector', hn[b][:], hsb[a][sub][:], ss[b][:, 0:1], None, ALU.mult, None, r=[hk, f'ss{b}'], w=[f'hn{b}'])
                yield
                for c in range(8):
                    P.tr(pT[1][:, c, :], hn[b][:, c * 128:(c + 1) * 128], ident[:], r=[f'hn{b}', 'ident'], w=['pT1'])
                P.cp('scalar', hnT[a][:, :, sub * 128:(sub + 1) * 128], pT[1][:], r=['pT1'], w=[f'hnT{a}'])
                yield

        def step(g_, n=1):
            if g_ is None:
                return None
            for _ in range(n):
                try:
                    next(g_)
                except StopIteration:
                    return None
            return g_

        fr = front(0)
        while fr is not None:
            fr = step(fr)
        wk = 0
        for gq in range(4):
            a = gq % 2
            nf = front(gq + 1) if gq < 3 else None
            for fs in range(NFS):
                b = fs % 2
                wb = wk % 3
                wk += 1
                P.dma('sync', wsl[wb][:], WGU[fs], r=['WGU'], w=[f'wsl{wb}'], sem=f'wl{wb}')
                pg, pu = pa[2 + 2 * b], pa[3 + 2 * b]
                for c in range(8):
                    P.mm(pg[:], wsl[wb][:, 0, c, :], hnT[a][:, c, :], c == 0, c == 7, r=[f'wsl{wb}', f'hnT{a}'], w=[f'pa{2 + 2 * b}'])
                for c in range(8):
                    P.mm(pu[:], wsl[wb][:, 1, c, :], hnT[a][:, c, :], c == 0, c == 7, r=[f'wsl{wb}', f'hnT{a}'], w=[f'pa{3 + 2 * b}'])
                P.act(sg[b][:], pg[:], AF.Silu, r=[f'pa{2 + 2 * b}'], w=[f'sg{b}'])
                P.tt('vector', hid[:, fs, :], sg[b][:], pu[:], ALU.mult, r=[f'sg{b}', f'pa{3 + 2 * b}'], w=[f'hid{fs}'])
                nf = step(nf)
            for sub in range(4):
                m = gq * 4 + sub
                b = sub % 2
                for nh in range(2):
                    for fs in range(NFS):
                        P.mm(pa[nh][:], hid[:, fs, sub * 128:(sub + 1) * 128], wd[:, fs, nh * 512:(nh + 1) * 512], fs == 0, fs == NFS - 1,
                             r=[f'hid{fs}', 'wd'], w=[f'pa{nh}'])
                    P.tt('vector', h2[b][:, nh * 512:(nh + 1) * 512], pa[nh][:], hsb[a][sub][:, nh * 512:(nh + 1) * 512], ALU.add,
                         r=[f'pa{nh}', f'h{a}_{sub}'], w=[f'h2{b}'])
                rms_scale(h2[b][:], f'h2{b}', 2 + b)
                P.stt(h2[b][:], h2[b][:], ss[2 + b][:, 0:1], gfb[:], ALU.mult, ALU.mult, r=[f'h2{b}', f'ss{2 + b}', 'gfb'], w=[f'h2{b}'])
                P.dma('sync', out[m * 128:(m + 1) * 128, :], h2[b][:], r=[f'h2{b}'], w=['outd'], sem=f'o{b}')
                nf = step(nf)
            while nf is not None:
                nf = step(nf)
        P.emit(st)


def block_r(nc, UT, w_rw, g1T_d, rwp_d, w2a2_d, g2_d, lnwb_d, rmask_d, bones_d, ident32_d, resetm_d, YR, ntiles=NT):
    P = Prog(nc)
    with ExitStack() as st:
        sb = lambda n, s, d=F32: st.enter_context(nc.sbuf_tensor("r_" + n, s, d))
        pb = [st.enter_context(nc.psum_tensor(f"r_pb{i}", [128, 512], F32)) for i in range(8)]
        g1T = sb("g1T", [128, 8])
        rwp = sb("rwp", [128, 16])
        w2a2 = sb("w2a2", [128, 128])
        g2 = sb("g2", [128, 128])
        lnwb = sb("lnwb", [128, 2, 64])
        rmask = sb("rmask", [128, 3, 64])
        bones = sb("bones", [128, 128])
        ident = sb("ident", [128, 128])
        resetm = sb("resetm", [128, 512])
        ones2 = sb("ones2", [128, 2])
        negw0 = sb("negw0", [128, 1])
        wrw = sb("wrw", [128, 8, 640], BF16)
        stg = [sb(f"stg{i}", [128, 640]) for i in range(2)]
        uT = [sb(f"uT{i}", [128, 8, 512], BF16) for i in range(2)]
        praw = sb("praw", [128, 5, 513])
        psh = sb("psh", [128, 5, 512])
        tA = sb("tA", [128, 512]); e2 = sb("e2", [128, 512]); cumE = sb("cumE", [128, 512]); cumX = sb("cumX", [128, 512])
        Wc = sb("Wc", [128, 512]); Winv = sb("Winv", [128, 512]); Wprev = sb("Wprev", [128, 512])
        av = sb("a", [128, 512]); kk = sb("kk", [128, 512]); tB = sb("tB", [128, 512]); km = sb("km", [128, 512])
        tK = sb("tK", [128, 512]); tX = sb("tX", [128, 512]); Bt = sb("Bt", [128, 512])
        AR = sb("AR", [128, 8, 2, 64])
        bdA = sb("bdA", [128, 8, 128]); bdR = sb("bdR", [128, 8, 128]); bdK = sb("bdK", [128, 8, 128])
        bdB = sb("bdB", [128, 8, 128]); bdV = sb("bdV", [128, 8, 128]); bdX = sb("bdX", [128, 8, 128])
        bdArb = sb("bdArb", [128, 8, 128]); bdAak = sb("bdAak", [128, 8, 128]); bdArk = sb("bdArk", [128, 8, 128])
        Lb = [sb(f"L{i}", [128, 8, 128]) for i in range(2)]
        Gb = [sb(f"G{i}", [128, 8, 128]) for i in range(2)]
        Qb = [sb(f"Q{i}", [128, 8, 128]) for i in range(2)]
        KT = sb("KT", [128, 8, 128]); BT = sb("BT", [128, 8, 128])
        Vst = sb("Vst", [128, 8, 64]); AV = sb("AV", [128, 8, 64]); YV = sb("YV", [128, 8, 64]); KV = sb("KV", [128, 8, 64])
        rk = sb("rk", [128, 8, 2]); gtok = sb("gtok", [128, 8, 64]); Ysb = sb("Ysb", [128, 8, 64])
        tq = sb("tq", [128, 8, 64]); yn = sb("yn", [128, 8, 64]); yo = sb("yo", [128, 8, 64], BF16)
        st8 = sb("st8", [128, 4, 8])
        H = [sb(f"H{i}", [128, 64]) for i in range(2)]
        rhsb = sb("rhsb", [128, 64]); Usb = sb("Usb", [128, 64]); tw = sb("tw", [128, 64])

        for (t, d_, k) in ((g1T, g1T_d, 'g1T'), (rwp, rwp_d, 'rwp'), (w2a2, w2a2_d, 'w2a2'), (g2, g2_d, 'g2'),
                           (lnwb, lnwb_d, 'lnwb'), (rmask, rmask_d, 'rmask'), (bones, bones_d, 'bones'),
                           (ident, ident32_d, 'ident'), (resetm, resetm_d, 'resetm')):
            P.dma('sync', t[:], d_, w=[k], sem='c')
        P.finish_group('c')
        wv = w_rw.rearrange("(c p) n -> p c n", p=128)
        for c in range(8):
            b = c % 2
            P.dma('sync', stg[b][:], wv[:, c, :], w=[f'stg{b}'], sem=f'w{b}')
            P.ts(rr(['gpsimd', 'vector'], c), wrw[:, c, :], stg[b][:], g1T[:, c:c + 1], None, ALU.mult, None,
                 r=[f'stg{b}', 'g1T'], w=['wrw'])
        for t, k in ((bdA, 'bdA'), (bdR, 'bdR'), (bdK, 'bdK'), (bdB, 'bdB'), (bdV, 'bdV'), (bdX, 'bdX'), (bdArb, 'bdArb'),
                     (bdAak, 'bdAak'), (bdArk, 'bdArk'), (Lb[0], 'L0'), (Gb[0], 'G0')):
            P.op('gpsimd', lambda e, t=t: e.memset(t[:], 0.0), w=[k])
        P.op('vector', lambda e: e.memset(H[0][:], 0.0), w=['H0'])
        P.op('vector', lambda e: e.memset(ones2[:], 1.0), w=['ones2'])
        P.op('vector', lambda e: e.memset(praw[:, :, 0:1], 0.0), w=['praw'])
        P.ts('vector', negw0[:], rwp[:, 5:6], -1.0, None, ALU.mult, None, r=['rwp'], w=['negw0'])

        c3 = lambda ap: ap.rearrange("p (c t) -> p c t", t=64)
        mTs, mTi, msk = rmask[:, 0, :], rmask[:, 1, :], rmask[:, 2, :]
        YRv = [y.rearrange("(q t) (h v) -> h t q v", t=64, h=2) for y in YR]

        for i in range(ntiles):
            b = i % 2
            P.dma('sync', uT[b][:], UT[:, :, i * 512:(i + 1) * 512], w=[f'uT{b}'], sem=f'u{b}')
            if i > 0:
                P.cp('gpsimd', praw[:, :, 0:1], praw[:, :, 512:513], r=['praw'], w=['praw'])
            for g in range(5):
                pg = pb[g % 2]
                for c in range(8):
                    P.mm(pg[:], wrw[:, c, g * 128:(g + 1) * 128], uT[b][:, c, :], c == 0, c == 7, r=['wrw', f'uT{b}'], w=[f'pb{g % 2}'])
                P.cp('scalar', praw[:, g, 1:513], pg[:], r=[f'pb{g % 2}'], w=['praw'])
            for g in range(5):
                P.tt('gpsimd', tA[:], praw[:, g, 0:512], praw[:, g, 1:513], ALU.subtract, r=['praw'], w=['tA'])
                P.stt(psh[:, g, :], tA[:], rwp[:, g:g + 1], praw[:, g, 1:513], ALU.mult, ALU.add, r=['tA', 'rwp', 'praw'], w=[f'psh{g}'])
            r_, k_, v_ = psh[:, 0, :], psh[:, 1, :], psh[:, 2, :]
            P.act(psh[0:64, 3, :], psh[0:64, 3, :], AF.Tanh, r=['psh3'], w=['psh3'])
            P.mm(pb[2][:], w2a2[0:64, :], psh[0:64, 3, :], True, True, r=['w2a2', 'psh3'], w=['pb2'])
            P.act(tA[:], pb[2][:], AF.Exp, r=['pb2', 'negw0'], w=['tA'], scale=-1.0, bias=negw0[:, 0:1])
            P.act(tA[:], tA[:], AF.Ln, r=['tA'], w=['tA'], bias=1.0)
            P.act(e2[:], tA[:], AF.Exp, r=['tA'], w=['e2'], scale=-1.0, bias=-0.5)
            P.op('vector', lambda e: e.tensor_tensor_scan(out=cumE[:], data0=resetm[:], data1=e2[:], initial=0.0,
                                                          op0=ALU.mult, op1=ALU.add), r=['resetm', 'e2'], w=['cumE'])
            P.tt('gpsimd', cumX[:], cumE[:], e2[:], ALU.subtract, r=['cumE', 'e2'], w=['cumX'])
            P.act(Wc[:], cumE[:], AF.Exp, r=['cumE'], w=['Wc'], scale=-1.0)
            P.act(Winv[:], cumE[:], AF.Exp, r=['cumE'], w=['Winv'])
            P.act(Wprev[:], cumX[:], AF.Exp, r=['cumX'], w=['Wprev'], scale=-1.0)
            P.mm(pb[3][:], w2a2[64:128, :], psh[64:128, 3, :], True, True, r=['w2a2', 'psh3'], w=['pb3'])
            P.act(av[:], pb[3][:], AF.Sigmoid, r=['pb3', 'rwp'], w=['a'], bias=rwp[:, 6:7])
            P.act(psh[:, 4, :], psh[:, 4, :], AF.Sigmoid, r=['psh4'], w=['psh4'])
            P.ts('gpsimd', kk[:], k_, rwp[:, 7:8], None, ALU.mult, None, r=['psh1', 'rwp'], w=['kk'])
            P.tt('gpsimd', tB[:], kk[:], kk[:], ALU.mult, r=['kk'], w=['tB'])
            P.mm(pb[2][:], bones[:], tB[:], True, True, r=['bones', 'tB'], w=['pb2'])
            P.act(tB[:], pb[2][:], AF.Sqrt, r=['pb2'], w=['tB'])
            P.ts('vector', tB[:], tB[:], 1e-12, None, ALU.max, None, r=['tB'], w=['tB'])
            P.op('vector', lambda e: e.reciprocal(out=tB[:], in_=tB[:]), r=['tB'], w=['tB'])
            P.tt('gpsimd', kk[:], kk[:], tB[:], ALU.mult, r=['kk', 'tB'], w=['kk'])
            P.ts('vector', tB[:], av[:], -1.0, rwp[:, 8:9], ALU.add, ALU.mult, r=['a', 'rwp'], w=['tB'])
            P.stt(km[:], tB[:], 1.0, k_, ALU.add, ALU.mult, r=['tB', 'psh1'], w=['km'])
            P.tt('gpsimd', av[:], kk[:], av[:], ALU.mult, r=['kk', 'a'], w=['a'])
            P.stt(AR[:, :, 0, :], c3(kk[:]), -1.0, c3(Wprev[:]), ALU.mult, ALU.mult, r=['kk', 'Wprev'], w=['AR'])
            P.tt('gpsimd', AR[:, :, 1, :], c3(r_), c3(Wc[:]), ALU.mult, r=['psh0', 'Wc'], w=['AR'])
            P.tt('vector', Bt[:], av[:], Winv[:], ALU.mult, r=['a', 'Winv'], w=['Bt'])
            P.tt('gpsimd', tK[:], km[:], Winv[:], ALU.mult, r=['km', 'Winv'], w=['tK'])
            P.stt(tX[:], r_, rwp[:, 9:10], km[:], ALU.mult, ALU.mult, r=['psh0', 'rwp', 'km'], w=['tX'])
            for h in range(2):
                hs = slice(64 * h, 64 * h + 64)
                for (dst, src, dk, sk, eng) in ((bdA, AR[hs, :, 0, :], 'bdA', 'AR', 'gpsimd'), (bdR, AR[hs, :, 1, :], 'bdR', 'AR', 'scalar'),
                                                (bdK, c3(tK[hs, :]), 'bdK', 'tK', 'gpsimd'), (bdB, c3(Bt[hs, :]), 'bdB', 'Bt', 'scalar'),
                                                (bdV, c3(psh[hs, 2, :]), 'bdV', 'psh2', 'gpsimd'), (bdX, c3(tX[hs, :]), 'bdX', 'tX', 'scalar')):
                    P.cp(eng, dst[hs, :, hs], src, r=[sk], w=[dk])
            for ch in range(8):
                q, o = ch // 4, (ch % 4) * 128
                arc = AR[:, ch, :, :].rearrange("p a t -> p (a t)")
                P.mm(pb[3 + q][:, o:o + 128], bdB[:, ch, :], arc, True, True, r=['bdB', 'AR'], w=[f'pb{3 + q}'])
                P.mm(pb[5 + q][:, o:o + 128], bdK[:, ch, :], arc, True, True, r=['bdK', 'AR'], w=[f'pb{5 + q}'])
                P.mm(pb[2][:, ch * 64:(ch + 1) * 64], bdA[:, ch, :], Bt[:, ch * 64:(ch + 1) * 64], True, True, r=['bdA', 'Bt'], w=['pb2'])
            L, G, Q = Lb[0], Gb[0], Qb[0]
            for h in range(2):
                hs = slice(64 * h, 64 * h + 64)
                bc4 = lambda m_: m_[hs, :].unsqueeze(1).to_broadcast([64, 4, 64])
                for q in range(2):
                    cs = slice(4 * q, 4 * q + 4)
                    v1 = pb[3 + q][hs, :].rearrange("p (c a t) -> p c a t", c=4, a=2)
                    v2 = pb[5 + q][hs, :].rearrange("p (c a t) -> p c a t", c=4, a=2)
                    P.tt('vector', L[hs, cs, hs], v1[:, :, 0, :], bc4(mTs), ALU.mult, r=[f'pb{3 + q}', 'rmask'], w=['L0'])
                    P.tt('vector', bdArb[hs, cs, hs], v1[:, :, 1, :], bc4(mTi), ALU.mult, r=[f'pb{3 + q}', 'rmask'], w=['bdArb'])
                    P.tt('vector', bdAak[hs, cs, hs], v2[:, :, 0, :], bc4(mTs), ALU.mult, r=[f'pb{5 + q}', 'rmask'], w=['bdAak'])
                    P.tt('vector', bdArk[hs, cs, hs], v2[:, :, 1, :], bc4(mTi), ALU.mult, r=[f'pb{5 + q}', 'rmask'], w=['bdArk'])
                P.tt('vector', G[hs, :, hs], c3(pb[2][hs, :]), msk[hs, :].unsqueeze(1).to_broadcast([64, 8, 64]), ALU.mult,
                     r=['pb2', 'rmask'], w=['G0'])
            P.tt('gpsimd', Q[:], L[:], ident[:].unsqueeze(1).to_broadcast([128, 8, 128]), ALU.add, r=['L0', 'ident'], w=['Q0'])
            li = gi = qi = 0
            for lev in range(5):
                Ln_, Gn_, Qn_ = Lb[1 - li], Gb[1 - gi], Qb[1 - qi]
                for ch in range(8):
                    q, o = ch // 4, (ch % 4) * 128
                    P.mm(pb[3 + q][:, o:o + 128], Lb[li][:, ch, :], Gb[gi][:, ch, :], True, True, r=[f'L{li}', f'G{gi}'], w=[f'pb{3 + q}'])
                if lev < 4:
                    for ch in range(8):
                        q, o = ch // 4, (ch % 4) * 128
                        P.mm(pb[5 + q][:, o:o + 128], Gb[gi][:, ch, :], Lb[li][:, ch, :], True, True, r=[f'L{li}', f'G{gi}'], w=[f'pb{5 + q}'])
                for q in range(2):
                    P.cp('scalar', Gn_[:, 4 * q:4 * q + 4, :].rearrange("p c t -> p (c t)"), pb[3 + q][:], r=[f'pb{3 + q}'], w=[f'G{1 - gi}'])
                if lev < 4:
                    for q in range(2):
                        P.cp('vector', Ln_[:, 4 * q:4 * q + 4, :].rearrange("p c t -> p (c t)"), pb[5 + q][:], r=[f'pb{5 + q}'], w=[f'L{1 - li}'])
                gi = 1 - gi
                if lev < 4:
                    li = 1 - li
                for ch in range(8):
                    q, o = ch // 4, (ch % 4) * 128
                    P.mm(pb[q][:, o:o + 128], Gb[gi][:, ch, :], Qb[qi][:, ch, :], True, True, r=[f'G{gi}', f'Q{qi}'], w=[f'pb{q}'])
                for q in range(2):
                    P.tt('vector', Qn_[:, 4 * q:4 * q + 4, :].rearrange("p c t -> p (c t)"), pb[q][:],
                         Qb[qi][:, 4 * q:4 * q + 4, :].rearrange("p c t -> p (c t)"), ALU.add, r=[f'pb{q}', f'Q{qi}'], w=[f'Q{1 - qi}'])
                qi = 1 - qi
            TT = Qb[qi]
            TTk = f'Q{qi}'
            for (src, dst, sk, dk, p0) in ((bdK, KT, 'bdK', 'KT', 3), (bdB, BT, 'bdB', 'BT', 5)):
                for ch in range(8):
                    q, o = ch // 4, (ch % 4) * 128
                    P.tr(pb[p0 + q][:, o:o + 128], src[:, ch, :], ident[:], r=[sk, 'ident'], w=[f'pb{p0 + q}'])
                for q in range(2):
                    P.cp('scalar', dst[:, 4 * q:4 * q + 4, :].rearrange("p c t -> p (c t)"), pb[p0 + q][:], r=[f'pb{p0 + q}'], w=[dk])
            for ch in range(8):
                q, o = ch // 4, (ch % 4) * 128
                P.tr(pb[q][:, o:o + 128], bdV[:, ch, :], ident[:], r=['bdV', 'ident'], w=[f'pb{q}'])
            for h in range(2):
                hs = slice(64 * h, 64 * h + 64)
                for q in range(2):
                    P.cp('vector', Vst[hs, 4 * q:4 * q + 4, :], pb[q][hs, :].rearrange("p (c t) -> p c t", c=4)[:, :, hs],
                         r=[f'pb{q}'], w=['Vst'])
            for ch in range(8):
                cs = slice(ch * 64, ch * 64 + 64)
                P.mm(pb[3][:, cs], bdAak[:, ch, :], Vst[:, ch, :], True, True, r=['bdAak', 'Vst'], w=['pb3'])
                P.mm(pb[4][:, cs], bdArk[:, ch, :], Vst[:, ch, :], True, True, r=['bdArk', 'Vst'], w=['pb4'])
                P.mm(pb[5][:, cs], KT[:, ch, :], Vst[:, ch, :], True, True, r=['KT', 'Vst'], w=['pb5'])
                P.mm(pb[6][:, 2 * ch:2 * ch + 2], bdX[:, ch, :], ones2[:], True, True, r=['bdX', 'ones2'], w=['pb6'])
                for h in range(2):
                    hs = slice(64 * h, 64 * h + 64)
                    P.mm(pb[1][hs, cs], psh[:, 4, cs], g2[:, hs], True, True, r=['psh4', 'g2'], w=['pb1'])
            P.cp('scalar', AV[:].rearrange("p c t -> p (c t)"), pb[3][:], r=['pb3'], w=['AV'])
            P.cp('scalar', YV[:].rearrange("p c t -> p (c t)"), pb[4][:], r=['pb4'], w=['YV'])
            P.cp('scalar', KV[:].rearrange("p c t -> p (c t)"), pb[5][:], r=['pb5'], w=['KV'])
            P.cp('vector', rk[:].rearrange("p c t -> p (c t)"), pb[6][:, 0:16], r=['pb6'], w=['rk'])
            P.cp('vector', gtok[:].rearrange("p c t -> p (c t)"), pb[1][:], r=['pb1'], w=['gtok'])
            for ch in range(8):
                gc = i * 8 + ch
                hc, hn_ = gc % 2, (gc + 1) % 2
                cs = slice(ch * 64, ch * 64 + 64)
                wcol = Wc[:, ch * 64 + 63:ch * 64 + 64]
                P.mm(pb[7][:, 0:64], bdA[:, ch, :], H[hc][:], True, True, r=['bdA', f'H{hc}'], w=['pb7'])
                P.tt('vector', rhsb[:], pb[7][:, 0:64], AV[:, ch, :], ALU.add, r=['pb7', 'AV'], w=['rhsb'])
                P.tt('gpsimd', tw[:], KV[:, ch, :], H[hc][:], ALU.add, r=['KV', f'H{hc}'], w=['tw'])
                P.ts('gpsimd', tw[:], tw[:], wcol, None, ALU.mult, None, r=['tw', 'Wc'], w=['tw'])
                P.mm(pb[7][:, 64:128], TT[:, ch, :], rhsb[:], True, True, r=[TTk, 'rhsb'], w=['pb7'])
                P.cp('scalar', Usb[:], pb[7][:, 64:128], r=['pb7'], w=['Usb'])
                P.mm(pb[7][:, 128:192], BT[:, ch, :], Usb[:], True, True, r=['BT', 'Usb'], w=['pb7'])
                P.stt(H[hn_][:], pb[7][:, 128:192], wcol, tw[:], ALU.mult, ALU.add, r=['pb7', 'Wc', 'tw'], w=[f'H{hn_}'])
                P.mm(pb[2][:, cs], bdR[:, ch, :], H[hc][:], True, False, r=['bdR', f'H{hc}'], w=['pb2'])
                P.mm(pb[2][:, cs], bdArb[:, ch, :], Usb[:], False, True, r=['bdArb', 'Usb'], w=['pb2'])
            P.tt('vector', Ysb[:].rearrange("p c t -> p (c t)"), pb[2][:], YV[:].rearrange("p c t -> p (c t)"), ALU.add,
                 r=['pb2', 'YV'], w=['Ysb'])
            s1, s2, mean, rstd = st8[:, 0, :], st8[:, 1, :], st8[:, 2, :], st8[:, 3, :]
            bc8 = lambda a_: a_.unsqueeze(2).to_broadcast([128, 8, 64])
            P.op('vector', lambda e: e.tensor_reduce(out=s1, in_=Ysb[:], axis=AX.X, op=ALU.add), r=['Ysb'], w=['st8'])
            P.tt('gpsimd', tq[:], Ysb[:], Ysb[:], ALU.mult, r=['Ysb'], w=['tq'])
            P.op('vector', lambda e: e.tensor_reduce(out=s2, in_=tq[:], axis=AX.X, op=ALU.add), r=['tq'], w=['st8'])
            P.ts('vector', mean, s1, 1.0 / 64, None, ALU.mult, None, r=['st8'], w=['st8'])
            P.tt('vector', s1, mean, mean, ALU.mult, r=['st8'], w=['st8'])
            P.stt(s2, s2, 1.0 / 64, s1, ALU.mult, ALU.subtract, r=['st8'], w=['st8'])
            P.ts('vector', s2, s2, 64e-5, None, ALU.add, None, r=['st8'], w=['st8'])
            P.act(s2, s2, AF.Sqrt, r=['st8'], w=['st8'])
            P.op('vector', lambda e: e.reciprocal(out=rstd, in_=s2), r=['st8'], w=['st8'])
            P.tt('gpsimd', yn[:], Ysb[:], bc8(mean), ALU.subtract, r=['Ysb', 'st8'], w=['yn'])
            P.tt('gpsimd', yn[:], yn[:], bc8(rstd), ALU.mult, r=['yn', 'st8'], w=['yn'])
            P.tt('vector', yn[:], yn[:], lnwb[:, 0, :].unsqueeze(1).to_broadcast([128, 8, 64]), ALU.mult, r=['yn', 'lnwb'], w=['yn'])
            P.tt('gpsimd', yn[:], yn[:], lnwb[:, 1, :].unsqueeze(1).to_broadcast([128, 8, 64]), ALU.add, r=['yn', 'lnwb'], w=['yn'])
            P.tt('vector', tq[:], Vst[:], bc8(rk[:, :, 0]), ALU.mult, r=['Vst', 'rk'], w=['tq'])
            P.tt('gpsimd', yn[:], yn[:], tq[:], ALU.add, r=['yn', 'tq'], w=['yn'])
            P.tt('vector', yo[:], yn[:], gtok[:], ALU.mult, r=['yn', 'gtok'], w=['yo'])
            for h in range(2):
                P.dma('gpsimd', YRv[i // 4][h][:, 8 * (i % 4):8 * (i % 4) + 8, :], yo[64 * h:64 * h + 64, :, :], r=['yo'], w=['YRd'],
                      sem=f'yo{h}')
        P.emit(st)


def block_r2(nc, UT, w_rw, g1T_d, rwp_d, w2a2_d, g2_d, lnwb_d, rmask_d, bones_d, ident32_d, resetm_d, YR, identb_d, ntiles=32,
             xb=None, xo=None, UTo=None, fpro=None, fpro2=None):
    P = Prog(nc)
    TS, NC_ = 256, 4
    with ExitStack() as st:
        sb = lambda n, s, d=F32: st.enter_context(nc.sbuf_tensor("r_" + n, s, d))
        pball = [st.enter_context(nc.psum_tensor(f"r_pb{i}", [128, 512], F32)) for i in range(8)]
        g1T = sb("g1T", [128, 8])
        rwp = sb("rwp", [128, 16])
        w2a2 = sb("w2a2", [128, 128])
        g2 = sb("g2", [128, 128])
        lnwb = sb("lnwb", [128, 2, 64])
        rmask = sb("rmask", [128, 3, 64])
        bones = sb("bones", [128, 128])
        ident = sb("ident", [128, 128])
        resetm = sb("resetm", [128, 512])
        ones2 = sb("ones2", [128, 2], BF16)
        identb = sb("identb", [128, 128], BF16)
        g2b = sb("g2b", [128, 128], BF16)
        Hb = [sb(f"Hb{i}", [128, 64], BF16) for i in range(2)]
        negw0 = sb("negw0", [128, 1])
        wrw = sb("wrw", [128, 8, 640], BF16)
        stg = [sb(f"stg{i}", [128, 640]) for i in range(2)]
        H = [sb(f"H{i}", [128, 64]) for i in range(2)]

        uTg = [sb(f"uTg{i}", [128, 8, 512], BF16) for i in range(3)]
        u_xt = [sb(f"u_xt{i}", [128, D]) for i in range(2)]
        u_junk = sb("u_junk", [128, D], BF16)
        u_xn = [sb(f"u_xn{i}", [128, D], BF16) for i in range(2)]
        u_ss = [sb(f"u_ss{i}", [128, 1]) for i in range(2)]
        if fpro is not None:
            f_g2T = sb("f_g2T", [128, 8])
            f_stg = [sb(f"f_stg{i}", [128, DFF]) for i in range(2)]
            f_cbuf = [sb(f"f_cbuf{i}", [128, DFF], BF16) for i in range(2)]

        class S:
            pass
        sets = []
        for s in range(2):
            o = S()
            o.s = s
            o.pb = pball[4 * s:4 * s + 4]
            o.pk = [f'pb{4 * s + i}' for i in range(4)]
            t2 = lambda n: sb(f"{n}_{s}", [128, TS])
            t3 = lambda n, w_, d=F32: sb(f"{n}_{s}", [128, NC_, w_], d)
            o.praw = sb(f"praw_{s}", [128, 5, TS + 1])
            o.psh = sb(f"psh_{s}", [128, 5, TS])
            for n in ('tA', 'e2', 'cumE', 'cumX', 'Wc', 'Winv', 'Wprev', 'av', 'kk', 'tB', 'km', 'tK', 'tX'):
                setattr(o, n, t2(n))
            o.Bt = sb(f"Bt_{s}", [128, TS], BF16)
            o.sgb = sb(f"sgb_{s}", [128, TS], BF16)
            o.AR = sb(f"AR_{s}", [128, NC_, 2, 64], BF16)
            for n in ('bdA', 'bdR', 'bdK', 'bdB', 'bdV', 'bdX', 'bdArb', 'bdAak', 'bdArk', 'KT', 'BT'):
                setattr(o, n, t3(n, 128, BF16))
            o.Lb = [t3(f"L{i}", 128, BF16) for i in range(2)]
            o.Gb = [t3(f"G{i}", 128, BF16) for i in range(2)]
            o.Qb = [t3(f"Q{i}", 128, BF16) for i in range(2)]
            o.Vst = t3('Vst', 64, BF16)
            for n in ('AV', 'YV', 'KV', 'gtok', 'Ysb', 'tq', 'yn'):
                setattr(o, n, t3(n, 64))
            o.rk = t3('rk', 2)
            o.yo = sb(f"yo_{s}", [128, NC_, 64], BF16)
            o.st8 = sb(f"st8_{s}", [128, 4, NC_])
            o.rhsb = sb(f"rhsb_{s}", [128, 64], BF16); o.Usb = sb(f"Usb_{s}", [128, 64], BF16); o.tw = sb(f"tw_{s}", [128, 64])
            sets.append(o)

        for (t, d_, k) in ((g1T, g1T_d, 'g1T'), (rwp, rwp_d, 'rwp'), (w2a2, w2a2_d, 'w2a2'), (g2, g2_d, 'g2'),
                           (lnwb, lnwb_d, 'lnwb'), (rmask, rmask_d, 'rmask'), (bones, bones_d, 'bones'),
                           (ident, ident32_d, 'ident'), (resetm, resetm_d, 'resetm'), (identb, identb_d, 'identb')):
            P.dma('sync', t[:], d_, w=[k], sem='c')
        P.finish_group('c')
        wv = w_rw.rearrange("(c p) n -> p c n", p=128)
        for c in range(8):
            b = c % 2
            P.dma('sync', stg[b][:], wv[:, c, :], w=[f'stg{b}'], sem=f'w{b}')
            P.ts(rr(['vector', 'scalar'], c), wrw[:, c, :], stg[b][:], g1T[:, c:c + 1], None, ALU.mult, None,
                 r=[f'stg{b}', 'g1T'], w=['wrw'])
        for o in sets:
            for n in ('bdA', 'bdR', 'bdK', 'bdB', 'bdV', 'bdX', 'bdArb', 'bdAak', 'bdArk'):
                P.op('gpsimd', lambda e, t=getattr(o, n): e.memset(t[:], 0.0), w=[f'{n}{o.s}'])
            P.op('gpsimd', lambda e, t=o.Lb[0]: e.memset(t[:], 0.0), w=[f'L0{o.s}'])
            P.op('gpsimd', lambda e, t=o.Gb[0]: e.memset(t[:], 0.0), w=[f'G0{o.s}'])
        P.op('vector', lambda e: e.memset(H[0][:], 0.0), w=['H0'])
        P.op('vector', lambda e: e.memset(Hb[0][:], 0.0), w=['Hb0'])
        P.cp('vector', g2b[:], g2[:], r=['g2'], w=['g2b'])
        P.op('vector', lambda e: e.memset(ones2[:], 1.0), w=['ones2'])
        P.op('vector', lambda e: e.memset(sets[0].praw[:, :, 0:1], 0.0), w=['praw0'])
        P.ts('vector', negw0[:], rwp[:, 5:6], -1.0, None, ALU.mult, None, r=['rwp'], w=['negw0'])

        c3 = lambda ap: ap.rearrange("p (c t) -> p c t", t=64)
        fl = lambda ap: ap.rearrange("p c t -> p (c t)")
        mTs, mTi, msk = rmask[:, 0, :], rmask[:, 1, :], rmask[:, 2, :]
        YRv = [y.rearrange("(q t) (h v) -> h t q v", t=64, h=2) for y in YR]

        def tile(i):
            o = sets[i % 2]
            s = o.s
            K = lambda n: f'{n}{s}'
            pb, pk = o.pb, o.pk
            prev = sets[(i - 1) % 2]
            ug = uTg[(i // 2) % 3]
            ugk = f'uTg{(i // 2) % 3}'
            uc = slice((i % 2) * TS, (i % 2) * TS + TS)
            if i > 0:
                P.cp(PE_SIDE, o.praw[:, :, 0:1], prev.praw[:, :, TS:TS + 1], r=[f'praw{prev.s}'], w=[K('praw')])
            for g in range(5):
                pg = pb[g % 2]
                for c in range(8):
                    P.mm(pg[:, 0:TS], wrw[:, c, g * 128:(g + 1) * 128], ug[:, c, uc], c == 0, c == 7, r=['wrw', ugk], w=[pk[g % 2]])
                P.cp('scalar', o.praw[:, g, 1:TS + 1], pg[:, 0:TS], r=[pk[g % 2]], w=[K('praw')])
                yield ('INPROJ_DONE' if g == 4 else None)
            if R_STAG == 'inproj':
                yield 'HALF'
            for g in range(5):
                P.tt(PE_ALT, o.tA[:], o.praw[:, g, 0:TS], o.praw[:, g, 1:TS + 1], ALU.subtract, r=[K('praw')], w=[K('tA')])
                P.stt(o.psh[:, g, :], o.tA[:], rwp[:, g:g + 1], o.praw[:, g, 1:TS + 1], ALU.mult, ALU.add, r=[K('tA'), 'rwp', K('praw')],
                      w=[K(f'psh{g}')])
            yield
            r_, k_, v_ = o.psh[:, 0, :], o.psh[:, 1, :], o.psh[:, 2, :]
            P.mm(pb[1][:, 0:TS], w2a2[64:128, :], o.psh[64:128, 3, :], True, True, r=['w2a2', K('psh3')], w=[pk[1]])
            P.act(o.psh[0:64, 3, :], o.psh[0:64, 3, :], AF.Sigmoid, r=[K('psh3')], w=[K('psh3')], scale=2.0)
            P.act(o.av[:], pb[1][:, 0:TS], AF.Sigmoid, r=[pk[1], 'rwp'], w=[K('a')], bias=rwp[:, 6:7])
            P.act(o.sgb[:], o.psh[:, 4, :], AF.Sigmoid, r=[K('psh4')], w=[K('sgb')])
            P.ts('vector', o.psh[0:64, 3, :], o.psh[0:64, 3, :], 2.0, -1.0, ALU.mult, ALU.add, r=[K('psh3')], w=[K('psh3')])
            P.mm(pb[2][:, 0:TS], w2a2[0:64, :], o.psh[0:64, 3, :], True, True, r=['w2a2', K('psh3')], w=[pk[2]])
            P.act(o.tA[:], pb[2][:, 0:TS], AF.Exp, r=[pk[2], 'negw0'], w=[K('tA')], scale=-1.0, bias=negw0[:, 0:1])
            P.act(o.tA[:], o.tA[:], AF.Ln, r=[K('tA')], w=[K('tA')], bias=1.0)
            P.act(o.e2[:], o.tA[:], AF.Exp, r=[K('tA')], w=[K('e2')], scale=-1.0, bias=-0.5)
            yield
            P.op('vector', lambda e: e.tensor_tensor_scan(out=o.cumE[:], data0=resetm[:, 0:TS], data1=o.e2[:], initial=0.0,
                                                          op0=ALU.mult, op1=ALU.add), r=['resetm', K('e2')], w=[K('cumE')])
            P.tt(PE_SIDE, o.cumX[:], o.cumE[:], o.e2[:], ALU.subtract, r=[K('cumE'), K('e2')], w=[K('cumX')])
            P.act(o.Wc[:], o.cumE[:], AF.Exp, r=[K('cumE')], w=[K('Wc')], scale=-1.0)
            P.act(o.Winv[:], o.cumE[:], AF.Exp, r=[K('cumE')], w=[K('Winv')])
            P.act(o.Wprev[:], o.cumX[:], AF.Exp, r=[K('cumX')], w=[K('Wprev')], scale=-1.0)
            yield
            P.ts(PE_ALT, o.kk[:], k_, rwp[:, 7:8], None, ALU.mult, None, r=[K('psh1'), 'rwp'], w=[K('kk')])
            P.tt(PE_ALT, o.tB[:], o.kk[:], o.kk[:], ALU.mult, r=[K('kk')], w=[K('tB')])
            P.mm(pb[2][:, 0:TS], bones[:], o.tB[:], True, True, r=['bones', K('tB')], w=[pk[2]])
            P.ts('vector', o.tB[:], pb[2][:, 0:TS], 2.0 ** -60, None, ALU.max, None, r=[pk[2]], w=[K('tB')])
            P.act(o.tB[:], o.tB[:], AF.Ln, r=[K('tB')], w=[K('tB')])
            P.act(o.tB[:], o.tB[:], AF.Exp, r=[K('tB')], w=[K('tB')], scale=-0.5)
            yield
            P.tt(PE_ALT, o.kk[:], o.kk[:], o.tB[:], ALU.mult, r=[K('kk'), K('tB')], w=[K('kk')])
            P.ts('vector', o.tB[:], o.av[:], -1.0, rwp[:, 8:9], ALU.add, ALU.mult, r=[K('a'), 'rwp'], w=[K('tB')])
            P.stt(o.km[:], o.tB[:], 1.0, k_, ALU.add, ALU.mult, r=[K('tB'), K('psh1')], w=[K('km')])
            P.tt(PE_ALT, o.av[:], o.kk[:], o.av[:], ALU.mult, r=[K('kk'), K('a')], w=[K('a')])
            yield
            P.stt(o.AR[:, :, 0, :], c3(o.kk[:]), -1.0, c3(o.Wprev[:]), ALU.mult, ALU.mult, r=[K('kk'), K('Wprev')], w=[K('AR')])
            P.tt(PE_SIDE, o.AR[:, :, 1, :], c3(r_), c3(o.Wc[:]), ALU.mult, r=[K('psh0'), K('Wc')], w=[K('AR')])
            P.tt('vector', o.Bt[:], o.av[:], o.Winv[:], ALU.mult, r=[K('a'), K('Winv')], w=[K('Bt')])
            P.tt(PE_SIDE, o.tK[:], o.km[:], o.Winv[:], ALU.mult, r=[K('km'), K('Winv')], w=[K('tK')])
            P.stt(o.tX[:], r_, rwp[:, 9:10], o.km[:], ALU.mult, ALU.mult, r=[K('psh0'), 'rwp', K('km')], w=[K('tX')])
            yield
            for h in range(2):
                hs = slice(64 * h, 64 * h + 64)
                for (dst, src_, dk, sk, eng) in ((o.bdA, o.AR[hs, :, 0, :], 'bdA', 'AR', PE_SIDE), (o.bdR, o.AR[hs, :, 1, :], 'bdR', 'AR', 'scalar'),
                                                 (o.bdK, c3(o.tK[hs, :]), 'bdK', 'tK', 'vector'), (o.bdB, c3(o.Bt[hs, :]), 'bdB', 'Bt', 'scalar'),
                                                 (o.bdV, c3(o.psh[hs, 2, :]), 'bdV', 'psh2', PE_SIDE), (o.bdX, c3(o.tX[hs, :]), 'bdX', 'tX', 'vector')):
                    P.cp(eng, dst[hs, :, hs], src_, r=[K(sk)], w=[K(dk)])
            yield
            for ch in range(NC_):
                oo = ch * 128
                arc = o.AR[:, ch, :, :].rearrange("p a t -> p (a t)")
                P.mm(pb[0][:, oo:oo + 128], o.bdB[:, ch, :], arc, True, True, r=[K('bdB'), K('AR')], w=[pk[0]])
                P.mm(pb[1][:, oo:oo + 128], o.bdK[:, ch, :], arc, True, True, r=[K('bdK'), K('AR')], w=[pk[1]])
                P.mm(pb[2][:, ch * 64:(ch + 1) * 64], o.bdA[:, ch, :], o.Bt[:, ch * 64:(ch + 1) * 64], True, True, r=[K('bdA'), K('Bt')], w=[pk[2]])
            yield
            L, G, Q = o.Lb[0], o.Gb[0], o.Qb[0]
            for h in range(2):
                hs = slice(64 * h, 64 * h + 64)
                bc4 = lambda m_: m_[hs, :].unsqueeze(1).to_broadcast([64, NC_, 64])
                v1 = pb[0][hs, :].rearrange("p (c a t) -> p c a t", c=NC_, a=2)
                v2 = pb[1][hs, :].rearrange("p (c a t) -> p c a t", c=NC_, a=2)
                P.tt('vector', L[hs, :, hs], v1[:, :, 0, :], bc4(mTs), ALU.mult, r=[pk[0], 'rmask'], w=[K('L0')])
                P.tt('vector', o.bdArb[hs, :, hs], v1[:, :, 1, :], bc4(mTi), ALU.mult, r=[pk[0], 'rmask'], w=[K('bdArb')])
                P.tt('vector', o.bdAak[hs, :, hs], v2[:, :, 0, :], bc4(mTs), ALU.mult, r=[pk[1], 'rmask'], w=[K('bdAak')])
                P.tt('vector', o.bdArk[hs, :, hs], v2[:, :, 1, :], bc4(mTi), ALU.mult, r=[pk[1], 'rmask'], w=[K('bdArk')])
                P.tt('vector', G[hs, :, hs], c3(pb[2][hs, 0:TS]), bc4(msk), ALU.mult, r=[pk[2], 'rmask'], w=[K('G0')])
            P.tt(PE_ALT, Q[:], L[:], ident[:].unsqueeze(1).to_broadcast([128, NC_, 128]), ALU.add, r=[K('L0'), 'ident'], w=[K('Q0')])
            yield ('HALF' if R_STAG == 'half' else None)
            li = gi = qi = 0
            for lev in range(5):
                Ln_, Gn_, Qn_ = o.Lb[1 - li], o.Gb[1 - gi], o.Qb[1 - qi]
                for ch in range(NC_):
                    oo = ch * 128
                    P.mm(pb[0][:, oo:oo + 128], o.Lb[li][:, ch, :], o.Gb[gi][:, ch, :], True, True, r=[K(f'L{li}'), K(f'G{gi}')], w=[pk[0]])
                if lev < 4:
                    for ch in range(NC_):
                        oo = ch * 128
                        P.mm(pb[1][:, oo:oo + 128], o.Gb[gi][:, ch, :], o.Lb[li][:, ch, :], True, True, r=[K(f'L{li}'), K(f'G{gi}')], w=[pk[1]])
                yield
                P.cp('scalar', fl(Gn_[:]), pb[0][:], r=[pk[0]], w=[K(f'G{1 - gi}')])
                if lev < 4:
                    P.cp('vector', fl(Ln_[:]), pb[1][:], r=[pk[1]], w=[K(f'L{1 - li}')])
                gi = 1 - gi
                if lev < 4:
                    li = 1 - li
                for ch in range(NC_):
                    oo = ch * 128
                    P.mm(pb[2][:, oo:oo + 128], o.Gb[gi][:, ch, :], o.Qb[qi][:, ch, :], True, True, r=[K(f'G{gi}'), K(f'Q{qi}')], w=[pk[2]])
                yield
                P.tt('vector', fl(Qn_[:]), pb[2][:], fl(o.Qb[qi][:]), ALU.add, r=[pk[2], K(f'Q{qi}')], w=[K(f'Q{1 - qi}')])
                qi = 1 - qi
            TT, TTk = o.Qb[qi], K(f'Q{qi}')
            if R_STAG == 'pre':
                yield 'HALF'
            for (src_, dst, sk, dk, p0) in ((o.bdK, o.KT, 'bdK', 'KT', 0), (o.bdB, o.BT, 'bdB', 'BT', 1)):
                pbv = pb[p0][:].bitcast(BF16)
                for ch in range(NC_):
                    oo = ch * 128
                    P.tr(pbv[:, oo:oo + 128], src_[:, ch, :], identb[:], r=[K(sk), 'identb'], w=[pk[p0]])
                P.cp('scalar', fl(dst[:]), pbv[:, 0:NC_ * 128], r=[pk[p0]], w=[K(dk)])
            pbv2 = pb[2][:].bitcast(BF16)
            for ch in range(NC_):
                oo = ch * 128
                P.tr(pbv2[:, oo:oo + 128], o.bdV[:, ch, :], identb[:], r=[K('bdV'), 'identb'], w=[pk[2]])
            for h in range(2):
                hs = slice(64 * h, 64 * h + 64)
                P.cp('vector', o.Vst[hs, :, :], pbv2[hs, 0:NC_ * 128].rearrange("p (c t) -> p c t", c=NC_)[:, :, hs], r=[pk[2]], w=[K('Vst')])
            yield
            for ch in range(NC_):
                cs = slice(ch * 64, ch * 64 + 64)
                P.mm(pb[0][:, cs], o.bdAak[:, ch, :], o.Vst[:, ch, :], True, True, r=[K('bdAak'), K('Vst')], w=[pk[0]])
                P.mm(pb[0][:, 256 + ch * 64:256 + ch * 64 + 64], o.bdArk[:, ch, :], o.Vst[:, ch, :], True, True, r=[K('bdArk'), K('Vst')], w=[pk[0]])
                P.mm(pb[1][:, cs], o.KT[:, ch, :], o.Vst[:, ch, :], True, True, r=[K('KT'), K('Vst')], w=[pk[1]])
                P.mm(pb[1][:, 256 + 2 * ch:256 + 2 * ch + 2], o.bdX[:, ch, :], ones2[:], True, True, r=[K('bdX'), 'ones2'], w=[pk[1]])
                for h in range(2):
                    hs = slice(64 * h, 64 * h + 64)
                    P.mm(pb[2][hs, cs], o.sgb[:, cs], g2b[:, hs], True, True, r=[K('sgb'), 'g2b'], w=[pk[2]])
            yield
            P.cp('scalar', fl(o.AV[:]), pb[0][:, 0:256], r=[pk[0]], w=[K('AV')])
            P.cp('scalar', fl(o.YV[:]), pb[0][:, 256:512], r=[pk[0]], w=[K('YV')])
            P.cp('scalar', fl(o.KV[:]), pb[1][:, 0:256], r=[pk[1]], w=[K('KV')])
            P.cp('vector', fl(o.rk[:]), pb[1][:, 256:256 + 2 * NC_], r=[pk[1]], w=[K('rk')])
            P.cp('vector', fl(o.gtok[:]), pb[2][:, 0:256], r=[pk[2]], w=[K('gtok')])
            if R_STAG == 'chain':
                yield 'HALF'
            yield 'CHAIN'
            for ch in range(NC_):
                gc = i * NC_ + ch
                hc, hn_ = gc % 2, (gc + 1) % 2
                cs = slice(ch * 64, ch * 64 + 64)
                wcol = o.Wc[:, ch * 64 + 63:ch * 64 + 64]
                P.mm(pb[3][:, 0:64], o.bdA[:, ch, :], Hb[hc][:], True, True, r=[K('bdA'), f'Hb{hc}'], w=[pk[3]])
                P.tt('vector', o.rhsb[:], pb[3][:, 0:64], o.AV[:, ch, :], ALU.add, r=[pk[3], K('AV')], w=[K('rhsb')])
                P.tt(PE_SIDE, o.tw[:], o.KV[:, ch, :], H[hc][:], ALU.add, r=[K('KV'), f'H{hc}'], w=[K('tw')])
                P.ts(PE_SIDE, o.tw[:], o.tw[:], wcol, None, ALU.mult, None, r=[K('tw'), K('Wc')], w=[K('tw')])
                P.mm(pb[3][:, 64:128], TT[:, ch, :], o.rhsb[:], True, True, r=[TTk, K('rhsb')], w=[pk[3]])
                P.cp('scalar', o.Usb[:], pb[3][:, 64:128], r=[pk[3]], w=[K('Usb')])
                P.mm(pb[3][:, 128:192], o.BT[:, ch, :], o.Usb[:], True, True, r=[K('BT'), K('Usb')], w=[pk[3]])
                P.stt(Hb[hn_][:], pb[3][:, 128:192], wcol, o.tw[:], ALU.mult, ALU.add, r=[pk[3], K('Wc'), K('tw')], w=[f'Hb{hn_}'])
                P.stt(H[hn_][:], pb[3][:, 128:192], wcol, o.tw[:], ALU.mult, ALU.add, r=[pk[3], K('Wc'), K('tw')], w=[f'H{hn_}'])
                P.mm(pb[3][:, 256 + ch * 64:256 + ch * 64 + 64], o.bdR[:, ch, :], Hb[hc][:], True, False, r=[K('bdR'), f'Hb{hc}'], w=[pk[3]])
                P.mm(pb[3][:, 256 + ch * 64:256 + ch * 64 + 64], o.bdArb[:, ch, :], o.Usb[:], False, True, r=[K('bdArb'), K('Usb')], w=[pk[3]])
                yield
            yield 'CHAIN_DONE'
            P.tt('vector', fl(o.Ysb[:]), pb[3][:, 256:512], fl(o.YV[:]), ALU.add, r=[pk[3], K('YV')], w=[K('Ysb')])
            s1, s2, mean, rstd = o.st8[:, 0, :], o.st8[:, 1, :], o.st8[:, 2, :], o.st8[:, 3, :]
            bc8 = lambda a_: a_.unsqueeze(2).to_broadcast([128, NC_, 64])
            P.op('vector', lambda e: e.tensor_reduce(out=s1, in_=o.Ysb[:], axis=AX.X, op=ALU.add), r=[K('Ysb')], w=[K('st8')])
            P.tt(PE_SIDE, o.tq[:], o.Ysb[:], o.Ysb[:], ALU.mult, r=[K('Ysb')], w=[K('tq')])
            P.op('vector', lambda e: e.tensor_reduce(out=s2, in_=o.tq[:], axis=AX.X, op=ALU.add), r=[K('tq')], w=[K('st8')])
            P.ts('vector', mean, s1, 1.0 / 64, None, ALU.mult, None, r=[K('st8')], w=[K('st8')])
            P.tt('vector', s1, mean, mean, ALU.mult, r=[K('st8')], w=[K('st8')])
            P.stt(s2, s2, 1.0 / 64, s1, ALU.mult, ALU.subtract, r=[K('st8')], w=[K('st8')])
            P.ts('vector', s2, s2, 64e-5, None, ALU.add, None, r=[K('st8')], w=[K('st8')])
            P.act(s2, s2, AF.Ln, r=[K('st8')], w=[K('st8')])
            P.act(rstd, s2, AF.Exp, r=[K('st8')], w=[K('st8')], scale=-0.5)
            yield
            P.tt(PE_ALT, o.yn[:], o.Ysb[:], bc8(mean), ALU.subtract, r=[K('Ysb'), K('st8')], w=[K('yn')])
            P.tt(PE_ALT, o.yn[:], o.yn[:], bc8(rstd), ALU.mult, r=[K('yn'), K('st8')], w=[K('yn')])
            P.tt('vector', o.yn[:], o.yn[:], lnwb[:, 0, :].unsqueeze(1).to_broadcast([128, NC_, 64]), ALU.mult, r=[K('yn'), 'lnwb'], w=[K('yn')])
            P.tt(PE_ALT, o.yn[:], o.yn[:], lnwb[:, 1, :].unsqueeze(1).to_broadcast([128, NC_, 64]), ALU.add, r=[K('yn'), 'lnwb'], w=[K('yn')])
            P.tt('vector', o.tq[:], o.Vst[:], bc8(o.rk[:, :, 0]), ALU.mult, r=[K('Vst'), K('rk')], w=[K('tq')])
            P.tt(PE_ALT, o.yn[:], o.yn[:], o.tq[:], ALU.add, r=[K('yn'), K('tq')], w=[K('yn')])
            P.tt('vector', o.yo[:], o.yn[:], o.gtok[:], ALU.mult, r=[K('yn'), K('gtok')], w=[K('yo')])
            for h in range(2):
                P.dma('sync', YRv[i // 8][h][:, NC_ * (i % 8):NC_ * (i % 8) + NC_, :], o.yo[64 * h:64 * h + 64, :, :], r=[K('yo')], w=['YRd'],
                      sem=f'yo{s}{h}')

        upT = pball[7][:].bitcast(BF16).rearrange("p (c t) -> p c t", c=8)

        def ugen():
            for s_ in range(64 + 16):
                b = s_ % 2
                g = s_ // 4
                gb = g % 3
                sub = s_ % 4
                if sub == 0:
                    yield ('UGROUP', g)
                srcx = xb[s_ * 128:(s_ + 1) * 128, :] if s_ < 64 else xo[(s_ - 64) * 128:(s_ - 63) * 128, :]
                P.dma('sync', u_xt[b][:], srcx, w=[f'uxt{b}'], sem=f'ux{b}')
                P.act(u_junk[:], u_xt[b][:], AF.Square, r=[f'uxt{b}'], w=['ujunk', f'uss{b}'], accum_out=u_ss[b][:])
                P.ts('vector', u_ss[b][:], u_ss[b][:], 1.0 / D, 1e-6, ALU.mult, ALU.add, r=[f'uss{b}'], w=[f'uss{b}'])
                P.act(u_ss[b][:], u_ss[b][:], AF.Ln, r=[f'uss{b}'], w=[f'uss{b}'])
                P.act(u_ss[b][:], u_ss[b][:], AF.Exp, r=[f'uss{b}'], w=[f'uss{b}'], scale=-0.5)
                P.ts(U_XN_ENG, u_xn[b][:], u_xt[b][:], u_ss[b][:, 0:1], None, ALU.mult, None, r=[f'uxt{b}', f'uss{b}'], w=[f'uxn{b}'])
                for c in range(8):
                    P.tr(upT[:, c, :], u_xn[b][:, c * 128:(c + 1) * 128], identb[:], r=[f'uxn{b}', 'identb'], w=['pb7'])
                P.cp('scalar', uTg[gb][:, :, sub * 128:(sub + 1) * 128], upT, r=['pb7'], w=[f'uTg{gb}'])
                if sub == 3:
                    dst = UT[:, :, g * 512:(g + 1) * 512] if s_ < 64 else UTo[:, :, (g - 16) * 512:(g - 15) * 512]
                    P.dma('sync', dst, uTg[gb][:], r=[f'uTg{gb}'], w=['UTd'], sem=f'us{gb}')
                    yield ('UDONE', g)
                else:
                    yield None

        def fgen():
            w_gate, w_up, g2T_d_, WGU = fpro
            P.dma('sync', f_g2T[:], g2T_d_, w=['fg2T'], sem='fc')
            WGUv = WGU.rearrange("fs p w c n -> p w c fs n")
            k = 0
            for wi, wv_ in enumerate((w_gate.rearrange("(c p) n -> p c n", p=128), w_up.rearrange("(c p) n -> p c n", p=128))):
                for c in range(8):
                    b = k % 2
                    P.dma('sync', f_stg[b][:], wv_[:, c, :], w=[f'fstg{b}'], sem=f'fw{b}')
                    P.ts(rr(['vector', 'scalar'], k), f_cbuf[b][:], f_stg[b][:], f_g2T[:, c:c + 1], None, ALU.mult, None,
                         r=[f'fstg{b}', 'fg2T'], w=[f'fcbuf{b}'])
                    P.dma('sync', WGUv[:, wi, c, :, :], f_cbuf[b][:].rearrange("p (fs n) -> p fs n", n=128), r=[f'fcbuf{b}'], w=['WGU'],
                          sem=f'fcs{b}')
                    k += 1
                    yield None
            w_out_, w_down_, WOb_, WDb_ = fpro2
            wov = w_out_.rearrange("(c p) n -> p c n", p=128)
            wdv = w_down_.rearrange("(c p) n -> p c n", p=128)
            for c in range(0, 8, 2):
                b = k % 2
                P.dma('sync', f_stg[b][:, 0:2 * D].rearrange("p (c n) -> p c n", c=2), wov[:, c:c + 2, :], w=[f'fstg{b}'], sem=f'fw{b}')
                P.cp(rr(['vector', 'scalar'], k), f_cbuf[b][:, 0:2 * D], f_stg[b][:, 0:2 * D], r=[f'fstg{b}'], w=[f'fcbuf{b}'])
                P.dma('sync', WOb_[:, c:c + 2, :], f_cbuf[b][:, 0:2 * D].rearrange("p (c n) -> p c n", c=2), r=[f'fcbuf{b}'], w=['WOb'],
                      sem=f'fcs{b}')
                k += 1
                yield None
            for c in range(0, NFS, 2):
                b = k % 2
                P.dma('sync', f_stg[b][:, 0:2 * D].rearrange("p (c n) -> p c n", c=2), wdv[:, c:c + 2, :], w=[f'fstg{b}'], sem=f'fw{b}')
                P.cp(rr(['vector', 'scalar'], k), f_cbuf[b][:, 0:2 * D], f_stg[b][:, 0:2 * D], r=[f'fstg{b}'], w=[f'fcbuf{b}'])
                P.dma('sync', WDb_[:, c:c + 2, :], f_cbuf[b][:, 0:2 * D].rearrange("p (c n) -> p c n", c=2), r=[f'fcbuf{b}'], w=['WDb'],
                      sem=f'fcs{b}')
                k += 1
                yield None

        active = []
        nxt = 0
        chain_turn = 0
        waiting = {}
        half_done = -1
        inproj_done = -1
        ug = ugen()
        fg = fgen() if fpro is not None else None
        u_groups_done = 0
        u_blocked = None
        u_alive = True
        rounds = 0
        while nxt < ntiles or active or u_alive or fg is not None:
            rounds += 1
            for _ in range(U_STEPS):
                if not u_alive:
                    break
                if u_blocked is not None:
                    if u_blocked >= 3 and u_blocked - 3 < 16 and inproj_done < min(ntiles - 1, 2 * (u_blocked - 3) + 1):
                        break
                    u_blocked = None
                try:
                    tok = next(ug)
                except StopIteration:
                    u_alive = False
                    break
                if tok is not None and tok[0] == 'UGROUP':
                    u_blocked = tok[1]
                elif tok is not None and tok[0] == 'UDONE':
                    u_groups_done = tok[1] + 1
            if fg is not None and rounds % F_EVERY == 0:
                try:
                    next(fg)
                except StopIteration:
                    fg = None
            while len(active) < 2 and nxt < ntiles and half_done >= nxt - 1 and u_groups_done > nxt // 2:
                active.append((nxt, tile(nxt)))
                nxt += 1
            for (ti, g) in list(active):
                for _ in range(T_STEPS):
                    if waiting.get(ti) == 'CHAIN' and chain_turn != ti:
                        break
                    waiting.pop(ti, None)
                    try:
                        tok = next(g)
                    except StopIteration:
                        active.remove((ti, g))
                        break
                    if tok == 'HALF':
                        half_done = ti
                    elif tok == 'INPROJ_DONE':
                        inproj_done = ti
                    elif tok == 'CHAIN':
                        waiting[ti] = 'CHAIN'
                    elif tok == 'CHAIN_DONE':
                        chain_turn = ti + 1
            assert rounds < 100000
        P.emit(st)


def nct_of(m):
    return min(4, (8 * (4 * m + 3) + 6) // 128 + 1)


def block_na(nc, pers, UT, w_kv, g1T_d, kaug_d, gather=None):
    P = Prog(nc)
    if gather is not None:
        YR_, YG_ = gather
        for k_ in range(4):
            P.cc(lambda e, k_=k_: e.collective_compute("AllGather", ALU.bypass, replica_groups=[[0, 1, 2, 3], [4, 5, 6, 7]],
                                                       ins=[YR_[k_].ap().opt()], outs=[YG_[k_].ap().opt()]), w=[f'YG{k_}'])
    KaS, KaW, Vs, Vw, kcT, vcT = pers['KaS'], pers['KaW'], pers['Vs'], pers['Vw'], pers['kcT'], pers['vcT']
    with ExitStack() as st:
        sb = lambda n, s, d=F32: st.enter_context(nc.sbuf_tensor("na_" + n, s, d))
        pb = [st.enter_context(nc.psum_tensor(f"na_pb{i}", [128, 512], F32)) for i in range(8)]
        g1T = sb("g1T", [128, 8])
        wkv = sb("wkv", [128, 8, 768], BF16)
        wst = sb("wst", [128, 8, 768])
        uT = [sb(f"uT{i}", [128, 8, 512], BF16) for i in range(2)]
        kst = [sb(f"kst{i}", [128, 512], BF16) for i in range(2)]
        P.dma('sync', g1T[:], g1T_d, w=['g1T'], sem='c')
        for kvh in range(2):
            P.dma('sync', KaS[kvh][64:68, :], kaug_d, w=[f'KaS{kvh}'], sem='c')
            P.dma('sync', KaW[kvh][64:68, :], kaug_d, w=[f'KaW{kvh}'], sem='c')
        P.finish_group('c')
        wv = w_kv.rearrange("(c p) n -> p c n", p=128)
        P.dma('sync', wst[:, 0:4], wv[:, 0:4], w=['wst0'], sem='w0')
        P.dma('gpsimd', wst[:, 4:8], wv[:, 4:8], w=['wst1'], sem='w1')
        for c in range(8):
            P.ts(rr(['vector', 'scalar'], c), wkv[:, c, :], wst[:, c, :], g1T[:, c:c + 1], None, ALU.mult, None,
                 r=[f'wst{c // 4}', 'g1T'], w=['wkv'])
        P.op('gpsimd', lambda e: e.memset(Vs[:, :, :, 64:66], 1.0), w=['Vs'])
        P.op('gpsimd', lambda e: e.memset(Vw[:, :, :, 64:66], 1.0), w=['Vw'])
        P.op('gpsimd', lambda e: e.memset(kcT[:, :, 512:514], 0.0), w=['kcT'])
        P.op('gpsimd', lambda e: e.memset(vcT[:, :, 512:514], 0.0), w=['vcT'])
        k = 0
        for i in range(NT):
            b = i % 2
            cs = slice(i * 512, (i + 1) * 512)
            P.dma('sync', uT[b][:], UT[:, :, cs], w=[f'uT{b}'], sem=f'u{b}')
            for (dst, dk, c0) in ((kcT, 'kcT', 0), (vcT, 'vcT', 128)) if '1' in NAV else ():
                pk = k % 4; k += 1
                for c in range(8):
                    P.mm(pb[pk][:], wkv[:, c, c0:c0 + 128], uT[b][:, c, :], c == 0, c == 7, r=['wkv', f'uT{b}'], w=[f'pb{pk}'])
                P.cp(rr(['scalar', 'vector'], k), dst[:, :, 32 * i:32 * i + 32], pb[pk][:].rearrange("p (n ph) -> p ph n", ph=16),
                     r=[f'pb{pk}'], w=[dk])
            for ti_, (dst, dk, c0) in enumerate(((KaS, 'KaS', 256), (KaW, 'KaW', 384))) if '2' in NAV else ():
                pk = k % 4; k += 1
                sk = kst[(2 * i + ti_) % 2]
                skk = f'kst{(2 * i + ti_) % 2}'
                for c in range(8):
                    P.mm(pb[pk][:], wkv[:, c, c0:c0 + 128], uT[b][:, c, :], c == 0, c == 7, r=['wkv', f'uT{b}'], w=[f'pb{pk}'])
                P.cp('scalar', dst[0][0:64, cs], pb[pk][0:64, :], r=[f'pb{pk}'], w=[f'{dk}0'])
                P.cp('vector', sk[64:128, :], pb[pk][64:128, :], r=[f'pb{pk}'], w=[skk])
                P.dma('gpsimd', dst[1][0:64, cs], sk[64:128, :], r=[skk], w=[f'{dk}1'], sem=f'km{(2 * i + ti_) % 2}')
            for sub in range(4) if '3' in NAV else ():
                ti = 4 * i + sub
                pk = 4 + (k % 4); k += 1
                for c in range(8):
                    P.mm(pb[pk][:, 0:256], uT[b][:, c, sub * 128:(sub + 1) * 128], wkv[:, c, 512:768], c == 0, c == 7,
                         r=['wkv', f'uT{b}'], w=[f'pb{pk}'])
                P.cp('scalar', Vs[:, ti, :, 0:64], pb[pk][:, 0:128].rearrange("p (h d) -> p h d", h=2), r=[f'pb{pk}'], w=['Vs'])
                P.cp('vector', Vw[:, ti, :, 0:64], pb[pk][:, 128:256].rearrange("p (h d) -> p h d", h=2), r=[f'pb{pk}'], w=['Vw'])
        P.emit(st)


def block_nb(nc, pers, w1k_d, w1v_d, w2k_d, w2v_d, pek_d, pev_d, kcaug_d, ovc_d):
    P = Prog(nc)
    kcT, vcT, KaC, CV = pers['kcT'], pers['vcT'], pers['KaC'], pers['CV']
    with ExitStack() as st:
        sb = lambda n, s, d=F32: st.enter_context(nc.sbuf_tensor("nb_" + n, s, d))
        pb = [st.enter_context(nc.psum_tensor(f"nb_pb{i}", [128, 512], F32)) for i in range(8)]
        w1b = [sb(f"w1b{i}", [128, 32, 256], BF16) for i in range(2)]
        stg = [sb(f"stg{i}", [128, 8, 256]) for i in range(2)]
        w2s = sb("w2s", [128, 2, 2, 64])
        w2b = sb("w2b", [128, 2, 2, 64], BF16)
        pes = sb("pes", [128, 2, 64])
        peb = sb("peb", [128, 2, 64], BF16)
        bias = sb("bias", [128, 2, 2, 2])
        hb = [sb(f"hb{i}", [128, 512]) for i in range(2)]
        t1 = [sb(f"t1{i}", [128, 512]) for i in range(2)]
        hT = [sb(f"hT{i}", [128, 512], BF16) for i in range(2)]
        P.dma('sync', w2s[:, 0, :, :], w2k_d.rearrange("(h p) d -> p h d", p=128), w=['w2s'], sem='c')
        P.dma('sync', w2s[:, 1, :, :], w2v_d.rearrange("(h p) d -> p h d", p=128), w=['w2s'], sem='c')
        P.dma('sync', pes[:, 0, :], pek_d, w=['pes'], sem='c')
        P.dma('sync', pes[:, 1, :], pev_d, w=['pes'], sem='c')
        for kvh in range(2):
            P.dma('sync', KaC[kvh][64:68, :], kcaug_d, w=[f'KaC{kvh}'], sem='c')
            P.dma('sync', CV[:, :, kvh, 0:128], ovc_d, w=['CV'], sem='c')
        P.finish_group('c')
        P.cp('vector', w2b[:], w2s[:], r=['w2s'], w=['w2b'])
        P.cp('vector', peb[:], pes[:], r=['pes'], w=['peb'])
        P.op('gpsimd', lambda e: e.memset(CV[:, :, :, 192:194], 1.0), w=['CV'])
        k = 0
        for z, w1d in enumerate((w1k_d, w1v_d)):
            w1v_ = w1d.rearrange("(l d) n -> d l n", d=64)
            for q in range(4):
                for dup in range(2):
                    b = k % 2; k += 1
                    P.dma('sync', stg[b][64 * dup:64 * dup + 64, :, :], w1v_[:, 8 * q:8 * q + 8, :], w=[f'stg{b}'], sem=f'w{b}')
                    P.cp(rr(['vector', 'scalar'], k), w1b[z][64 * dup:64 * dup + 64, 8 * q:8 * q + 8, :], stg[b][64 * dup:64 * dup + 64, :, :],
                         r=[f'stg{b}'], w=[f'w1b{z}'])
        kb = 0
        for z in range(2):
            zT = kcT if z == 0 else vcT
            zk = 'kcT' if z == 0 else 'vcT'
            for half in range(2):
                for l in range(32):
                    P.mm(pb[7][:, 2 * half:2 * half + 2], w1b[z][0:64, l, half * 128:(half + 1) * 128], peb[0:64, z, 2 * l:2 * l + 2],
                         l == 0, l == 31, r=[f'w1b{z}', 'peb'], w=['pb7'])
                P.cp('vector', bias[:, z, half, :], pb[7][:, 2 * half:2 * half + 2], r=['pb7'], w=['bias'])
            for kvh in range(2):
                ks = slice(64 * kvh, 64 * kvh + 64)
                for half in range(2):
                    pk = kb % 4; kb += 1
                    b = half
                    for l in range(32):
                        P.mm(pb[pk][:], w1b[z][ks, l, half * 128:(half + 1) * 128], zT[ks, l % 16, l // 16:l // 16 + 512], l == 0, l == 31,
                             r=[f'w1b{z}', zk], w=[f'pb{pk}'])
                    P.act(hb[b][:], pb[pk][:], AF.Identity, r=[f'pb{pk}', 'bias'], w=[f'hb{b}'], bias=bias[:, z, half, 0:1])
                    P.tt('vector', t1[b][:], hb[b][:], hb[b][:], ALU.mult, r=[f'hb{b}'], w=[f't1{b}'])
                    P.ts('vector', t1[b][:], t1[b][:], 0.044715, 1.0, ALU.mult, ALU.add, r=[f't1{b}'], w=[f't1{b}'])
                    P.tt('vector', t1[b][:], t1[b][:], hb[b][:], ALU.mult, r=[f't1{b}', f'hb{b}'], w=[f't1{b}'])
                    P.act(t1[b][:], t1[b][:], AF.Sigmoid, r=[f't1{b}'], w=[f't1{b}'], scale=1.5957691216057308)
                    P.tt('vector', hT[b][:], hb[b][:], t1[b][:], ALU.mult, r=[f'hb{b}', f't1{b}'], w=[f'hT{b}'])
                if z == 0:
                    for half in range(2):
                        P.mm(pb[4][0:64, :], w2b[:, 0, half, :], hT[half][:], half == 0, half == 1, r=['w2b', f'hT{half}'], w=['pb4'])
                    P.cp('scalar', KaC[kvh][0:64, :], pb[4][0:64, :], r=['pb4'], w=[f'KaC{kvh}'])
                else:
                    for ct in range(4):
                        for half in range(2):
                            P.mm(pb[5][:, ct * 64:(ct + 1) * 64], hT[half][:, ct * 128:(ct + 1) * 128], w2b[:, 1, half, :], half == 0, half == 1,
                                 r=['w2b', f'hT{half}'], w=['pb5'])
                    P.cp('scalar', CV[:, :, kvh, 128:192], pb[5][:, 0:256].rearrange("p (c d) -> p c d", c=4), r=['pb5'], w=['CV'])
        P.emit(st)


def block_nc(nc, pers, UTo, w_q, w_gl, g1T_d, identb_d, ident32_d, OHx_d, cmask_d, seldiag_d, winmask_d, fmk_d, fma_d, qaug_d, YN,
             nm=NM):
    P = Prog(nc)
    KaS, KaW, Vs, Vw, KaC, CV = pers['KaS'], pers['KaW'], pers['Vs'], pers['Vw'], pers['KaC'], pers['CV']
    with ExitStack() as st:
        sb = lambda n, s, d=F32: st.enter_context(nc.sbuf_tensor("nc_" + n, s, d))
        sc = [st.enter_context(nc.psum_tensor(f"nc_sc{i}", [128, 512], F32)) for i in range(3)]
        Os = st.enter_context(nc.psum_tensor("nc_Os", [128, 512], F32))
        Ow = st.enter_context(nc.psum_tensor("nc_Ow", [128, 512], F32))
        pc = [st.enter_context(nc.psum_tensor(f"nc_pc{i}", [128, 512], F32)) for i in range(2)]
        pm = st.enter_context(nc.psum_tensor("nc_pm", [128, 512], F32))
        g1T = sb("g1T", [128, 8])
        identb = sb("identb", [128, 128], BF16)
        ident32 = sb("ident32", [128, 128])
        cmask = sb("cmask", [128, NM, 4, 128], BF16)
        seldiag = sb("seldiag", [128, 4, 512], BF16)
        winmask = sb("winmask", [128, 8, 512], BF16)
        fmk = sb("fmk", [128, NM, 128], BF16)
        fma = sb("fma", [128, NM, 128], BF16)
        wq = sb("wq", [128, 8, 512], BF16)
        wgl = sb("wgl", [128, 8, 24], BF16)
        wqs = sb("wqs", [128, 8, 512])
        wgs = sb("wgs", [128, 8, 24])
        uTo = [sb(f"uTo{i}", [128, 8, 128], BF16) for i in range(2)]
        QAx = [[sb(f"QA{i}_{p}", [128, 512], BF16) for p in range(3)] for i in range(2)]
        gates = [sb(f"gates{i}", [128, 24]) for i in range(2)]
        PT = [sb(f"PT{i}", [128, 512], BF16) for i in range(4)]
        PcT = [[sb(f"PcT{b}_{i}", [128, 512], BF16) for i in range(4)] for b in range(2)]
        NMr = [sb(f"NMr{i}", [128, 512], BF16) for i in range(2)]
        imp = sb("imp", [128, 128]); imp2 = sb("imp2", [128, 128]); tmpi = sb("tmpi", [128, 128]); negm = sb("negm", [128, 128])
        m8 = sb("m8", [128, 2, 8])
        zc = [sb(f"zc{i}", [128, 3, 4]) for i in range(2)]
        rg = [sb(f"rg{i}", [128, 3, 4]) for i in range(2)]
        Osb = [sb(f"Osb{i}", [66, 512]) for i in range(2)]
        yacc = [sb(f"yacc{i}", [128, 4, 64]) for i in range(2)]
        ysb = [sb(f"ysb{i}", [128, 512], BF16) for i in range(2)]

        for (t, d_, k) in ((g1T, g1T_d, 'g1T'), (identb, identb_d, 'identb'), (ident32, ident32_d, 'ident32'),
                           (cmask, cmask_d, 'cmask'), (seldiag, seldiag_d, 'seldiag'), (winmask, winmask_d, 'winmask'),
                           (fmk, fmk_d, 'fmk'), (fma, fma_d, 'fma')):
            P.dma('sync', t[:], d_, w=[k], sem='c')
        for kvh in range(2):
            P.dma('sync', KaS[kvh][68:128, :], OHx_d, w=[f'KaS{kvh}'], sem='c')
        P.finish_group('c')
        for kvh in range(2):
            for p in range(3):
                P.op('gpsimd', lambda e, t=QAx[kvh][p]: e.memset(t[:], 0.0), w=[f'QA{kvh}_{p}'])
        wqv = w_q.rearrange("(c p) n -> p c n", p=128)
        wgv = w_gl.rearrange("(c p) n -> p c n", p=128)
        P.dma('gpsimd', wqs[:], wqv, w=['wqs'], sem='w0')
        P.dma('gpsimd', wgs[:], wgv, w=['wgs'], sem='w1')
        for c in range(8):
            P.ts(rr(['vector', 'scalar'], c), wq[:, c, :], wqs[:, c, :], g1T[:, c:c + 1], None, ALU.mult, None,
                 r=['wqs', 'g1T'], w=['wq'])
            P.ts(rr(['scalar', 'vector'], c), wgl[:, c, :], wgs[:, c, :], g1T[:, c:c + 1], None, ALU.mult, None,
                 r=['wgs', 'g1T'], w=['wgl'])
        for bi in range(2):
            P.op('gpsimd', lambda e, bi=bi: e.memset(Osb[bi][:], 0.0), w=[f'Osb{bi}'])

        units = [(m, kvh) for m in range(nm) for kvh in range(2)]
        NU = len(units)
        gview = lambda m, kvh: gates[m % 2][:, 12 * kvh:12 * kvh + 12].rearrange("p (g b) -> p g b", b=3)

        def stA(u):
            m, kvh = units[u]
            ub = m % 2
            nph = (8 * m + 6) // 60 + 1
            if kvh == 0:
                P.dma('sync', uTo[ub][:], UTo[:, :, m * 128:(m + 1) * 128], w=[f'uTo{ub}'], sem=f'u{ub}')
                for c in range(8):
                    P.mm(pc[1][:, 0:24], uTo[ub][:, c, :], wgl[:, c, :], c == 0, c == 7, r=[f'uTo{ub}', 'wgl'], w=['pc1'])
                P.act(gates[ub][:], pc[1][:, 0:24], AF.Exp, r=['pc1'], w=[f'gates{ub}'], scale=-1.0)
                P.ts('vector', gates[ub][:], gates[ub][:], 1.0, None, ALU.add, None, r=[f'gates{ub}'], w=[f'gates{ub}'])
                P.op('vector', lambda e, ub=ub: e.reciprocal(out=gates[ub][:], in_=gates[ub][:]), r=[f'gates{ub}'], w=[f'gates{ub}'])
            for p in range(nph):
                P.dma('sync', QAx[kvh][p][64:68, :], qaug_d[m, kvh], w=[f'QA{kvh}_{p}'], sem=f'qa{kvh}_{p}')
            for g in range(4):
                hq = 4 * kvh + g
                for c in range(8):
                    P.mm(pc[0][0:64, g * 128:(g + 1) * 128], wq[:, c, hq * 64:(hq + 1) * 64], uTo[ub][:, c, :], c == 0, c == 7,
                         r=['wq', f'uTo{ub}'], w=['pc0'])
            for p in range(nph):
                P.ts('vector', QAx[kvh][p][0:64, :], pc[0][0:64, :], 0.125, None, ALU.mult, None, r=['pc0'], w=[f'QA{kvh}_{p}'])

        def stB(u):
            m, kvh = units[u]
            qa, qk = QAx[kvh][0], f'QA{kvh}_0'
            for ct in range(nct_of(m)):
                P.mm(pm[:], KaC[kvh][0:68, ct * 128:(ct + 1) * 128], qa[0:68, :], True, False, r=[f'KaC{kvh}', qk], w=['pm'])
                P.mm(pm[:].rearrange("p (g t) -> p g t", g=4), identb[:],
                     cmask[:, m, ct, :].unsqueeze(1).to_broadcast([128, 4, 128]), False, True, r=['identb', 'cmask'], w=['pm'])
                P.act(PcT[u % 2][ct][:], pm[:], AF.Exp, r=['pm'], w=[f'PcT{u % 2}_{ct}'])
        stB.sck = 0

        def stC(u):
            m, kvh = units[u]
            nct = nct_of(m)
            for g in range(4):
                pcg = pc[g // 2][:, (g % 2) * 193:(g % 2) * 193 + 193]
                for ct in range(nct):
                    P.mm(pcg, PcT[u % 2][ct][:, g * 128:(g + 1) * 128], CV[:, ct, kvh, 0:193], ct == 0, ct == nct - 1,
                         r=[f'PcT{u % 2}_{ct}', 'CV'], w=[f'pc{g // 2}'])

        def stD(u):
            m, kvh = units[u]
            b = u % 2
            gv = gview(m, kvh)
            gk = f'gates{m % 2}'
            for g in range(4):
                zz = pc[g // 2][:, (g % 2) * 193 + 192:(g % 2) * 193 + 193]
                P.ts('vector', zc[b][:, 0, g:g + 1], zz, 1e-30, None, ALU.max, None, r=[f'pc{g // 2}'], w=[f'zc{b}0'])
            P.op('vector', lambda e: e.reciprocal(out=zc[b][:, 0, :], in_=zc[b][:, 0, :]), r=[f'zc{b}0'], w=[f'zc{b}0'])
            P.tt('vector', rg[b][:, 0, :], zc[b][:, 0, :], gv[:, :, 0], ALU.mult, r=[f'zc{b}0', gk], w=[f'rg{b}0'])
            for g in range(4):
                pcg = pc[g // 2][:, (g % 2) * 193:(g % 2) * 193 + 193]
                if g == 0:
                    P.ts('vector', imp[:], pcg[:, 0:128], zc[b][:, 0, 0:1], None, ALU.mult, None, r=['pc0', f'zc{b}0'], w=['imp'])
                else:
                    P.stt(imp[:], pcg[:, 0:128], zc[b][:, 0, g:g + 1], imp[:], ALU.mult, ALU.add, r=[f'pc{g // 2}', f'zc{b}0', 'imp'], w=['imp'])
                P.ts('vector', yacc[b][:, g, :], pcg[:, 128:192], rg[b][:, 0, g:g + 1], None, ALU.mult, None,
                     r=[f'pc{g // 2}', f'rg{b}0'], w=[f'yacc{b}'])
            P.tt('vector', imp2[:], imp[:], fmk[:, m, :], ALU.mult, r=['imp', 'fmk'], w=['imp2'])
            P.tt('vector', imp2[:], imp2[:], fma[:, m, :], ALU.add, r=['imp2', 'fma'], w=['imp2'])
            P.op('vector', lambda e: e.max(out=m8[:, 0, :], in_=imp2[:]), r=['imp2'], w=['m8'])
            P.op('vector', lambda e: e.match_replace(out=tmpi[:], in_to_replace=m8[:, 0, :], in_values=imp2[:], imm_value=-3.0e38),
                 r=['imp2', 'm8'], w=['tmpi'])
            P.op('vector', lambda e: e.max(out=m8[:, 1, :], in_=tmpi[:]), r=['tmpi'], w=['m8'])
            P.ts('vector', tmpi[:], imp2[:], m8[:, 1, 7:8], None, ALU.is_ge, None, r=['imp2', 'm8'], w=['tmpi'])
            P.stt(tmpi[:], imp2[:], -5.0e29, tmpi[:], ALU.is_gt, ALU.mult, r=['imp2', 'tmpi'], w=['tmpi'])
            P.ts('vector', negm[:], tmpi[:], -NEGM, NEGM, ALU.mult, ALU.add, r=['tmpi'], w=['negm'])

        def stE(u):
            b = u % 2
            P.tr(pm[:, 0:128], negm[:], ident32[:], r=['negm', 'ident32'], w=['pm'])
            P.cp('vector', NMr[b][:].rearrange("p (g t) -> p g t", g=4), pm[:, 0:128].unsqueeze(1).to_broadcast([128, 4, 128]),
                 r=['pm'], w=[f'NMr{b}'])
            m, kvh = units[u]
            for p in range((8 * m + 6) // 60 + 1):
                rows = min(60, 128 - 60 * p)
                P.dma('sync', QAx[kvh][p][68:68 + rows, :], NMr[b][60 * p:60 * p + rows, :], r=[f'NMr{b}'], w=[f'QA{kvh}_{p}'],
                      sem=f'nm{kvh}_{p}')

        def tiles_of(u):
            m, kvh = units[u]
            tl = [('s', i, None) for i in range(4 * m + 4)]
            tl += [('w', 4 * m - 4 + r_, r_) for r_ in range(8) if 4 * m - 4 + r_ >= 0]
            return tl

        st_ = {'sck': 0}

        def score(u, tile):
            m, kvh = units[u]
            qa, qk = QAx[kvh][0], f'QA{kvh}_0'
            kind, i, r_ = tile
            k = stB.sck % 3; stB.sck += 1
            ks_ = slice(i * 128, (i + 1) * 128)
            if kind == 'c':
                m1, kvh1 = units[u + 1]
                P.mm(sc[k][:], KaC[kvh1][0:68, ks_], QAx[kvh1][0][0:68, :], True, False, r=[f'KaC{kvh1}', f'QA{kvh1}_0'], w=[f'sc{k}'])
                P.mm(sc[k][:].rearrange("p (g t) -> p g t", g=4), identb[:],
                     cmask[:, m1, i, :].unsqueeze(1).to_broadcast([128, 4, 128]), False, True, r=['identb', 'cmask'], w=[f'sc{k}'])
                return k
            if kind == 's':
                diag = i >= 4 * m
                p = (2 * i) // 60
                P.mm(sc[k][:], KaS[kvh][0:128, ks_], QAx[kvh][p][0:128, :], True, not diag, r=[f'KaS{kvh}', f'QA{kvh}_{p}'], w=[f'sc{k}'])
                if diag:
                    P.mm(sc[k][:], identb[:], seldiag[:, i - 4 * m, :], False, True, r=['identb', 'seldiag'], w=[f'sc{k}'])
            else:
                P.mm(sc[k][:], KaW[kvh][0:68, ks_], qa[0:68, :], True, False, r=[f'KaW{kvh}', qk], w=[f'sc{k}'])
                P.mm(sc[k][:], identb[:], winmask[:, r_, :], False, True, r=['identb', 'winmask'], w=[f'sc{k}'])
            return k

        def exp_pv(u, tile, k, pt, first, last):
            m, kvh = units[u]
            kind, i, r_ = tile
            if kind == 'c':
                P.act(PcT[(u + 1) % 2][i][:], sc[k][:], AF.Exp, r=[f'sc{k}'], w=[f'PcT{(u + 1) % 2}_{i}'])
                return
            P.act(PT[pt][:], sc[k][:], AF.Exp, r=[f'sc{k}'], w=[f'PT{pt}'])
            if kind == 's':
                P.mm(Os[0:65, :], Vs[:, i, kvh, 0:65], PT[pt][:], first, last, r=['Vs', f'PT{pt}'], w=['Os'])
            else:
                P.mm(Ow[0:65, :], Vw[:, i, kvh, 0:65], PT[pt][:], first, last, r=['Vw', f'PT{pt}'], w=['Ow'])

        def copyO(u, which):
            if which == 's':
                P.cp('vector', Osb[0][0:65, :], Os[0:65, :], r=['Os'], w=['Osb0'])
            else:
                P.cp('vector', Osb[1][0:65, :], Ow[0:65, :], r=['Ow'], w=['Osb1'])

        def fin(u, parts=(0, 1)):
            m, kvh = units[u]
            b = u % 2
            ub = m % 2
            gv = gview(m, kvh)
            gk = f'gates{m % 2}'
            for bi in parts:
                br = bi + 1
                for g in range(4):
                    P.tr(pm[:, g * 66:(g + 1) * 66], Osb[bi][0:66, g * 128:(g + 1) * 128], ident32[0:66, 0:66],
                         r=[f'Osb{bi}', 'ident32'], w=['pm'])
                pv = pm[:, 0:264].rearrange("p (g c) -> p g c", c=66)
                P.ts('vector', zc[b][:, br, :], pv[:, :, 64], 1e-30, None, ALU.max, None, r=['pm'], w=[f'zc{b}{br}'])
                P.op('vector', lambda e, br=br: e.reciprocal(out=zc[b][:, br, :], in_=zc[b][:, br, :]), r=[f'zc{b}{br}'], w=[f'zc{b}{br}'])
                P.tt('vector', rg[b][:, br, :], zc[b][:, br, :], gv[:, :, br], ALU.mult, r=[f'zc{b}{br}', gk], w=[f'rg{b}{br}'])
                for g in range(4):
                    P.stt(yacc[b][:, g, :], pv[:, g, 0:64], rg[b][:, br, g:g + 1], yacc[b][:, g, :], ALU.mult, ALU.add,
                          r=['pm', f'rg{b}{br}', f'yacc{b}'], w=[f'yacc{b}'])
            if 1 not in parts:
                return
            P.cp('vector', ysb[ub][:, kvh * 256:(kvh + 1) * 256], yacc[b][:].rearrange("p g d -> p (g d)"), r=[f'yacc{b}'], w=[f'ysb{ub}'])
            if kvh == 1:
                P.dma('sync', YN[m * 128:(m + 1) * 128, :], ysb[ub][:], r=[f'ysb{ub}'], w=['YNd'], sem=f'y{ub}')

        stA(0); stB(0); stC(0); stD(0); stE(0)
        ptk = 0
        pre = []
        for u in range(NU):
            tl = tiles_of(u)
            ctl = [('c', ct, None) for ct in range(nct_of(units[u + 1][0]))] if u + 1 < NU else []
            ins = min(7, len(tl))
            tl = tl[:ins] + ctl + tl[ins:]
            n = len(tl)
            idx_s = [i for i, t_ in enumerate(tl) if t_[0] == 's']
            idx_w = [i for i, t_ in enumerate(tl) if t_[0] == 'w']
            c_end = ins + len(ctl) - 1
            kq = list(pre)
            pre = []
            for i in range(len(kq), min(2, n)):
                kq.append(score(u, tl[i]))
            tl_next = tiles_of(u + 1) if u + 1 < NU else []
            for i in range(n):
                if i + 2 < n:
                    kq.append(score(u, tl[i + 2]))
                elif u + 1 < NU and len(pre) < min(2, len(tl_next)):
                    pre.append(score(u + 1, tl_next[len(pre)]))
                kind = tl[i][0]
                first = (i == idx_s[0]) if kind == 's' else (kind == 'w' and i == idx_w[0])
                last = (i == idx_s[-1]) if kind == 's' else (kind == 'w' and i == idx_w[-1])
                exp_pv(u, tl[i], kq[i], ptk % 4, first, last)
                if kind != 'c':
                    ptk += 1
                if i == idx_s[-1]:
                    copyO(u, 's')
                if i == 0 and u > 0:
                    fin(u - 1, (0,))
                if i == 2 and u > 0:
                    fin(u - 1, (1,))
                if u + 1 < NU:
                    if i == 1:
                        stA(u + 1)
                    if i == c_end:
                        stC(u + 1)
                        stD(u + 1)
                    if i == max(c_end + 1, n - 8):
                        stE(u + 1)
            copyO(u, 'w')
        fin(NU - 1)
        P.emit(st)


def block_g(nc, YR, YG):
    P = Prog(nc)
    with ExitStack() as st:
        for k in range(4):
            P.cc(lambda e, k=k: e.collective_compute("AllGather", ALU.bypass, replica_groups=[[0, 1, 2, 3], [4, 5, 6, 7]],
                                                     ins=[YR[k].ap().opt()], outs=[YG[k].ap().opt()]), w=[f'YG{k}'])
        P.emit(st)


def _f32(a):
    return np.ascontiguousarray(a, dtype=np.float32)


def own_rows(j):
    return np.concatenate([np.arange(128 * (4 * m + j), 128 * (4 * m + j) + 128) for m in range(NM)])


def _r_consts():
    s = np.arange(64)
    mTs = (s[None, :] > s[:, None]).astype(np.float32)
    mTi = (s[None, :] >= s[:, None]).astype(np.float32)
    ms = (s[None, :] < s[:, None]).astype(np.float32)
    rmask = np.stack([np.tile(mTs, (2, 1)), np.tile(mTi, (2, 1)), np.tile(ms, (2, 1))], axis=1)
    bones = np.kron(np.eye(2, dtype=np.float32), np.ones((64, 64), np.float32))
    resetm = np.ones((128, 512), np.float32)
    resetm[:, ::64] = 0.0
    return {'rmask': _f32(rmask), 'bones': bones, 'ident32': np.eye(128, dtype=np.float32), 'resetm': resetm}


R_CONSTS = _r_consts()


def _bf(a):
    import ml_dtypes
    return np.ascontiguousarray(np.asarray(a, dtype=np.float32).astype(ml_dtypes.bfloat16))


def _nsa_consts_common():
    jpos = np.arange(T)
    kaug = np.stack([jpos // 64, jpos % 64, np.ones(T), np.ones(T)])
    ce = 16 * np.arange(512) + 31
    kcaug = np.stack([ce // 64, ce % 64, np.ones(512), np.ones(512)])
    OH = ((jpos[None, :] // 64) % 60 == np.arange(60)[:, None])
    cst = 16 * np.arange(512)
    sst = 64 * np.arange(128)
    ov = np.clip(np.minimum(cst[:, None] + 32, sst[None, :] + 64) - np.maximum(cst[:, None], sst[None, :]), 0, None) / 16.0
    ov[511] = 0.0
    ovc = ov.reshape(4, 128, 128).transpose(1, 0, 2)
    return {'kaug': _bf(kaug), 'kcaug': _bf(kcaug), 'OHx': _bf(OH), 'ovc': _bf(ovc)}


def _nsa_consts_core(j):
    tt = np.arange(128)
    jj = np.arange(128)
    qaug = np.zeros((NM, 2, 4, 4, 128), np.float32)
    cmask = np.zeros((128, NM, 4, 128), np.float32)
    fmk = np.zeros((128, NM, 128), np.float32)
    fma = np.zeros((128, NM, 128), np.float32)
    n = np.arange(128)
    for m in range(NM):
        t = 128 * (4 * m + j) + tt
        for kvh in range(2):
            for g in range(4):
                s = 2.0 ** -(4 * kvh + g + 1)
                qaug[m, kvh, 0, g] = 64 * s
                qaug[m, kvh, 1, g] = s
                qaug[m, kvh, 2, g] = -64 * s * (t // 64)
                qaug[m, kvh, 3, g] = -s * (t % 64)
        for ct in range(4):
            c = 128 * ct + jj
            vis = (c[:, None] <= 510) & (16 * c[:, None] + 31 <= t[None, :])
            cmask[:, m, ct, :] = np.where(vis, 0.0, NEGM)
        cur = t // 64
        valid = n[None, :] <= cur[:, None]
        forced = (n[None, :] == 0) | (n[None, :] == cur[:, None]) | (n[None, :] == cur[:, None] - 1)
        fmk[:, m, :] = (valid & ~forced)
        fma[:, m, :] = np.where(valid, np.where(forced, 1e30, 0.0), -1e30)
    seldiag = np.zeros((128, 4, 4, 128), np.float32)
    for r in range(4):
        if r == j:
            seldiag[:, r, :, :] = np.where(jj[:, None] <= tt[None, :], 0.0, NEGM)[:, None, :]
        elif r > j:
            seldiag[:, r] = NEGM
    winmask = np.zeros((128, 8, 4, 128), np.float32)
    for r in range(8):
        dist = 128 * (j + 4 - r) + tt[None, :] - jj[:, None]
        winmask[:, r, :, :] = np.where((dist >= 0) & (dist < 512), 0.0, NEGM)[:, None, :]
    return {'qaug': _bf(qaug.reshape(NM, 2, 4, 512)), 'cmask': _bf(cmask), 'fmk': _bf(fmk), 'fma': _bf(fma),
            'seldiag': _bf(seldiag.reshape(128, 4, 512)), 'winmask': _bf(winmask.reshape(128, 8, 512))}


N_COMMON = _nsa_consts_common()
N_CORE = [_nsa_consts_core(j) for j in range(4)]


def build_inputs(inp, stages):
    x = inp['x']
    maps = []
    ident = np.eye(128, dtype=np.float32)
    import ml_dtypes
    identb = ident.astype(ml_dtypes.bfloat16)
    for c in range(8):
        b, j = c // 4, c % 4
        d = {}
        d['xb'] = _f32(x[b])
        d['xo'] = _f32(x[b][own_rows(j)])
        d['identb'] = identb
        if 'R' in stages:
            hA = 2 * j
            wi = inp['w_in'][0]
            hc = slice(128 * j, 128 * j + 128)
            d['w_rw'] = _f32(np.concatenate([wi[:, 0:512][:, hc], wi[:, 512:1024][:, hc], wi[:, 1024:1536][:, hc],
                                             wi[:, 1536:1792]], axis=1))
            d['g1T'] = _f32(inp['norm1_g'][0].reshape(8, 128).T)
            mu = inp['mu_shift'][0]
            rwp = np.zeros((128, 16), np.float32)
            rwp[:, 0] = mu[0:512][hc]; rwp[:, 1] = mu[512:1024][hc]; rwp[:, 2] = mu[1024:1536][hc]
            rwp[:, 3] = mu[1536:1664]; rwp[:, 4] = mu[1664:1792]
            rwp[:, 5] = inp['rwkv_w0'][0][hc]; rwp[:, 6] = inp['rwkv_a0'][0][hc]
            rwp[:, 7] = inp['rwkv_k_k'][0][hc]; rwp[:, 8] = inp['rwkv_k_a'][0][hc]
            rwp[:, 9] = inp['rwkv_r_k'][0].reshape(512)[hc]
            d['rwp'] = rwp
            d['w2a2'] = _f32(np.concatenate([inp['rwkv_w2'][0][:, hc], inp['rwkv_a2'][0][:, hc]], axis=0))
            d['g2'] = _f32(inp['rwkv_g2'][0][:, hc])
            lw = inp['rwkv_lnx_w'][0][hc].reshape(2, 1, 64)
            lb = inp['rwkv_lnx_b'][0][hc].reshape(2, 1, 64)
            d['lnwb'] = _f32(np.stack([np.broadcast_to(lw, (2, 64, 64)).reshape(128, 64),
                                       np.broadcast_to(lb, (2, 64, 64)).reshape(128, 64)], axis=1))
            d.update(R_CONSTS)
        if 'N' in stages:
            wi = inp['w_in'][0]
            o = 1792
            cols = lambda a, b_: wi[:, o + a:o + b_]
            d['w_kv'] = _f32(np.concatenate([cols(512, 640), cols(640, 768), cols(768, 896), cols(1024, 1152),
                                             cols(896, 1024), cols(1152, 1280)], axis=1))
            d['w_q'] = _f32(cols(0, 512))
            d['w_gl'] = _f32(cols(1280, 1304))
            d['g1T'] = _f32(inp['norm1_g'][0].reshape(8, 128).T)
            d['w1k'] = _f32(inp['nsa_cmp_k_w1'][0]); d['w1v'] = _f32(inp['nsa_cmp_v_w1'][0])
            d['w2k'] = _f32(inp['nsa_cmp_k_w2'][0]); d['w2v'] = _f32(inp['nsa_cmp_v_w2'][0])
            pe2 = lambda pe: _f32(np.tile(np.repeat(pe.T, 2, axis=1), (2, 1)))
            d['pek'] = pe2(inp['nsa_pe_k'][0]); d['pev'] = pe2(inp['nsa_pe_v'][0])
            d['ident32'] = np.eye(128, dtype=np.float32)
            d.update(N_COMMON)
            d.update(N_CORE[j])
        if 'F' in stages:
            sel = np.zeros((128, 4), np.float32)
            sel[:, j] = 1.0
            d['selt'] = sel
            d['w_out'] = _f32(inp['w_out'][0])
            d['g2T'] = _f32(inp['norm2_g'][0].reshape(8, 128).T)
            d['w_gate'] = _f32(inp['ffn_w_gate'][0])
            d['w_up'] = _f32(inp['ffn_w_up'][0])
            d['w_down'] = _f32(inp['ffn_w_down'][0])
            d['gfb'] = _f32(np.broadcast_to(inp['norm_f_g'][None, :], (128, D)))
        maps.append(d)
    return maps


def build_program(stages, dbg=()):
    nc = bass.Bass("TRN2", target_bir_lowering=False)
    declared = []

    def ein(n, s, d=F32):
        declared.append(n)
        return nc.dram_tensor(n, s, d, kind="ExternalInput").ap()
    nc._declared_inputs = declared
    xb = ein("xb", [T, D])
    xo = ein("xo", [NM * 128, D])
    identb = ein("identb", [128, 128], BF16)
    out = nc.dram_tensor("out", [NM * 128, D], F32, kind="ExternalOutput").ap()
    kind = lambda n: "ExternalOutput" if n in dbg else "Internal"
    UT = nc.dram_tensor("UT", [128, 8, T], BF16, kind=kind("UT")).ap()
    UTo = nc.dram_tensor("UTo", [128, 8, NM * 128], BF16, kind=kind("UTo")).ap()
    YN = nc.dram_tensor("YN", [NM * 128, 512], BF16, kind=kind("YN")).ap()
    YR = [nc.dram_tensor(f"YR{k}", [2048, 128], BF16, **({'kind': 'ExternalOutput'} if 'YR' in dbg else {})) for k in range(4)]
    YG = [nc.dram_tensor(f"YG{k}", [4 * 2048, 128], BF16) for k in range(4)]
    merged = R2 and 'R' in stages and 'U' in stages
    WGU = nc.dram_tensor("WGU", [NFS, 128, 2, 8, 128], BF16, kind="Internal").ap()
    WOb = nc.dram_tensor("WOb", [128, 8, D], BF16, kind="Internal").ap()
    WDb = nc.dram_tensor("WDb", [128, NFS, D], BF16, kind="Internal").ap()
    f_in = None
    if 'F' in stages:
        f_in = dict(selt=ein("selt", [128, 4]), w_out=ein("w_out", [D, D]), g2T=ein("g2T", [128, 8]), w_gate=ein("w_gate", [D, DFF]),
                    w_up=ein("w_up", [D, DFF]), w_down=ein("w_down", [DFF, D]), gfb=ein("gfb", [128, D]))
    if 'U' in stages and not merged:
        block_u(nc, xb, xo, UT, UTo, identb)
    g1T_d = i32_d = None
    if 'R' in stages:
        g1T_d = ein("g1T", [128, 8])
        i32_d = ein("ident32", [128, 128])
        (block_r2 if R2 else block_r)(nc, UT, ein("w_rw", [D, 640]), g1T_d, ein("rwp", [128, 16]), ein("w2a2", [128, 128]),
                ein("g2", [128, 128]), ein("lnwb", [128, 2, 64]), ein("rmask", [128, 3, 64]), ein("bones", [128, 128]),
                i32_d, ein("resetm", [128, 512]), [y.ap() for y in YR], *((identb,) if R2 else ()), ntiles=(2 * RT if R2 else RT),
                **(dict(xb=xb, xo=xo, UTo=UTo, fpro=((f_in['w_gate'], f_in['w_up'], f_in['g2T'], WGU) if f_in else None),
                        fpro2=((f_in['w_out'], f_in['w_down'], WOb, WDb) if f_in else None)) if merged else {}))
    if 'N' in stages:
        with ExitStack() as ps_:
            sbp = lambda n, s, d: ps_.enter_context(nc.sbuf_tensor("p_" + n, s, d))
            pers = {'KaS': [sbp(f"KaS{i}", [128, T], BF16) for i in range(2)], 'KaW': [sbp(f"KaW{i}", [68, T], BF16) for i in range(2)],
                    'Vs': sbp("Vs", [128, 64, 2, 66], BF16), 'Vw': sbp("Vw", [128, 64, 2, 66], BF16),
                    'KaC': [sbp(f"KaC{i}", [68, 512], BF16) for i in range(2)], 'CV': sbp("CV", [128, 4, 2, 194], BF16)}
            g1T_d = ein("g1T", [128, 8]) if 'R' not in stages else g1T_d
            i32_d = ein("ident32", [128, 128]) if 'R' not in stages else i32_d
            with ExitStack() as ps2:
                pers['kcT'] = ps2.enter_context(nc.sbuf_tensor("p_kcT", [128, 16, 514], BF16))
                pers['vcT'] = ps2.enter_context(nc.sbuf_tensor("p_vcT", [128, 16, 514], BF16))
                if 'a' in NSUB:
                    block_na(nc, pers, UT, ein("w_kv", [D, 768]), g1T_d, ein("kaug", [4, T], BF16),
                             gather=((YR, YG) if ('G' in stages and 'R' in stages) else None))
                if 'b' in NSUB:
                  block_nb(nc, pers, ein("w1k", [2048, 256]), ein("w1v", [2048, 256]), ein("w2k", [256, 64]), ein("w2v", [256, 64]),
                         ein("pek", [128, 64]), ein("pev", [128, 64]), ein("kcaug", [4, 512], BF16), ein("ovc", [128, 4, 128], BF16))
            if 'c' in NSUB:
              block_nc(nc, pers, UTo, ein("w_q", [D, 512]), ein("w_gl", [D, 24]), g1T_d, identb, i32_d, ein("OHx", [60, T], BF16),
                     ein("cmask", [128, NM, 4, 128], BF16), ein("seldiag", [128, 4, 512], BF16), ein("winmask", [128, 8, 512], BF16),
                     ein("fmk", [128, NM, 128], BF16), ein("fma", [128, NM, 128], BF16), ein("qaug", [NM, 2, 4, 512], BF16), YN, nm=NQ)
    if 'G' in stages and not ('N' in stages and 'a' in NSUB and 'R' in stages):
        block_g(nc, YR, YG)
    if 'F' in stages:
        block_f(nc, xo, YN, [y.ap() for y in YG], f_in['selt'], f_in['w_out'], f_in['g2T'], f_in['w_gate'], f_in['w_up'], f_in['w_down'],
                f_in['gfb'], identb, out, WGU, use_y=('Y' in stages), wgu_done=merged, WOb=WOb, WDb=WDb)
    return nc


def run(inp, stages, dbg=()):
    nc = build_program(stages, dbg)
    maps = build_inputs(inp, stages)
    maps = [{k: v for k, v in mp.items() if k in nc._declared_inputs} for mp in maps]
    res = run_bass_kernel_spmd(nc, maps, core_ids=list(range(8)))
    return res


def kernel(**inputs):
    inp = {k: np.asarray(v) for k, v in inputs.items()}
    res = run(inp, stages=('U', 'R', 'N', 'G', 'Y', 'F'))
    out = np.empty((2, T, D), np.float32)
    for c in range(8):
        b, j = c // 4, c % 4
        out[b][own_rows(j)] = res.results[c]['out']
    return out
```
